# Optimizing a Trainium2 kernel written in Bass

```python
import jax, jax.numpy as jnp
from jax import lax
import numpy as np

D_MODEL = 2048
BATCH = 2
SEQ = 8192
DEPTH = 4
DEC_BATCH = 32
DEC_SEQ = 32
PAST_LEN = 2048

CHUNK = 64
N_MIXERS = 3
N_A_LAYERS = (DEPTH + 2) // 3
N_B_LAYERS = (DEPTH + 1) // 3
N_C_LAYERS = DEPTH // 3
WINDOW = 128
WIN_CHUNKS = WINDOW // CHUNK
A_HEADS = 32
A_KV_HEADS = 4
A_HEAD_DIM = 64
A_GROUP = A_HEADS // A_KV_HEADS
ROPE_THETA = 10000.0
CONV_WIDTH = 31
GLA_HEADS = 4
GLA_DK = D_MODEL // 2
GLA_DV = D_MODEL
GLA_DK_HEAD = GLA_DK // GLA_HEADS
GLA_DV_HEAD = GLA_DV // GLA_HEADS
GLA_GATE_RANK = 16
GLA_TAU = 16.0
D_FF = 5632
FFN_CONV_WIDTH = 3
PLE_DIM = 256
EPS = 1e-6

kernel_name = "hybrid_streaming_encoder_step"


def rms_norm(x, g):
    xf = x.astype(jnp.float32)
    y = xf * lax.rsqrt(jnp.mean(xf * xf, axis=-1, keepdims=True) + EPS)
    return (y * g.astype(jnp.float32)).astype(x.dtype)


def layer_norm(x, g, b):
    xf = x.astype(jnp.float32)
    mu = jnp.mean(xf, axis=-1, keepdims=True)
    xc = xf - mu
    y = xc * lax.rsqrt(jnp.mean(xc * xc, axis=-1, keepdims=True) + EPS)
    return (y * g.astype(jnp.float32) + b.astype(jnp.float32)).astype(x.dtype)


def rope(x, pos):
    half = x.shape[-1] // 2
    inv = 1.0 / (ROPE_THETA ** (jnp.arange(half, dtype=jnp.float32) / half))
    ang = pos.astype(jnp.float32)[:, None] * inv[None, :]
    cos = jnp.cos(ang)[None, :, None, :]
    sin = jnp.sin(ang)[None, :, None, :]
    xf = x.astype(jnp.float32)
    x1, x2 = xf[..., :half], xf[..., half:]
    return jnp.concatenate([x1 * cos - x2 * sin, x2 * cos + x1 * sin], axis=-1).astype(x.dtype)


def causal_dwconv(x, buf, w, b):
    xp = jnp.concatenate([buf.astype(x.dtype), x], axis=1)
    y = lax.conv_general_dilated(xp, w.astype(x.dtype)[:, None, :], window_strides=(1,), padding='VALID',
                                 dimension_numbers=('NWC', 'WIO', 'NWC'), feature_group_count=x.shape[-1])
    return y + b.astype(x.dtype), xp[:, xp.shape[1] - (w.shape[0] - 1):]


def attn_qkv(xn, w_qkv, q_norm, k_norm, pos):
    B, L, _ = xn.shape
    q, k, v = jnp.split(xn @ w_qkv, [A_HEADS * A_HEAD_DIM, (A_HEADS + A_KV_HEADS) * A_HEAD_DIM], axis=-1)
    q = rope(rms_norm(q.reshape(B, L, A_HEADS, A_HEAD_DIM), q_norm), pos)
    k = rope(rms_norm(k.reshape(B, L, A_KV_HEADS, A_HEAD_DIM), k_norm), pos)
    v = v.reshape(B, L, A_KV_HEADS, A_HEAD_DIM)
    return q, k, v


def sink_softmax(scores, sinks, mask):
    s = sinks.astype(jnp.float32)[:, :, None, None]
    if mask is not None:
        scores = jnp.where(mask, scores, -jnp.inf)
    m = jnp.maximum(jnp.max(scores, axis=-1, keepdims=True), s)
    e = jnp.exp(scores - m)
    return e / (jnp.sum(e, axis=-1, keepdims=True) + jnp.exp(s - m))


def attn_prompt(xn, w_qkv, q_norm, k_norm, sinks, w_o):
    B, L, _ = xn.shape
    nC = L // CHUNK
    q, k, v = attn_qkv(xn, w_qkv, q_norm, k_norm, jnp.arange(L))
    qb = q.reshape(B, nC, CHUNK, A_KV_HEADS, A_GROUP, A_HEAD_DIM)
    pad = jnp.zeros((B, WIN_CHUNKS * CHUNK, A_KV_HEADS, A_HEAD_DIM), k.dtype)
    kp = jnp.concatenate([pad, k], axis=1).reshape(B, nC + WIN_CHUNKS, CHUNK, A_KV_HEADS, A_HEAD_DIM)
    vp = jnp.concatenate([pad.astype(v.dtype), v], axis=1).reshape(B, nC + WIN_CHUNKS, CHUNK, A_KV_HEADS, A_HEAD_DIM)
    kb = jnp.concatenate([kp[:, j:j + nC] for j in range(WIN_CHUNKS + 1)], axis=2)
    vb = jnp.concatenate([vp[:, j:j + nC] for j in range(WIN_CHUNKS + 1)], axis=2)
    key_chunk = jnp.arange(nC)[:, None] - WIN_CHUNKS + jnp.arange((WIN_CHUNKS + 1) * CHUNK)[None, :] // CHUNK
    mask = (key_chunk >= 0)[None, :, None, None, None, :]
    scores = jnp.einsum('bnqkgd,bnskd->bnkgqs', qb, kb, preferred_element_type=jnp.float32) * (A_HEAD_DIM ** -0.5)
    probs = sink_softmax(scores, sinks.reshape(A_KV_HEADS, A_GROUP), mask)
    o = jnp.einsum('bnkgqs,bnskd->bnqkgd', probs.astype(v.dtype), vb)
    y = o.reshape(B, L, A_HEADS * A_HEAD_DIM) @ w_o
    return y, k[:, L - WINDOW:], v[:, L - WINDOW:]


def attn_sample(xn, cache_k, cache_v, w_qkv, q_norm, k_norm, sinks, w_o):
    B, T, _ = xn.shape
    q, k, v = attn_qkv(xn, w_qkv, q_norm, k_norm, PAST_LEN + jnp.arange(T))
    kk = jnp.concatenate([cache_k.astype(k.dtype), k], axis=1)
    vv = jnp.concatenate([cache_v.astype(v.dtype), v], axis=1)
    qg = q.reshape(B, T, A_KV_HEADS, A_GROUP, A_HEAD_DIM)
    scores = jnp.einsum('bqkgd,bskd->bkgqs', qg, kk, preferred_element_type=jnp.float32) * (A_HEAD_DIM ** -0.5)
    probs = sink_softmax(scores, sinks.reshape(A_KV_HEADS, A_GROUP), None)
    o = jnp.einsum('bkgqs,bskd->bqkgd', probs.astype(vv.dtype), vv)
    y = o.reshape(B, T, A_HEADS * A_HEAD_DIM) @ w_o
    n = kk.shape[1]
    return y, kk[:, n - WINDOW:], vv[:, n - WINDOW:]


def conformer_conv(xn, buf, w_pw1, w_dw, b_dw, ln_g, ln_b, w_pw2):
    a, g = jnp.split(xn @ w_pw1, 2, axis=-1)
    u = a * jax.nn.sigmoid(g)
    c, new_buf = causal_dwconv(u, buf, w_dw, b_dw)
    c = layer_norm(c, ln_g, ln_b)
    return jax.nn.silu(c) @ w_pw2, new_buf


def gla_project(xn, w_in, w_gate_up, gate_bias):
    B, L, _ = xn.shape
    q, k, v, r, gl = jnp.split(xn @ w_in, [GLA_DK, 2 * GLA_DK, 2 * GLA_DK + GLA_DV, 2 * GLA_DK + 2 * GLA_DV], axis=-1)
    logit = (gl @ w_gate_up + gate_bias).astype(jnp.float32)
    log_a = jax.nn.log_sigmoid(logit) / GLA_TAU
    q = q.reshape(B, L, GLA_HEADS, GLA_DK_HEAD) * (GLA_DK_HEAD ** -0.5)
    k = k.reshape(B, L, GLA_HEADS, GLA_DK_HEAD)
    v = v.reshape(B, L, GLA_HEADS, GLA_DV_HEAD)
    log_a = log_a.reshape(B, L, GLA_HEADS, GLA_DK_HEAD)
    return q, k, v, log_a, r


def gla_chunk(S, q, k, v, log_a):
    qf, kf, vf = q.astype(jnp.float32), k.astype(jnp.float32), v.astype(jnp.float32)
    C = q.shape[1]
    b = jnp.cumsum(log_a, axis=1)
    qt = qf * jnp.exp(b)
    kt = kf * jnp.exp(-b)
    o_inter = jnp.einsum('bchk,bhkv->bchv', qt, S)
    A = jnp.einsum('bihk,bjhk->bhij', qt, kt)
    A = jnp.where(jnp.tril(jnp.ones((C, C), dtype=bool)), A, 0.0)
    o_intra = jnp.einsum('bhij,bjhv->bihv', A, vf)
    b_last = b[:, -1]
    k_dec = kf * jnp.exp(b_last[:, None] - b)
    S_new = jnp.exp(b_last)[..., None] * S + jnp.einsum('bchk,bchv->bhkv', k_dec, vf)
    return o_inter + o_intra, S_new


def gla_out(o, r, out_norm, w_o):
    B, L = o.shape[:2]
    o = rms_norm(o, out_norm).reshape(B, L, GLA_DV).astype(r.dtype)
    return (o * jax.nn.silu(r)) @ w_o


def gla_prompt(xn, w_in, w_gate_up, gate_bias, out_norm, w_o):
    B, L, _ = xn.shape
    nC = L // CHUNK
    q, k, v, log_a, r = gla_project(xn, w_in, w_gate_up, gate_bias)

    def to_chunks(t):
        return jnp.moveaxis(t.reshape(B, nC, CHUNK, *t.shape[2:]), 1, 0)

    def step(S, inp):
        o, S = gla_chunk(S, *inp)
        return S, o

    S0 = jnp.zeros((B, GLA_HEADS, GLA_DK_HEAD, GLA_DV_HEAD), jnp.float32)
    S, o = lax.scan(step, S0, (to_chunks(q), to_chunks(k), to_chunks(v), to_chunks(log_a)))
    o = jnp.moveaxis(o, 0, 1).reshape(B, L, GLA_HEADS, GLA_DV_HEAD)
    return gla_out(o, r, out_norm, w_o), S.astype(xn.dtype)


def gla_sample(xn, state, w_in, w_gate_up, gate_bias, out_norm, w_o):
    q, k, v, log_a, r = gla_project(xn, w_in, w_gate_up, gate_bias)
    o, S = gla_chunk(state.astype(jnp.float32), q, k, v, log_a)
    return gla_out(o, r, out_norm, w_o), S.astype(xn.dtype)


def conv_ffn(xn, buf, w_up, conv_w, conv_b, w_down):
    g, u = jnp.split(xn @ w_up, 2, axis=-1)
    gc, new_buf = causal_dwconv(g, buf, conv_w, conv_b)
    return (jax.nn.silu(gc) * u) @ w_down, new_buf


def per_layer_embed(h, p, w_proj, g, w_gate):
    return (p.astype(h.dtype) @ w_proj) * jax.nn.sigmoid(rms_norm(h, g) @ w_gate)


def setup_inputs(seed: int = 0) -> dict:
    key = jax.random.key(seed)
    ks = iter(list(jax.random.split(key, 48)))

    def nrm(shape, scale):
        return jax.random.normal(next(ks), shape, jnp.float32) * scale

    def gain(shape):
        return 1.0 + nrm(shape, 0.02)

    D = D_MODEL
    qkv_w = (A_HEADS + 2 * A_KV_HEADS) * A_HEAD_DIM
    gla_in = 2 * GLA_DK + 2 * GLA_DV + GLA_GATE_RANK
    return {
        "x_prompt": nrm((BATCH, SEQ, D), 1.0),
        "x_sample": nrm((DEC_BATCH, DEC_SEQ, D), 1.0),
        "p_prompt": nrm((DEPTH, BATCH, SEQ, PLE_DIM), 1.0),
        "p_sample": nrm((DEPTH, DEC_BATCH, DEC_SEQ, PLE_DIM), 1.0),
        "cache_k_a": nrm((N_A_LAYERS, DEC_BATCH, WINDOW, A_KV_HEADS, A_HEAD_DIM), 1.0),
        "cache_v_a": nrm((N_A_LAYERS, DEC_BATCH, WINDOW, A_KV_HEADS, A_HEAD_DIM), 1.0),
        "state_conv_b": nrm((N_B_LAYERS, DEC_BATCH, CONV_WIDTH - 1, D), 0.5),
        "state_gla_c": nrm((N_C_LAYERS, DEC_BATCH, GLA_HEADS, GLA_DK_HEAD, GLA_DV_HEAD), 1.0),
        "state_ffn_conv": nrm((DEPTH, DEC_BATCH, FFN_CONV_WIDTH - 1, D_FF), 1.0),
        "norm_mix": gain((DEPTH, D)),
        "norm_ffn": gain((DEPTH, D)),
        "a_w_qkv": nrm((N_A_LAYERS, D, qkv_w), D ** -0.5),
        "a_q_norm": gain((N_A_LAYERS, A_HEAD_DIM)),
        "a_k_norm": gain((N_A_LAYERS, A_HEAD_DIM)),
        "a_sinks": nrm((N_A_LAYERS, A_HEADS), 0.5),
        "a_w_o": nrm((N_A_LAYERS, A_HEADS * A_HEAD_DIM, D), (A_HEADS * A_HEAD_DIM) ** -0.5),
        "b_w_pw1": nrm((N_B_LAYERS, D, 2 * D), D ** -0.5),
        "b_w_dw": nrm((N_B_LAYERS, CONV_WIDTH, D), CONV_WIDTH ** -0.5),
        "b_dw_bias": nrm((N_B_LAYERS, D), 0.02),
        "b_ln_g": gain((N_B_LAYERS, D)),
        "b_ln_b": nrm((N_B_LAYERS, D), 0.02),
        "b_w_pw2": nrm((N_B_LAYERS, D, D), D ** -0.5),
        "c_w_in": nrm((N_C_LAYERS, D, gla_in), D ** -0.5),
        "c_w_gate_up": nrm((N_C_LAYERS, GLA_GATE_RANK, GLA_DK), GLA_GATE_RANK ** -0.5),
        "c_gate_bias": nrm((N_C_LAYERS, GLA_DK), 0.02),
        "c_out_norm": gain((N_C_LAYERS, GLA_DV_HEAD)),
        "c_w_o": nrm((N_C_LAYERS, GLA_DV, D), GLA_DV ** -0.5),
        "ffn_w_up": nrm((DEPTH, D, 2 * D_FF), D ** -0.5),
        "ffn_conv_w": nrm((DEPTH, FFN_CONV_WIDTH, D_FF), FFN_CONV_WIDTH ** -0.5),
        "ffn_conv_b": nrm((DEPTH, D_FF), 0.02),
        "ffn_w_down": nrm((DEPTH, D_FF, D), D_FF ** -0.5),
        "ple_w_proj": nrm((DEPTH, PLE_DIM, D), PLE_DIM ** -0.5),
        "ple_norm": gain((DEPTH, D)),
        "ple_w_gate": nrm((DEPTH, D, D), D ** -0.5),
    }


def reference(x_prompt, x_sample, p_prompt, p_sample, cache_k_a, cache_v_a, state_conv_b, state_gla_c,
              state_ffn_conv, norm_mix, norm_ffn, a_w_qkv, a_q_norm, a_k_norm, a_sinks, a_w_o,
              b_w_pw1, b_w_dw, b_dw_bias, b_ln_g, b_ln_b, b_w_pw2,
              c_w_in, c_w_gate_up, c_gate_bias, c_out_norm, c_w_o,
              ffn_w_up, ffn_conv_w, ffn_conv_b, ffn_w_down, ple_w_proj, ple_norm, ple_w_gate):
    hp, hs = x_prompt, x_sample
    Bp = hp.shape[0]
    kp_l, vp_l, ks_l, vs_l = [], [], [], []
    cbp_l, cbs_l, gp_l, gs_l = [], [], [], []
    fp_l, fs_l = [], []
    for i in range(DEPTH):
        kind, slot = i % N_MIXERS, i // N_MIXERS
        xpn = rms_norm(hp, norm_mix[i])
        xsn = rms_norm(hs, norm_mix[i])
        if kind == 0:
            mp, kp_, vp_ = attn_prompt(xpn, a_w_qkv[slot], a_q_norm[slot], a_k_norm[slot], a_sinks[slot], a_w_o[slot])
            ms, ks_, vs_ = attn_sample(xsn, cache_k_a[slot], cache_v_a[slot], a_w_qkv[slot], a_q_norm[slot],
                                       a_k_norm[slot], a_sinks[slot], a_w_o[slot])
            kp_l.append(kp_); vp_l.append(vp_); ks_l.append(ks_); vs_l.append(vs_)
        elif kind == 1:
            buf0 = jnp.zeros((Bp, CONV_WIDTH - 1, D_MODEL), hp.dtype)
            mp, cbp = conformer_conv(xpn, buf0, b_w_pw1[slot], b_w_dw[slot], b_dw_bias[slot], b_ln_g[slot],
                                     b_ln_b[slot], b_w_pw2[slot])
            ms, cbs = conformer_conv(xsn, state_conv_b[slot], b_w_pw1[slot], b_w_dw[slot], b_dw_bias[slot],
                                     b_ln_g[slot], b_ln_b[slot], b_w_pw2[slot])
            cbp_l.append(cbp); cbs_l.append(cbs)
        else:
            mp, gp = gla_prompt(xpn, c_w_in[slot], c_w_gate_up[slot], c_gate_bias[slot], c_out_norm[slot], c_w_o[slot])
            ms, gs = gla_sample(xsn, state_gla_c[slot], c_w_in[slot], c_w_gate_up[slot], c_gate_bias[slot],
                                c_out_norm[slot], c_w_o[slot])
            gp_l.append(gp); gs_l.append(gs)
        hp = hp + mp
        hs = hs + ms
        fbuf0 = jnp.zeros((Bp, FFN_CONV_WIDTH - 1, D_FF), hp.dtype)
        fp, fbp = conv_ffn(rms_norm(hp, norm_ffn[i]), fbuf0, ffn_w_up[i], ffn_conv_w[i], ffn_conv_b[i], ffn_w_down[i])
        fs, fbs = conv_ffn(rms_norm(hs, norm_ffn[i]), state_ffn_conv[i], ffn_w_up[i], ffn_conv_w[i], ffn_conv_b[i],
                           ffn_w_down[i])
        fp_l.append(fbp); fs_l.append(fbs)
        hp = hp + fp
        hs = hs + fs
        hp = hp + per_layer_embed(hp, p_prompt[i], ple_w_proj[i], ple_norm[i], ple_w_gate[i])
        hs = hs + per_layer_embed(hs, p_sample[i], ple_w_proj[i], ple_norm[i], ple_w_gate[i])
    new_k_a_prompt = jnp.stack(kp_l, axis=0)
    new_v_a_prompt = jnp.stack(vp_l, axis=0)
    new_conv_b_prompt = jnp.stack(cbp_l, axis=0)
    new_gla_c_prompt = jnp.stack(gp_l, axis=0)
    new_ffn_conv_prompt = jnp.stack(fp_l, axis=0)
    new_k_a_sample = jnp.stack(ks_l, axis=0)
    new_v_a_sample = jnp.stack(vs_l, axis=0)
    new_conv_b_sample = jnp.stack(cbs_l, axis=0)
    new_gla_c_sample = jnp.stack(gs_l, axis=0)
    new_ffn_conv_sample = jnp.stack(fs_l, axis=0)
    return (hp, hs, new_k_a_prompt, new_v_a_prompt, new_conv_b_prompt, new_gla_c_prompt, new_ffn_conv_prompt,
            new_k_a_sample, new_v_a_sample, new_conv_b_sample, new_gla_c_sample, new_ffn_conv_sample)
```

```python
import numpy as np
from contextlib import ExitStack
import concourse.bass as bass
import concourse.mybir as mybir
from concourse.bass_utils import run_bass_kernel_spmd

F32 = mybir.dt.float32
BF16 = mybir.dt.bfloat16
AF = mybir.ActivationFunctionType
ALU = mybir.AluOpType

D = 2048
DC = D // 128
DFF = 5632
FC = DFF // 128
PLE = 256
DEPTH = 4
EPS = 1e-6
TP = 512
TS = 32
SEQ = 8192
DEC_B = 32
DEC_T = 32

WBLK = 8192
NSLOT = 3
CW = 31


class _Eng:
    def __init__(self, name, eng, sem, same_engine_sync):
        self.name, self.e, self.sem = name, eng, sem
        self.count = 0
        self.pending = False
        self.seen = {}
        self.same_engine_sync = same_engine_sync
        self.q = []

    def replay(self, handle):
        self.e = handle
        for item in self.q:
            if item[0] == "wait":
                handle.wait_ge(item[1], item[2])
            elif item[0] == "ins":
                ins = item[1]()
                if item[2]:
                    ins.then_inc(self.sem, 1)
            else:
                _, out, in_, kw, sem = item
                handle.dma_start(out=out, in_=in_, **kw).then_inc(sem, 16)


class Prog:
    def __init__(self, nc, st, same_engine_sync=True):
        self.nc, self.st = nc, st
        self.res = {}
        self.sems = {}
        mk = lambda n: st.enter_context(nc.semaphore(n))
        self.pe = _Eng("pe", None, mk("s_pe"), False)
        self.act = _Eng("act", None, mk("s_act"), same_engine_sync)
        self.dve = _Eng("dve", None, mk("s_dve"), same_engine_sync)
        self.pool = _Eng("pool", None, mk("s_pool"), same_engine_sync)
        self.sp = _Eng("sp", None, mk("s_sp"), False)
        self.dma_sems = {}
        self.n_instr = 0

    def _r(self, key):
        r = self.res.get(key)
        if r is None:
            r = self.res[key] = {"w": None, "r": {}}
        return r

    def _wait(self, E, tok):
        if tok is None:
            return
        sem, val = tok
        k = id(sem)
        if sem is E.sem and not E.same_engine_sync:
            return
        if E.seen.get(k, 0) >= val:
            return
        E.q.append(("wait", sem, val))
        E.seen[k] = val
        self.n_instr += 1

    def _deps(self, E, reads, writes):
        for key in reads:
            self._wait(E, self._r(key)["w"])
        for key in writes:
            r = self._r(key)
            self._wait(E, r["w"])
            for tok in r["r"].values():
                self._wait(E, tok)

    def _commit(self, tok, reads, writes):
        sem, val = tok
        for key in writes:
            r = self._r(key)
            r["w"] = tok
            r["r"] = {}
        for key in reads:
            r = self._r(key)
            old = r["r"].get(id(sem))
            if old is None or old[1] < val:
                r["r"][id(sem)] = tok

    def op(self, E, fn, reads=(), writes=(), signal=True):
        self._deps(E, reads, writes)
        E.q.append(("ins", fn, signal))
        self.n_instr += 1
        if signal:
            E.count += 1
            E.pending = False
            tok = (E.sem, E.count)
        else:
            E.pending = True
            tok = (E.sem, E.count + 1)
        self._commit(tok, reads, writes)
        return tok

    def dma_sem(self, name):
        s = self.dma_sems.get(name)
        if s is None:
            s = self.dma_sems[name] = [self.st.enter_context(self.nc.semaphore("d_" + name)), 0]
        return s

    def dma(self, Q, semname, out, in_, reads=(), writes=(), **kw):
        self._deps(Q, reads, writes)
        s = self.dma_sem(semname)
        s[1] += 16
        Q.q.append(("dma", out, in_, kw, s[0]))
        self.n_instr += 1
        tok = (s[0], s[1])
        self._commit(tok, reads, writes)
        return tok

    def drain_all(self, E):
        for s, tot in self.dma_sems.values():
            if tot:
                self._wait(E, (s, tot))
        for X in (self.pe, self.act, self.dve, self.pool):
            if X.count and X is not E:
                self._wait(E, (X.sem, X.count))


def _wspec(layer, kind, slot):
    out = []
    if kind == 0:
        out.append(("q", "a_w_qkv", slot, D, 0, 2048))
        out.append(("kd", "a_w_qkv", slot, D, 2048, 512))
        out.append(("v", "a_w_qkv", slot, D, 2304, 256))
        out.append(("wo", "a_w_o", slot, D, 0, D))
    elif kind == 1:
        out.append(("pw1", "b_w_pw1", slot, D, 0, 2 * D))
        out.append(("pw2", "b_w_pw2", slot, D, 0, D))
    else:
        out.append(("gq", "c_w_in", slot, D, 0, 1024))
        out.append(("gk", "c_w_in", slot, D, 1024, 1024))
        out.append(("ggl", "c_w_in", slot, D, 6144, 16))
        out.append(("gv", "c_w_in", slot, D, 2048, 2048))
        out.append(("gr", "c_w_in", slot, D, 4096, 2048))
        out.append(("cwo", "c_w_o", slot, D, 0, D))
    out.append(("up", "ffn_w_up", layer, D, 0, 2 * DFF))
    out.append(("down", "ffn_w_down", layer, DFF, 0, D))
    out.append(("pgate", "ple_w_gate", layer, D, 0, D))
    out.append(("pproj", "ple_w_proj", layer, PLE, 0, D))
    return out


def _blk_cols(K, nm=None):
    if K == PLE:
        return D
    if nm == "v":
        return 256
    if nm == "ggl":
        return 16
    kc = K // 128
    c = WBLK // kc
    return min(512, (c // 128) * 128)


def _kinds_slots(kinds):
    cnt = {0: 0, 1: 0, 2: 0}
    ks = []
    for k in kinds:
        if k is None:
            ks.append((None, 0))
        else:
            ks.append((k, cnt[k]))
            cnt[k] += 1
    return ks, cnt


DEFAULT_KINDS = tuple(i % 3 for i in range(DEPTH))
SKIP_FFN = False
ATTN_VSTAGE = 9
GLA_CUT = 99
ATTN_CUT = 99


def build_program(n_ptiles, spc, kinds=DEFAULT_KINDS, same_engine_sync=True):
    nc = bass.Bass("TRN2", target_bir_lowering=False)
    layers = len(kinds)
    LP = max(n_ptiles * TP, TP)
    SPC = max(spc, 1)
    KS, NSL = _kinds_slots(kinds)
    NPT_R = n_ptiles + 1
    dt = lambda name, shape, dtype=F32, kind="ExternalInput": nc.dram_tensor(name, list(shape), dtype, kind=kind).ap()

    x_p = dt("x_p", [LP, D])
    p_p = dt("p_p", [layers, LP, PLE])
    x_s = dt("x_s", [SPC, TS, D])
    p_s = dt("p_s", [layers, SPC, TS, PLE])
    s_ffn = dt("s_ffn", [layers, SPC, 2, DFF])
    ident_in = dt("ident", [128, 128])
    vecs = {}
    for nm, shp in (("norm_mix", [layers, D]), ("norm_ffn", [layers, D]), ("ple_norm", [layers, D]),
                    ("ffn_conv_w", [layers, 3, DFF]), ("ffn_conv_b", [layers, DFF])):
        vecs[nm] = dt(nm, shp)
    wts = {}
    for nm, shp in (("ffn_w_up", [layers, D, 2 * DFF]), ("ffn_w_down", [layers, DFF, D]),
                    ("ple_w_proj", [layers, PLE, D]), ("ple_w_gate", [layers, D, D])):
        wts[nm] = dt(nm, shp)
    NA = NSL[0]
    if NA:
        wts["a_w_qkv"] = dt("a_w_qkv", [NA, D, 2560]); wts["a_w_o"] = dt("a_w_o", [NA, D, D])
        for nm, shp in (("a_q_norm", [NA, 64]), ("a_k_norm", [NA, 64]), ("a_sinks", [NA, 32])):
            vecs[nm] = dt(nm, shp)
        ck_in = dt("ck", [NA, SPC, 128, 256]); cv_in = dt("cv", [NA, SPC, 128, 256])
        rope_in = dt("rope", [NPT_R, 2, 128, TP])
        psign_in = dt("psign", [128, 128])
        o_k_p = dt("o_k_p", [NA, 128, 256], kind="ExternalOutput"); o_v_p = dt("o_v_p", [NA, 128, 256], kind="ExternalOutput")
        o_k_s = dt("o_k_s", [NA, SPC, 128, 256], kind="ExternalOutput"); o_v_s = dt("o_v_s", [NA, SPC, 128, 256], kind="ExternalOutput")
    NG = NSL[2]
    if NG:
        wts["c_w_in"] = dt("c_w_in", [NG, D, 6160]); wts["c_w_o"] = dt("c_w_o", [NG, D, D])
        for nm, shp in (("c_w_gate_up", [NG, 16, 1024]), ("c_gate_bias", [NG, 1024]), ("c_out_norm", [NG, 512])):
            vecs[nm] = dt(nm, shp)
        s_gla = dt("s_gla", [NG, SPC, 4, 256, 512])
        tri_in = dt("tri4", [128, 256])
        o_gla_p = dt("o_gla_p", [NG, 4, 256, 512], kind="ExternalOutput")
        o_gla_s = dt("o_gla_s", [NG, SPC, 4, 256, 512], kind="ExternalOutput")
        gla_carry = [dt(f"gla_carry{g_}", [128, 8, 512], kind="Internal") for g_ in range(NG)]
    NB = NSL[1]
    if NB:
        wts["b_w_pw1"] = dt("b_w_pw1", [NB, D, 2 * D]); wts["b_w_pw2"] = dt("b_w_pw2", [NB, D, D])
        for nm, shp in (("b_w_dw", [NB, CW, D]), ("b_dw_bias", [NB, D]), ("b_ln_g", [NB, D]), ("b_ln_b", [NB, D])):
            vecs[nm] = dt(nm, shp)
        s_conv = dt("s_conv", [NB, SPC, CW - 1, D])
        o_conv_p = dt("o_conv_p", [NB, CW - 1, D], kind="ExternalOutput")
        o_conv_s = dt("o_conv_s", [NB, SPC, CW - 1, D], kind="ExternalOutput")

    y_p = dt("y_p", [LP, D], kind="ExternalOutput")
    y_s = dt("y_s", [SPC, TS, D], kind="ExternalOutput")
    o_ffn_p = dt("o_ffn_p", [layers, 2, DFF], kind="ExternalOutput")
    o_ffn_s = dt("o_ffn_s", [layers, SPC, 2, DFF], kind="ExternalOutput")

    scratch = {}
    for l in range(layers):
        for (nm, src, slot, K, c0, ncols) in _wspec(l, *KS[l]):
            if src not in wts:
                continue
            bc = _blk_cols(K, nm)
            nblk = ncols // bc
            assert nblk * bc == ncols, (nm, ncols, bc)
            scratch[(l, nm)] = (dt(f"wb_{l}_{nm}", [nblk, 128, (K // 128) * bc], BF16, kind="Internal"),
                                 src, slot, K, c0, bc, nblk)

    with ExitStack() as st:
        E = st.enter_context
        sb = lambda name, shape, dtype=F32: E(nc.sbuf_tensor(name, list(shape), dtype))
        P = Prog(nc, st, same_engine_sync=same_engine_sync)
        PE, ACT, DVE, POOL, SP = P.pe, P.act, P.dve, P.pool, P.sp

        ident = sb("ident_sb", [128, 128])
        ident_bf = sb("ident_bf", [128, 128], BF16)
        ones_bf = sb("ones_bf", [128, 128], BF16)
        g_mix = sb("g_mix", [128, layers, DC]); g_ffn = sb("g_ffn", [128, layers, DC]); g_ple = sb("g_ple", [128, layers, DC])
        cw = sb("cw", [128, layers, 3, FC]); cb = sb("cb", [128, layers, FC])
        h = sb("h", [128, DC, TP])
        xn = sb("xn", [128, DC, TP], BF16)
        act = sb("act", [128, FC, TP], BF16)
        sq = act
        rstd = sb("rstd", [128, TP])
        gext = [sb(f"gext{i}", [128, TP + 2]) for i in range(2)]
        acc = [sb(f"acc{i}", [128, TP]) for i in range(2)]
        sil = [sb(f"sil{i}", [128, TP]) for i in range(2)]
        fhist = sb("fhist", [128, layers, 2, FC])
        stage_i = sb("stage_i", [128, 512])
        stage_o = sb("stage_o", [128, 512])
        pT = sb("pT", [128, 2, TP], BF16)
        gate = sb("gate", [128, TP])
        tr_i = sb("tr_i", [128, 128]); tr_o = sb("tr_o", [128, 128])
        wpp = sb("wpp", [128, 2, D], BF16)
        wring = sb("wring", [128, NSLOT, WBLK], BF16)
        act32 = act[:].rearrange("p c t -> p (c t)").bitcast(F32).rearrange("p (c t) -> p c t", t=TP)
        assert tuple(act32.shape) == (128, FC // 2, TP), act32.shape
        mean = sb("mean", [128, TP])
        if NA:
            bd = sb("bd", [128, 128], BF16)
            psign = sb("psign_sb", [128, 128])
            gqk = sb("gqk", [128, NA, 2])
            rgm = sb("rgm", [128, NA, 2, 128])
            esink = sb("esink", [128, NA, 16])
            khist = sb("khist", [128, NA, 4, 128], BF16)
            vhist = sb("vhist", [64, NA, 2, 256], BF16)
            kout = sb("kout", [128, 4, 128])
            vout = sb("vout", [64, 2, 256])
            dnm = sb("dnm", [128, 256])
        if NG:
            wgu = sb("wgu", [16, NG, 1024], BF16)
            ngb = sb("ngb", [128, NG, 8])
            onw = sb("onw", [128, NG, 4])
            elast = sb("elast", [128, 8, 8])
            cmask = sb("cmask", [128, TP], BF16)
            tri4 = sb("tri4_sb", [128, 256])
            osq = sb("osq", [128, 1024], BF16)
            glT = sb("glT", [16, TP], BF16)
        if NB:
            ones_f = sb("ones_f", [128, 128])
            chist = sb("chist", [128, NB, DC, CW - 1])
            dww = sb("dww", [128, NB, CW, DC]); dwb = sb("dwb", [128, NB, DC])
            lng = sb("lng", [128, NB, DC]); lnb = sb("lnb", [128, NB, DC])
        psA = [E(nc.psum_tensor(f"ps{i}", [128, 512], F32)) for i in range(8)]

        SQ_KEYS = [("act", c) for c in range(DC)]
        XN_ALL = [("xn", c) for c in range(DC)]
        ACT_ALL = [("act", j) for j in range(FC)]
        FH = lambda l: [("fhist", l, j) for j in range(FC)]

        P.dma(SP, "const", ident[:], ident_in, writes=["const"])
        P.op(POOL, lambda: POOL.e.memset(ones_bf[:], 1.0), writes=["ones"])
        P.op(DVE, lambda: DVE.e.tensor_copy(out=ident_bf[:], in_=ident[:]), reads=["const"], writes=["ident_bf"])
        for nm, t in (("norm_mix", g_mix), ("norm_ffn", g_ffn), ("ple_norm", g_ple)):
            P.dma(SP, "const", t[:], vecs[nm].rearrange("l (c p) -> p l c", p=128), writes=["const"],
                  allow_slow_non_contiguous=True)
        P.dma(SP, "const", cw[:], vecs["ffn_conv_w"].rearrange("l t (c p) -> p l t c", p=128), writes=["const"],
              allow_slow_non_contiguous=True)
        P.dma(SP, "const", cb[:], vecs["ffn_conv_b"].rearrange("l (c p) -> p l c", p=128), writes=["const"],
              allow_slow_non_contiguous=True)

        if NA:
            P.op(POOL, lambda: POOL.e.memset(bd[:], 0.0), writes=["bd"])
            P.op(POOL, lambda: POOL.e.memset(bd[0:64, 0:64], 1.0), writes=["bd"])
            P.op(POOL, lambda: POOL.e.memset(bd[64:128, 64:128], 1.0), writes=["bd"])
            P.dma(SP, "const", psign[:], psign_in, writes=["const"])
            for a in range(NA):
                for j, nm in enumerate(("a_q_norm", "a_k_norm")):
                    for hh in range(2):
                        P.dma(SP, "const", gqk[64 * hh:64 * hh + 64, a, j:j + 1], vecs[nm][a].rearrange("(d o) -> d o", o=1),
                              writes=["const"], allow_slow_non_contiguous=True)
                for hh in range(2):
                    P.dma(SP, "const", esink[64 * hh:64 * hh + 64, a, :],
                          vecs["a_sinks"][a].rearrange("(kp h) -> h kp", h=2)[hh].partition_broadcast(64), writes=["const"],
                          allow_slow_non_contiguous=True)
            for a in range(NA):
                for j in range(2):
                    P.op(DVE, lambda a=a, j=j: DVE.e.tensor_scalar(out=rgm[:, a, j, :], in0=psign[:], scalar1=gqk[:, a, j:j + 1],
                                                                   scalar2=None, op0=ALU.mult), reads=["const"], writes=["rgm"])
                P.op(ACT, lambda a=a: ACT.e.activation(out=esink[:, a, :], in_=esink[:, a, :], func=AF.Exp),
                     reads=["const"], writes=["esink"])
        if NG:
            P.dma(SP, "const", tri4[:], tri_in, writes=["const"])
            for g_ in range(NG):
                P.dma(POOL, "ld_wgu", wgu[:, g_, :], vecs["c_w_gate_up"][g_], writes=["wgu"])
                P.dma(SP, "const", ngb[:, g_, :], vecs["c_gate_bias"][g_].rearrange("(c p) -> p c", p=128), writes=["const"],
                      allow_slow_non_contiguous=True)
                P.dma(SP, "const", onw[:, g_, :], vecs["c_out_norm"][g_].rearrange("(c p) -> p c", p=128), writes=["const"],
                      allow_slow_non_contiguous=True)
            P.op(DVE, lambda: DVE.e.tensor_scalar(out=ngb[:], in0=ngb[:], scalar1=-1.0, scalar2=None, op0=ALU.mult),
                 reads=["const"], writes=["ngb"])
            P.op(POOL, lambda: POOL.e.memset(cmask[:], 1.0), writes=["cmask"])
            P.op(POOL, lambda: POOL.e.memset(cmask[:].rearrange("p (c t) -> p c t", t=64)[:, :, 0:1], 0.0), writes=["cmask"])
        if NB:
            P.op(POOL, lambda: POOL.e.memset(ones_f[:], 1.0), writes=["ones"])
            P.dma(SP, "const", dww[:], vecs["b_w_dw"].rearrange("n t (c p) -> p n t c", p=128), writes=["const"],
                  allow_slow_non_contiguous=True)
            for nm, t in (("b_dw_bias", dwb), ("b_ln_g", lng), ("b_ln_b", lnb)):
                P.dma(SP, "const", t[:], vecs[nm].rearrange("n (c p) -> p n c", p=128), writes=["const"],
                      allow_slow_non_contiguous=True)

        for (l, nm), (wb, src, slot, K, c0, bc, nblk) in scratch.items():
            kc = K // 128
            if SKIP_FFN and nm in ("up", "down", "pgate", "pproj"):
                continue
            if nm == "kd":
                dstv = wb[0].rearrange("p (kc c) -> p kc c", kc=kc)
                for kv in range(4):
                    for r_ in range(2):
                        src_ap = wts[src][slot, :, 2048 + kv * 64: 2048 + (kv + 1) * 64].rearrange("(kc p) c -> p kc c", p=128)
                        P.dma(POOL, "wcast", dstv[:, :, kv * 128 + r_ * 64: kv * 128 + (r_ + 1) * 64], src_ap, writes=["wb"])
                continue
            for b in range(nblk):
                src_ap = wts[src][slot, :, c0 + b * bc: c0 + (b + 1) * bc].rearrange("(kc p) c -> p kc c", p=128)
                dst_ap = wb[b].rearrange("p (kc c) -> p kc c", kc=kc)
                P.dma(POOL, "wcast", dst_ap, src_ap, writes=["wb"])

        ring = {"n": 0}

        def wload(l, nm, b):
            wb, src, slot_, K, c0, bc, nblk = scratch[(l, nm)]
            kc = K // 128
            s = ring["n"] % NSLOT
            ring["n"] += 1
            P.dma(SP, f"w{s}", wring[:, s, 0:kc * bc], wb[b], reads=["wb"], writes=[("ring", s)])
            return s, wring[:, s, 0:kc * bc].rearrange("p (kc c) -> p kc c", kc=kc)

        psn = {"n": 0}

        def nextps():
            i = psn["n"] % 8
            psn["n"] += 1
            return i

        def rmsnorm(gt, l, T):
            P.op(ACT, lambda: ACT.e.activation(out=sq[:, 0:DC, 0:T], in_=h[:, :, 0:T], func=AF.Square),
                 reads=["h"], writes=SQ_KEYS)
            pi = nextps()
            for c in range(DC):
                P.op(PE, lambda c=c, pi=pi: PE.e.matmul(psA[pi][:, 0:T], lhsT=ones_bf[:], rhs=sq[:, c, 0:T],
                                                        start=(c == 0), stop=(c == DC - 1)),
                     reads=SQ_KEYS + ["ones"], writes=[("ps", pi)], signal=(c == DC - 1))
            P.op(DVE, lambda pi=pi: DVE.e.tensor_scalar(out=rstd[:, 0:T], in0=psA[pi][:, 0:T], scalar1=1.0 / D,
                                                        scalar2=EPS, op0=ALU.mult, op1=ALU.add),
                 reads=[("ps", pi)], writes=["rstd"])
            P.op(ACT, lambda: ACT.e.activation(out=rstd[:, 0:T], in_=rstd[:, 0:T], func=AF.Sqrt),
                 reads=["rstd"], writes=["rstd"])
            P.op(DVE, lambda: DVE.e.reciprocal(out=rstd[:, 0:T], in_=rstd[:, 0:T]), reads=["rstd"], writes=["rstd"])
            for c in range(DC):
                P.op(DVE, lambda c=c: DVE.e.scalar_tensor_tensor(out=xn[:, c, 0:T], in0=h[:, c, 0:T],
                                                                 scalar=gt[:, l, c:c + 1], in1=rstd[:, 0:T],
                                                                 op0=ALU.mult, op1=ALU.mult),
                     reads=["h", "rstd", "const"], writes=[("xn", c)])

        def conv_ffn(l, T, zero_hist):
            rmsnorm(g_ffn, l, T)
            wb, src, slot_, K, c0, bc, nblk = scratch[(l, "up")]
            cpb = bc // 128
            live = {}
            for j in range(FC):
                ops = []
                for col_chunk in (j, FC + j):
                    b = col_chunk // cpb
                    if b not in live:
                        live[b] = wload(l, "up", b)
                    ops.append((live[b], col_chunk % cpb))
                pg, pu = nextps(), nextps()
                for ((s, wv), off), pi in zip(ops, (pg, pu)):
                    for k in range(DC):
                        P.op(PE, lambda k=k, wv=wv, off=off, pi=pi: PE.e.matmul(
                            psA[pi][:, 0:T], lhsT=wv[:, k, off * 128:(off + 1) * 128], rhs=xn[:, k, 0:T],
                            start=(k == 0), stop=(k == DC - 1)),
                            reads=[("ring", s)] + XN_ALL, writes=[("ps", pi)], signal=(k == DC - 1))
                i = j % 2
                ge, ac, si = gext[i], acc[i], sil[i]
                if zero_hist:
                    P.op(POOL, lambda ge=ge: POOL.e.memset(ge[:, 0:2], 0.0), writes=[("gext", i)])
                else:
                    P.op(POOL, lambda ge=ge, j=j: POOL.e.tensor_copy(out=ge[:, 0:2], in_=fhist[:, l, :, j]),
                         reads=[("fhist", l, j)], writes=[("gext", i)])
                P.op(ACT, lambda ge=ge, pg=pg: ACT.e.activation(out=ge[:, 2:T + 2], in_=psA[pg][:, 0:T], func=AF.Copy),
                     reads=[("ps", pg)], writes=[("gext", i)])
                P.op(POOL, lambda ge=ge, j=j: POOL.e.tensor_copy(out=fhist[:, l, :, j], in_=ge[:, T:T + 2]),
                     reads=[("gext", i)], writes=[("fhist", l, j)])
                P.op(DVE, lambda ge=ge, ac=ac, j=j: DVE.e.tensor_scalar(
                    out=ac[:, 0:T], in0=ge[:, 2:T + 2], scalar1=cw[:, l, 2, j:j + 1], scalar2=cb[:, l, j:j + 1],
                    op0=ALU.mult, op1=ALU.add), reads=[("gext", i), "const"], writes=[("acc", i)])
                for tap in (1, 0):
                    P.op(DVE, lambda ge=ge, ac=ac, j=j, tap=tap: DVE.e.scalar_tensor_tensor(
                        out=ac[:, 0:T], in0=ge[:, tap:T + tap], scalar=cw[:, l, tap, j:j + 1], in1=ac[:, 0:T],
                        op0=ALU.mult, op1=ALU.add), reads=[("gext", i), ("acc", i), "const"], writes=[("acc", i)])
                P.op(ACT, lambda ac=ac, si=si: ACT.e.activation(out=si[:, 0:T], in_=ac[:, 0:T], func=AF.Silu),
                     reads=[("acc", i)], writes=[("sil", i)])
                P.op(DVE, lambda si=si, pu=pu, j=j: DVE.e.tensor_tensor(out=act[:, j, 0:T], in0=psA[pu][:, 0:T],
                                                                       in1=si[:, 0:T], op=ALU.mult),
                     reads=[("sil", i), ("ps", pu)], writes=[("act", j)])
            wb, src, slot_, K, c0, bc, nblk = scratch[(l, "down")]
            cpb = bc // 128
            for b in range(nblk):
                s, wv = wload(l, "down", b)
                for off in range(cpb):
                    m = b * cpb + off
                    pi = nextps()
                    for k in range(FC):
                        P.op(PE, lambda k=k, wv=wv, off=off, pi=pi: PE.e.matmul(
                            psA[pi][:, 0:T], lhsT=wv[:, k, off * 128:(off + 1) * 128], rhs=act[:, k, 0:T],
                            start=(k == 0), stop=(k == FC - 1)),
                            reads=[("ring", s)] + ACT_ALL, writes=[("ps", pi)], signal=(k == FC - 1))
                    P.op(DVE, lambda m=m, pi=pi: DVE.e.tensor_tensor(out=h[:, m, 0:T], in0=psA[pi][:, 0:T],
                                                                     in1=h[:, m, 0:T], op=ALU.add),
                         reads=[("ps", pi), "h"], writes=["h"])

        def conformer(l, slot, T, zero_hist):
            rmsnorm(g_mix, l, T)
            W = CW - 1
            c32 = act32[:, 0:DC, :]
            uext = [act32[:, 16 + 2 * i: 18 + 2 * i, :].rearrange("p a t -> p (a t)") for i in range(2)]
            C32 = [("act", j) for j in range(0, 32)]
            UX = [[("act", j) for j in range(32 + 4 * i, 36 + 4 * i)] for i in range(2)]
            cpb = scratch[(l, "pw1")][5] // 128
            live = {}
            for m in range(DC):
                ops = []
                for col_chunk in (m, DC + m):
                    b = col_chunk // cpb
                    if b not in live:
                        live[b] = wload(l, "pw1", b)
                    ops.append((live[b], col_chunk % cpb))
                pa, pg = nextps(), nextps()
                for ((s_, wv), off), pi in zip(ops, (pa, pg)):
                    for k in range(DC):
                        P.op(PE, lambda k=k, wv=wv, off=off, pi=pi: PE.e.matmul(
                            psA[pi][:, 0:T], lhsT=wv[:, k, off * 128:(off + 1) * 128], rhs=xn[:, k, 0:T],
                            start=(k == 0), stop=(k == DC - 1)),
                            reads=[("ring", s_)] + XN_ALL, writes=[("ps", pi)], signal=(k == DC - 1))
                i = m % 2
                ux, si = uext[i], sil[i]
                if zero_hist:
                    P.op(POOL, lambda ux=ux: POOL.e.memset(ux[:, 0:W], 0.0), writes=UX[i])
                else:
                    P.op(POOL, lambda ux=ux, m=m: POOL.e.tensor_copy(out=ux[:, 0:W], in_=chist[:, slot, m, :]),
                         reads=[("chist", m)], writes=UX[i])
                P.op(ACT, lambda si=si, pg=pg: ACT.e.activation(out=si[:, 0:T], in_=psA[pg][:, 0:T], func=AF.Sigmoid),
                     reads=[("ps", pg)], writes=[("sil", i)])
                P.op(DVE, lambda ux=ux, si=si, pa=pa: DVE.e.tensor_tensor(out=ux[:, W:W + T], in0=psA[pa][:, 0:T],
                                                                        in1=si[:, 0:T], op=ALU.mult),
                     reads=[("ps", pa), ("sil", i)], writes=UX[i])
                P.op(POOL, lambda ux=ux, m=m: POOL.e.tensor_copy(out=chist[:, slot, m, :], in_=ux[:, T:T + W]),
                     reads=UX[i], writes=[("chist", m)])
                P.op(DVE, lambda ux=ux, m=m: DVE.e.tensor_scalar(
                    out=c32[:, m, 0:T], in0=ux[:, 0:T], scalar1=dww[:, slot, 0, m:m + 1], scalar2=dwb[:, slot, m:m + 1],
                    op0=ALU.mult, op1=ALU.add), reads=UX[i] + ["const"], writes=[("act", 2 * m), ("act", 2 * m + 1)])
                for j in range(1, CW):
                    P.op(DVE, lambda ux=ux, m=m, j=j: DVE.e.scalar_tensor_tensor(
                        out=c32[:, m, 0:T], in0=ux[:, j:j + T], scalar=dww[:, slot, j, m:m + 1], in1=c32[:, m, 0:T],
                        op0=ALU.mult, op1=ALU.add), reads=UX[i] + ["const", ("act", 2 * m), ("act", 2 * m + 1)],
                        writes=[("act", 2 * m), ("act", 2 * m + 1)])
            P.op(ACT, lambda: ACT.e.activation(out=xn[:, :, 0:T], in_=c32[:, :, 0:T], func=AF.Square),
                 reads=C32, writes=XN_ALL)
            p1, p2 = nextps(), nextps()
            for c in range(DC):
                P.op(PE, lambda c=c, p1=p1: PE.e.matmul(psA[p1][:, 0:T], lhsT=ones_f[:], rhs=c32[:, c, 0:T],
                                                        start=(c == 0), stop=(c == DC - 1)),
                     reads=C32 + ["ones"], writes=[("ps", p1)], signal=(c == DC - 1))
            for c in range(DC):
                P.op(PE, lambda c=c, p2=p2: PE.e.matmul(psA[p2][:, 0:T], lhsT=ones_bf[:], rhs=xn[:, c, 0:T],
                                                        start=(c == 0), stop=(c == DC - 1)),
                     reads=XN_ALL + ["ones"], writes=[("ps", p2)], signal=(c == DC - 1))
            P.op(DVE, lambda p1=p1: DVE.e.tensor_scalar(out=mean[:, 0:T], in0=psA[p1][:, 0:T], scalar1=1.0 / D, scalar2=None,
                                                        op0=ALU.mult), reads=[("ps", p1)], writes=["mean"])
            P.op(DVE, lambda: DVE.e.tensor_tensor(out=gate[:, 0:T], in0=mean[:, 0:T], in1=mean[:, 0:T], op=ALU.mult),
                 reads=["mean"], writes=["gate"])
            P.op(DVE, lambda p2=p2: DVE.e.scalar_tensor_tensor(out=rstd[:, 0:T], in0=psA[p2][:, 0:T], scalar=1.0 / D,
                                                               in1=gate[:, 0:T], op0=ALU.mult, op1=ALU.subtract),
                 reads=[("ps", p2), "gate"], writes=["rstd"])
            P.op(DVE, lambda: DVE.e.tensor_scalar(out=rstd[:, 0:T], in0=rstd[:, 0:T], scalar1=EPS, scalar2=None, op0=ALU.add),
                 reads=["rstd"], writes=["rstd"])
            P.op(ACT, lambda: ACT.e.activation(out=rstd[:, 0:T], in_=rstd[:, 0:T], func=AF.Sqrt), reads=["rstd"], writes=["rstd"])
            P.op(DVE, lambda: DVE.e.reciprocal(out=rstd[:, 0:T], in_=rstd[:, 0:T]), reads=["rstd"], writes=["rstd"])
            for m in range(DC):
                i = m % 2
                ac = acc[i]
                P.op(DVE, lambda m=m, ac=ac: DVE.e.tensor_tensor(out=ac[:, 0:T], in0=c32[:, m, 0:T], in1=mean[:, 0:T],
                                                                 op=ALU.subtract),
                     reads=[("act", 2 * m), ("act", 2 * m + 1), "mean"], writes=[("acc", i)])
                P.op(DVE, lambda ac=ac: DVE.e.tensor_tensor(out=ac[:, 0:T], in0=ac[:, 0:T], in1=rstd[:, 0:T], op=ALU.mult),
                     reads=[("acc", i), "rstd"], writes=[("acc", i)])
                P.op(ACT, lambda m=m, ac=ac: ACT.e.activation(out=xn[:, m, 0:T], in_=ac[:, 0:T], func=AF.Silu,
                                                              scale=lng[:, slot, m:m + 1], bias=lnb[:, slot, m:m + 1]),
                     reads=[("acc", i), "const"], writes=[("xn", m)])
            cpb2 = scratch[(l, "pw2")][5] // 128
            for b in range(scratch[(l, "pw2")][6]):
                s_, wv = wload(l, "pw2", b)
                for off in range(cpb2):
                    m = b * cpb2 + off
                    pi = nextps()
                    for k in range(DC):
                        P.op(PE, lambda k=k, wv=wv, off=off, pi=pi: PE.e.matmul(
                            psA[pi][:, 0:T], lhsT=wv[:, k, off * 128:(off + 1) * 128], rhs=xn[:, k, 0:T],
                            start=(k == 0), stop=(k == DC - 1)),
                            reads=[("ring", s_)] + XN_ALL, writes=[("ps", pi)], signal=(k == DC - 1))
                    P.op(DVE, lambda m=m, pi=pi: DVE.e.tensor_tensor(out=h[:, m, 0:T], in0=psA[pi][:, 0:T],
                                                                     in1=h[:, m, 0:T], op=ALU.add),
                         reads=[("ps", pi), "h"], writes=["h"])

        CH_ALL = [("chist", m) for m in range(DC)]

        def attention(l, slot, T, CL, first_prompt, rope_idx, sample_idx, emit_out):
            rmsnorm(g_mix, l, T)
            NQC = T // CL
            actf = act[:].rearrange("p c t -> p (c t)")
            qT = act[:, 0:16, :]
            oT = act[:, 16:32, :]
            KT = actf[:, 32 * TP: 32 * TP + 4 * 640].rearrange("p (k s) -> p k s", k=4)
            VV = actf[0:64, 37 * TP: 37 * TP + 10 * 256].rearrange("p (c f) -> p c f", c=10)
            sqh = [act[:, 42 + i, :] for i in range(2)]
            ET = xn[0:64, 0:4, :]
            QT_K = [("act", j) for j in range(0, 16)]; OT_K = [("act", j) for j in range(16, 32)]
            KT_K = [("act", j) for j in range(32, 37)]; VV_K = [("act", j) for j in range(37, 42)]
            cosT, sinT = gext[0], gext[1]
            if ATTN_CUT <= -1:
                return
            P.dma(SP, "ld_gext0", cosT[:, 0:TP], rope_in[rope_idx, 0], writes=[("gext", 0)])
            P.dma(SP, "ld_gext1", sinT[:, 0:TP], rope_in[rope_idx, 1], writes=[("gext", 1)])
            if sample_idx is None:
                if not first_prompt:
                    P.op(POOL, lambda: POOL.e.tensor_copy(out=KT[:, :, 0:128], in_=khist[:, slot]), reads=["khist"], writes=KT_K)
                    P.op(POOL, lambda: POOL.e.tensor_copy(out=VV[:, 0:2, :], in_=vhist[:, slot]), reads=["vhist"], writes=VV_K)
            else:
                for r_ in range(2):
                    P.dma(SP, "ld_stage", stage_i[:, :].rearrange("p (k r d) -> p k r d", k=4, r=2)[:, :, r_, :],
                          ck_in[slot, sample_idx].rearrange("s (k d) -> s k d", k=4), writes=["stage_i"])
                pi = nextps()
                for kv in range(4):
                    P.op(PE, lambda kv=kv, pi=pi: PE.e.transpose(psA[pi][:, kv * 128:(kv + 1) * 128],
                                                                 stage_i[:, kv * 128:(kv + 1) * 128], ident[:]),
                         reads=["stage_i", "const"], writes=[("ps", pi)], signal=(kv == 3))
                P.op(ACT, lambda pi=pi: ACT.e.activation(out=KT[:, :, 0:128], in_=psA[pi][:, :].rearrange("p (k s) -> p k s", k=4),
                                                         func=AF.Copy), reads=[("ps", pi)], writes=KT_K)
                for c2 in range(2):
                    P.dma(SP, "ld_vout", vout[:, c2, :], cv_in[slot, sample_idx, c2 * 64:(c2 + 1) * 64, :], writes=["vout"])
                P.op(POOL, lambda: POOL.e.tensor_copy(out=VV[:, 0:2, :], in_=vout[:]), reads=["vout"], writes=VV_K)

            if ATTN_CUT <= 0:
                return
            sv, wvv = wload(l, "v", 0)
            nvc = (T + 63) // 64
            for c in range(nvc):
                if ATTN_VSTAGE < 1:
                    break
                rows = min(64, T - c * 64)
                pi = nextps()
                for k in range(DC):
                    P.op(PE, lambda k=k, c=c, rows=rows, pi=pi: PE.e.matmul(
                        psA[pi][0:rows, 0:256], lhsT=xn[:, k, c * 64: c * 64 + rows], rhs=wvv[:, k, :],
                        start=(k == 0), stop=(k == DC - 1)),
                        reads=[("ring", sv)] + XN_ALL, writes=[("ps", pi)], signal=(k == DC - 1))
                if ATTN_VSTAGE < 2:
                    continue
                if ATTN_VSTAGE >= 3 and emit_out and c >= nvc - 2:
                    vi = (c - (nvc - 2)) if nvc >= 2 else 0
                    P.op(ACT, lambda rows=rows, pi=pi, vi=vi: ACT.e.activation(out=vout[0:rows, vi, :], in_=psA[pi][0:rows, 0:256],
                                                                              func=AF.Copy), reads=[("ps", pi)], writes=["vout"])
                    P.op(POOL, lambda c=c, rows=rows, vi=vi: POOL.e.tensor_copy(out=VV[0:rows, 2 + c, :], in_=vout[0:rows, vi, :]),
                         reads=["vout"], writes=VV_K)
                else:
                    P.op(ACT, lambda c=c, rows=rows, pi=pi: ACT.e.activation(out=VV[0:rows, 2 + c, :], in_=psA[pi][0:rows, 0:256],
                                                                            func=AF.Copy), reads=[("ps", pi)], writes=VV_K)

            if ATTN_CUT <= 1:
                return
            def qk_post(pi, gsel, dst_bf, dst_keys, dst_f32, i):
                xs, t1, rs_ = acc[i], sil[i], (rstd if i == 0 else mean)
                RS = "rstd" if i == 0 else "mean"
                P.op(ACT, lambda: ACT.e.activation(out=xs[:, 0:T], in_=psA[pi][:, 0:T], func=AF.Copy),
                     reads=[("ps", pi)], writes=[("acc", i)])
                P.op(ACT, lambda: ACT.e.activation(out=sqh[i][:, 0:T], in_=xs[:, 0:T], func=AF.Square),
                     reads=[("acc", i)], writes=[("act", 42 + i)])
                p2, pr = nextps(), nextps()
                P.op(PE, lambda: PE.e.matmul(psA[p2][:, 0:T], lhsT=bd[:], rhs=sqh[i][:, 0:T], start=True, stop=True),
                     reads=[("act", 42 + i), "bd"], writes=[("ps", p2)])
                P.op(PE, lambda: PE.e.matmul(psA[pr][:, 0:T], lhsT=rgm[:, slot, gsel, :], rhs=xs[:, 0:T], start=True, stop=True),
                     reads=[("acc", i), "rgm"], writes=[("ps", pr)])
                P.op(DVE, lambda: DVE.e.tensor_scalar(out=rs_[:, 0:T], in0=psA[p2][:, 0:T], scalar1=1.0 / 64, scalar2=EPS,
                                                      op0=ALU.mult, op1=ALU.add), reads=[("ps", p2)], writes=[RS])
                P.op(ACT, lambda: ACT.e.activation(out=rs_[:, 0:T], in_=rs_[:, 0:T], func=AF.Sqrt), reads=[RS], writes=[RS])
                P.op(DVE, lambda: DVE.e.reciprocal(out=rs_[:, 0:T], in_=rs_[:, 0:T]), reads=[RS], writes=[RS])
                P.op(DVE, lambda: DVE.e.scalar_tensor_tensor(out=t1[:, 0:T], in0=xs[:, 0:T], scalar=gqk[:, slot, gsel:gsel + 1],
                                                             in1=cosT[:, 0:T], op0=ALU.mult, op1=ALU.mult),
                     reads=[("acc", i), "const", ("gext", 0)], writes=[("sil", i)])
                P.op(DVE, lambda: DVE.e.tensor_tensor(out=xs[:, 0:T], in0=psA[pr][:, 0:T], in1=sinT[:, 0:T], op=ALU.mult),
                     reads=[("ps", pr), ("gext", 1)], writes=[("acc", i)])
                P.op(DVE, lambda: DVE.e.tensor_tensor(out=t1[:, 0:T], in0=t1[:, 0:T], in1=xs[:, 0:T], op=ALU.add),
                     reads=[("sil", i), ("acc", i)], writes=[("sil", i)])
                P.op(DVE, lambda: DVE.e.tensor_tensor(out=t1[:, 0:T], in0=t1[:, 0:T], in1=rs_[:, 0:T], op=ALU.mult),
                     reads=[("sil", i), ("rs", i)], writes=[("sil", i)])
                P.op(ACT, lambda: ACT.e.activation(out=dst_bf, in_=t1[:, 0:T], func=AF.Copy), reads=[("sil", i)], writes=dst_keys)
                if dst_f32 is not None:
                    P.op(POOL, lambda: POOL.e.tensor_copy(out=dst_f32, in_=t1[:, T - min(T, 128):T]),
                         reads=[("sil", i)], writes=["kout"])

            sk, wvk = wload(l, "kd", 0)
            for kv in range(4):
                pi = nextps()
                for k in range(DC):
                    P.op(PE, lambda k=k, kv=kv, pi=pi: PE.e.matmul(
                        psA[pi][:, 0:T], lhsT=wvk[:, k, kv * 128:(kv + 1) * 128], rhs=xn[:, k, 0:T],
                        start=(k == 0), stop=(k == DC - 1)),
                        reads=[("ring", sk)] + XN_ALL, writes=[("ps", pi)], signal=(k == DC - 1))
                qk_post(pi, 1, KT[:, kv, 128:128 + T], KT_K, kout[:, kv, 0:min(T, 128)] if emit_out else None, kv % 2)
            if ATTN_CUT <= 2:
                return
            cpbq = scratch[(l, "q")][5] // 128
            cur = None
            for m in range(DC):
                b, off = m // cpbq, m % cpbq
                if off == 0:
                    cur = wload(l, "q", b)
                sq_, wvq = cur
                pi = nextps()
                for k in range(DC):
                    P.op(PE, lambda k=k, wvq=wvq, off=off, pi=pi: PE.e.matmul(
                        psA[pi][:, 0:T], lhsT=wvq[:, k, off * 128:(off + 1) * 128], rhs=xn[:, k, 0:T],
                        start=(k == 0), stop=(k == DC - 1)),
                        reads=[("ring", sq_)] + XN_ALL, writes=[("ps", pi)], signal=(k == DC - 1))
                qk_post(pi, 0, qT[:, m, 0:T], [("act", m)], None, m % 2)

            if ATTN_CUT <= 3:
                return
            unit = 0
            sc_rot = 0
            for c in range(NQC):
                if sample_idx is None:
                    kchunks = [(c + j, 64) for j in range(3) if not (first_prompt and c + j < 2)]
                else:
                    kchunks = [(0, 64), (1, 64), (2, T)]
                for kv in range(4):
                    pO, pD = 2 * (unit % 2), 2 * (unit % 2) + 1
                    unit += 1
                    for ki, (kc, ks) in enumerate(kchunks):
                        sl_ = sc_rot % 2
                        sc_rot += 1
                        pS = [4 + 2 * sl_, 5 + 2 * sl_]
                        for par in range(2):
                            for pr_ in range(4):
                                hd = kv * 8 + 2 * pr_ + par
                                qc = hd // 2
                                P.op(PE, lambda kc=kc, ks=ks, par=par, pr_=pr_, qc=qc, kv=kv, c=c, pS=pS: PE.e.matmul(
                                    psA[pS[par]][0:ks, pr_ * CL:(pr_ + 1) * CL],
                                    lhsT=KT[64 * par:64 * par + 64, kv, kc * 64: kc * 64 + ks],
                                    rhs=qT[64 * par:64 * par + 64, qc, c * CL:(c + 1) * CL], start=True, stop=True),
                                    reads=KT_K + [("act", qc)], writes=[("ps", pS[par])], signal=(pr_ == 3))
                            P.op(ACT, lambda ks=ks, par=par, sl_=sl_, pS=pS: ACT.e.activation(
                                out=ET[0:ks, 2 * sl_ + par, 0:4 * CL], in_=psA[pS[par]][0:ks, 0:4 * CL], func=AF.Exp, scale=0.125),
                                reads=[("ps", pS[par])], writes=[("xn", 2 * sl_ + par)])
                        for par in range(2):
                            first, last = (ki == 0), (ki == len(kchunks) - 1)
                            P.op(PE, lambda kc=kc, ks=ks, par=par, sl_=sl_, kv=kv, pO=pO, first=first, last=last: PE.e.matmul(
                                psA[pO][64 * par:64 * par + 64, 0:4 * CL], lhsT=VV[0:ks, kc, kv * 64:(kv + 1) * 64],
                                rhs=ET[0:ks, 2 * sl_ + par, 0:4 * CL], start=first, stop=last),
                                reads=VV_K + [("xn", 2 * sl_ + par)], writes=[("ps", pO)], signal=False)
                            P.op(PE, lambda ks=ks, par=par, sl_=sl_, pD=pD, first=first, last=last: PE.e.matmul(
                                psA[pD][64 * par:64 * par + 64, 0:4 * CL], lhsT=ones_bf[0:ks, 0:64],
                                rhs=ET[0:ks, 2 * sl_ + par, 0:4 * CL], start=first, stop=last),
                                reads=["ones", ("xn", 2 * sl_ + par)], writes=[("ps", pD)], signal=(last and par == 1))
                    for pr_ in range(4):
                        P.op(DVE, lambda pr_=pr_, kv=kv, pD=pD: DVE.e.tensor_scalar(
                            out=dnm[:, pr_ * CL:(pr_ + 1) * CL], in0=psA[pD][:, pr_ * CL:(pr_ + 1) * CL],
                            scalar1=esink[:, slot, kv * 4 + pr_: kv * 4 + pr_ + 1], scalar2=None, op0=ALU.add),
                            reads=[("ps", pD), "esink"], writes=["dnm"])
                    P.op(DVE, lambda: DVE.e.reciprocal(out=dnm[:, 0:4 * CL], in_=dnm[:, 0:4 * CL]), reads=["dnm"], writes=["dnm"])
                    P.op(DVE, lambda kv=kv, c=c, pO=pO: DVE.e.tensor_tensor(
                        out=oT[:, kv * 4:(kv + 1) * 4, c * CL:(c + 1) * CL],
                        in0=psA[pO][:, 0:4 * CL].rearrange("p (a q) -> p a q", a=4),
                        in1=dnm[:, 0:4 * CL].rearrange("p (a q) -> p a q", a=4), op=ALU.mult),
                        reads=[("ps", pO), "dnm"], writes=[("act", 16 + kv * 4 + a_) for a_ in range(4)])

            if ATTN_CUT <= 4:
                return
            if sample_idx is None:
                P.op(POOL, lambda: POOL.e.tensor_copy(out=khist[:, slot], in_=KT[:, :, T:T + 128]), reads=KT_K, writes=["khist"])
                P.op(POOL, lambda: POOL.e.tensor_copy(out=vhist[:, slot], in_=VV[:, nvc:nvc + 2, :]), reads=VV_K, writes=["vhist"])

            cpbo = scratch[(l, "wo")][5] // 128
            for b in range(scratch[(l, "wo")][6]):
                so, wvo = wload(l, "wo", b)
                for off in range(cpbo):
                    m = b * cpbo + off
                    pi = nextps()
                    for k in range(DC):
                        P.op(PE, lambda k=k, wvo=wvo, off=off, pi=pi: PE.e.matmul(
                            psA[pi][:, 0:T], lhsT=wvo[:, k, off * 128:(off + 1) * 128], rhs=oT[:, k, 0:T],
                            start=(k == 0), stop=(k == DC - 1)),
                            reads=[("ring", so)] + OT_K, writes=[("ps", pi)], signal=(k == DC - 1))
                    P.op(DVE, lambda m=m, pi=pi: DVE.e.tensor_tensor(out=h[:, m, 0:T], in0=psA[pi][:, 0:T],
                                                                     in1=h[:, m, 0:T], op=ALU.add),
                         reads=[("ps", pi), "h"], writes=["h"])

        def attn_cache_out(slot, T, dst_k, dst_v, src_k=None, src_v=None):
            n_new = min(T, 128)
            if n_new < 128:
                P.dma(SP, "st_kcopy", dst_k[0:128 - n_new, :], src_k[n_new:128, :])
                P.dma(SP, "st_kcopy", dst_v[0:128 - n_new, :], src_v[n_new:128, :])
            pi = nextps()
            for kv in range(4):
                P.op(PE, lambda kv=kv, pi=pi: PE.e.transpose(psA[pi][0:n_new, kv * 128:(kv + 1) * 128], kout[:, kv, 0:n_new], ident[:]),
                     reads=["kout", "const"], writes=[("ps", pi)], signal=(kv == 3))
            P.op(ACT, lambda pi=pi: ACT.e.activation(
                out=stage_o[0:n_new, 0:256].rearrange("p (k d) -> p k d", k=4),
                in_=psA[pi][0:n_new, :].rearrange("p (k x) -> p k x", k=4)[:, :, 0:64], func=AF.Copy),
                reads=[("ps", pi)], writes=["stage_o"])
            P.dma(SP, "st_stage_o", dst_k[128 - n_new:128, :], stage_o[0:n_new, 0:256], reads=["stage_o"])
            if n_new == 128:
                for c2 in range(2):
                    P.dma(SP, "st_vout", dst_v[c2 * 64:(c2 + 1) * 64, :], vout[:, c2, :], reads=["vout"])
            else:
                P.dma(SP, "st_vout", dst_v[128 - n_new:128, :], vout[0:n_new, 0, :], reads=["vout"])

        def gla(l, slot, T, CL, first_prompt, sample_idx, last_prompt):
            NCH = T // CL
            actf = act[:].rearrange("p c t -> p (c t)")
            qtT = act[:, 0:8, :]; ktT = act[:, 8:16, :]; vT = act[:, 16:32, :]
            Sb = act[:, 32:40, :]
            vtok = actf[:, 40 * TP: 44 * TP]
            Sst = xn[:].rearrange("p c t -> p (c t)").bitcast(F32).rearrange("p (c t) -> p c t", t=TP)
            ktok = gext[0][:, 0:512].bitcast(BF16)
            AmT = gext[1][:, 0:128].bitcast(BF16).rearrange("p (a q) -> p a q", a=4)
            QT_K = [("act", j) for j in range(0, 8)]; KT_K = [("act", j) for j in range(8, 16)]
            VT_K = [("act", j) for j in range(16, 32)]; SB_K = [("act", j) for j in range(32, 40)]
            VTOK_K = [("act", j) for j in range(40, 44)]

            rmsnorm(g_mix, l, T)
            sg, wgl = wload(l, "ggl", 0)
            pi = nextps()
            for k in range(DC):
                P.op(PE, lambda k=k, pi=pi: PE.e.matmul(psA[pi][0:16, 0:T], lhsT=wgl[:, k, 0:16], rhs=xn[:, k, 0:T],
                                                        start=(k == 0), stop=(k == DC - 1)),
                     reads=[("ring", sg)] + XN_ALL, writes=[("ps", pi)], signal=(k == DC - 1))
            P.op(ACT, lambda pi=pi: ACT.e.activation(out=glT[:, 0:T], in_=psA[pi][0:16, 0:T], func=AF.Copy),
                 reads=[("ps", pi)], writes=["glT"])
            blk_q = {}; blk_k = {}
            for m in range(8):
                i = m % 2
                for nm_, cache in (("gq", blk_q), ("gk", blk_k)):
                    b = m // 4
                    if b not in cache:
                        cache[b] = wload(l, nm_, b)
                (sq_, wq), (sk_, wk) = blk_q[m // 4], blk_k[m // 4]
                off = m % 4
                pq, pk, pl = nextps(), nextps(), nextps()
                for (pi_, wv_, s__) in ((pq, wq, sq_), (pk, wk, sk_)):
                    for k in range(DC):
                        P.op(PE, lambda k=k, pi_=pi_, wv_=wv_, off=off: PE.e.matmul(
                            psA[pi_][:, 0:T], lhsT=wv_[:, k, off * 128:(off + 1) * 128], rhs=xn[:, k, 0:T],
                            start=(k == 0), stop=(k == DC - 1)),
                            reads=[("ring", s__)] + XN_ALL, writes=[("ps", pi_)], signal=(k == DC - 1))
                P.op(PE, lambda m=m, pl=pl: PE.e.matmul(psA[pl][:, 0:T], lhsT=wgu[:, slot, m * 128:(m + 1) * 128], rhs=glT[:, 0:T],
                                                        start=True, stop=True), reads=["wgu", "glT"], writes=[("ps", pl)])
                lt, eb, enb = acc[i], sil[i], (rstd if i == 0 else mean)
                EB = "rstd" if i == 0 else "mean"
                P.op(ACT, lambda m=m, pl=pl, lt=lt: ACT.e.activation(out=lt[:, 0:T], in_=psA[pl][:, 0:T], func=AF.Exp, scale=-1.0,
                                                                     bias=ngb[:, slot, m:m + 1]), reads=[("ps", pl), "ngb"], writes=[("acc", i)])
                P.op(ACT, lambda lt=lt: ACT.e.activation(out=lt[:, 0:T], in_=lt[:, 0:T], func=AF.Ln, bias=1.0),
                     reads=[("acc", i)], writes=[("acc", i)])
                P.op(DVE, lambda lt=lt: DVE.e.tensor_tensor_scan(out=gate[:, 0:T], data0=cmask[:, 0:T], data1=lt[:, 0:T], initial=0.0,
                                                                 op0=ALU.mult, op1=ALU.add), reads=[("acc", i), "cmask"], writes=["gate"])
                P.op(ACT, lambda eb=eb: ACT.e.activation(out=eb[:, 0:T], in_=gate[:, 0:T], func=AF.Exp, scale=-1.0 / 16),
                     reads=["gate"], writes=[("sil", i)])
                P.op(ACT, lambda enb=enb: ACT.e.activation(out=enb[:, 0:T], in_=gate[:, 0:T], func=AF.Exp, scale=1.0 / 16),
                     reads=["gate"], writes=[EB])
                P.op(DVE, lambda m=m, pq=pq, eb=eb: DVE.e.scalar_tensor_tensor(out=qtT[:, m, 0:T], in0=psA[pq][:, 0:T], scalar=1.0 / 16,
                                                                               in1=eb[:, 0:T], op0=ALU.mult, op1=ALU.mult),
                     reads=[("ps", pq), ("sil", i)], writes=[("act", m)])
                P.op(DVE, lambda m=m, pk=pk, enb=enb: DVE.e.tensor_tensor(out=ktT[:, m, 0:T], in0=psA[pk][:, 0:T], in1=enb[:, 0:T],
                                                                          op=ALU.mult), reads=[("ps", pk), EB], writes=[("act", 8 + m)])
                P.op(POOL, lambda m=m, eb=eb: POOL.e.tensor_copy(
                    out=elast[:, m, 0:NCH], in_=eb[:, 0:T].rearrange("p (c t) -> p c t", t=CL)[:, :, CL - 1]),
                    reads=[("sil", i)], writes=["elast"])
            cur = None
            for m in range(DC):
                if m % 4 == 0:
                    cur = wload(l, "gv", m // 4)
                sv_, wv_ = cur
                off = m % 4
                pi = nextps()
                for k in range(DC):
                    P.op(PE, lambda k=k, pi=pi, wv_=wv_, off=off: PE.e.matmul(
                        psA[pi][:, 0:T], lhsT=wv_[:, k, off * 128:(off + 1) * 128], rhs=xn[:, k, 0:T],
                        start=(k == 0), stop=(k == DC - 1)),
                        reads=[("ring", sv_)] + XN_ALL, writes=[("ps", pi)], signal=(k == DC - 1))
                P.op(ACT, lambda m=m, pi=pi: ACT.e.activation(out=vT[:, m, 0:T], in_=psA[pi][:, 0:T], func=AF.Copy),
                     reads=[("ps", pi)], writes=[("act", 16 + m)])

            if GLA_CUT <= 1:
                return
            if sample_idx is not None:
                P.dma(SP, "ld_sst", Sst, s_gla[slot, sample_idx].rearrange("h (kk p) v -> p (h kk) v", p=128), writes=XN_ALL)
            elif first_prompt:
                P.op(POOL, lambda: POOL.e.memset(Sst, 0.0), writes=XN_ALL)
            else:
                P.dma(SP, "ld_sst", Sst, gla_carry[slot], reads=["gla_carry"], writes=XN_ALL)
            P.op(POOL, lambda: POOL.e.memset(vtok[:, :], 0.0), writes=VTOK_K)
            P.op(POOL, lambda: POOL.e.memset(gext[0][:, 0:512], 0.0), writes=[("gext", 0)])
            P.op(POOL, lambda: POOL.e.memset(gext[1][:, 0:128], 0.0), writes=[("gext", 1)])

            for c in range(NCH):
                cs = slice(c * CL, (c + 1) * CL)
                for g4 in range(4):
                    pb = nextps()
                    for a_ in range(4):
                        m = 4 * g4 + a_
                        P.op(PE, lambda m=m, a_=a_, pb=pb, cs=cs: PE.e.matmul(
                            psA[pb][0:CL, a_ * 128:(a_ + 1) * 128], lhsT=vT[:, m, cs], rhs=ident_bf[:], start=True, stop=True),
                            reads=[("act", 16 + m), "ident_bf"], writes=[("ps", pb)], signal=(a_ == 3))
                    P.op(ACT, lambda g4=g4, pb=pb: ACT.e.activation(out=vtok[0:CL, g4 * 512:(g4 + 1) * 512], in_=psA[pb][0:CL, :],
                                                                  func=AF.Copy), reads=[("ps", pb)], writes=VTOK_K)
                for g2 in range(2):
                    pb = nextps()
                    for a_ in range(4):
                        m = 4 * g2 + a_
                        P.op(PE, lambda m=m, a_=a_, pb=pb, cs=cs: PE.e.matmul(
                            psA[pb][0:CL, a_ * 128:(a_ + 1) * 128], lhsT=ktT[:, m, cs], rhs=ident_bf[:], start=True, stop=True),
                            reads=[("act", 8 + m), "ident_bf"], writes=[("ps", pb)], signal=(a_ == 3))
                    P.op(ACT, lambda g2=g2, pb=pb: ACT.e.activation(out=ktok[0:CL, g2 * 512:(g2 + 1) * 512], in_=psA[pb][0:CL, :],
                                                                  func=AF.Copy), reads=[("ps", pb)], writes=[("gext", 0)])
                for j in range(8):
                    P.op(POOL if j % 2 else ACT, (lambda j=j: POOL.e.tensor_copy(out=Sb[:, j, :], in_=Sst[:, j, :])) if j % 2 else
                         (lambda j=j: ACT.e.activation(out=Sb[:, j, :], in_=Sst[:, j, :], func=AF.Copy)),
                         reads=[("xn", 2 * j), ("xn", 2 * j + 1)], writes=[("act", 32 + j)])
                pa = nextps()
                for hd in range(4):
                    for kk in range(2):
                        P.op(PE, lambda hd=hd, kk=kk, pa=pa, cs=cs: PE.e.matmul(
                            psA[pa][0:CL, hd * CL:(hd + 1) * CL], lhsT=ktT[:, 2 * hd + kk, cs], rhs=qtT[:, 2 * hd + kk, cs],
                            start=(kk == 0), stop=(kk == 1)),
                            reads=QT_K + KT_K, writes=[("ps", pa)], signal=(hd == 3 and kk == 1))
                P.op(DVE, lambda pa=pa: DVE.e.tensor_tensor(
                    out=AmT[0:CL, :, 0:CL], in0=psA[pa][0:CL, 0:4 * CL].rearrange("p (a q) -> p a q", a=4),
                    in1=tri4[0:CL, :].rearrange("p (a q) -> p a q", a=4)[:, :, 0:CL], op=ALU.mult),
                    reads=[("ps", pa), "const"], writes=[("gext", 1)])
                po = [nextps(), nextps()]
                for f in range(DC):
                    hd = f // 4
                    dst = psA[po[f // 8]][:, (f % 8) * CL:(f % 8 + 1) * CL]
                    for kk in range(2):
                        P.op(PE, lambda f=f, hd=hd, kk=kk, dst=dst, cs=cs: PE.e.matmul(
                            dst, lhsT=Sb[:, 2 * hd + kk, (f % 4) * 128:(f % 4 + 1) * 128], rhs=qtT[:, 2 * hd + kk, cs],
                            start=(kk == 0), stop=False),
                            reads=SB_K + QT_K, writes=[("ps", po[f // 8])], signal=False)
                    P.op(PE, lambda f=f, hd=hd, dst=dst: PE.e.matmul(
                        dst, lhsT=vtok[:, f * 128:(f + 1) * 128], rhs=AmT[:, hd, 0:CL], start=False, stop=True),
                        reads=VTOK_K + [("gext", 1)], writes=[("ps", po[f // 8])], signal=(f % 8 == 7))
                o32 = [acc[0], acc[1]]
                for hf in range(2):
                    P.op(ACT, lambda hf=hf, po=po: ACT.e.activation(out=o32[hf][:, 0:8 * CL], in_=psA[po[hf]][:, 0:8 * CL], func=AF.Copy),
                         reads=[("ps", po[hf])], writes=[("acc", hf)])
                    P.op(ACT, lambda hf=hf: ACT.e.activation(out=osq[:, hf * 512: hf * 512 + 8 * CL], in_=o32[hf][:, 0:8 * CL],
                                                             func=AF.Square), reads=[("acc", hf)], writes=[("osq", hf)])
                for j in range(8):
                    hd = j // 2
                    pd = nextps()
                    P.op(PE, lambda j=j, hd=hd, pd=pd: PE.e.matmul(psA[pd][:, :], lhsT=ktok[:, j * 128:(j + 1) * 128],
                                                                  rhs=vtok[:, hd * 512:(hd + 1) * 512], start=True, stop=True),
                         reads=[("gext", 0)] + VTOK_K, writes=[("ps", pd)])
                    SK = [("xn", 2 * j), ("xn", 2 * j + 1)]
                    P.op(POOL, lambda j=j, c=c: POOL.e.tensor_scalar(out=Sst[:, j, :], in0=Sst[:, j, :], scalar1=elast[:, j, c:c + 1],
                                                                     scalar2=None, op0=ALU.mult), reads=SK + ["elast"], writes=SK)
                    P.op(DVE, lambda j=j, c=c, pd=pd: DVE.e.scalar_tensor_tensor(
                        out=Sst[:, j, :], in0=psA[pd][:, :], scalar=elast[:, j, c:c + 1], in1=Sst[:, j, :],
                        op0=ALU.mult, op1=ALU.add), reads=[("ps", pd), "elast"] + SK, writes=SK)
                pn = nextps()
                for hd in range(4):
                    for fi in range(4):
                        f = 4 * hd + fi
                        P.op(PE, lambda hd=hd, fi=fi, f=f, pn=pn: PE.e.matmul(
                            psA[pn][:, hd * CL:(hd + 1) * CL], lhsT=ones_bf[:],
                            rhs=osq[:, (f // 8) * 512 + (f % 8) * CL: (f // 8) * 512 + (f % 8 + 1) * CL],
                            start=(fi == 0), stop=(fi == 3)),
                            reads=[("osq", f // 8), "ones"], writes=[("ps", pn)], signal=(hd == 3 and fi == 3))
                P.op(DVE, lambda pn=pn: DVE.e.tensor_scalar(out=gate[:, 0:4 * CL], in0=psA[pn][:, 0:4 * CL], scalar1=1.0 / 512, scalar2=EPS,
                                                            op0=ALU.mult, op1=ALU.add), reads=[("ps", pn)], writes=["gate"])
                P.op(ACT, lambda: ACT.e.activation(out=gate[:, 0:4 * CL], in_=gate[:, 0:4 * CL], func=AF.Sqrt), reads=["gate"], writes=["gate"])
                P.op(DVE, lambda: DVE.e.reciprocal(out=gate[:, 0:4 * CL], in_=gate[:, 0:4 * CL]), reads=["gate"], writes=["gate"])
                for f in range(DC):
                    hd, fi = f // 4, f % 4
                    P.op(DVE, lambda f=f, hd=hd, fi=fi, cs=cs: DVE.e.scalar_tensor_tensor(
                        out=vT[:, f, cs], in0=o32[f // 8][:, (f % 8) * CL:(f % 8 + 1) * CL], scalar=onw[:, slot, fi:fi + 1],
                        in1=gate[:, hd * CL:(hd + 1) * CL], op0=ALU.mult, op1=ALU.mult),
                        reads=[("acc", f // 8), "const", "gate"], writes=[("act", 16 + f)])

            if sample_idx is not None:
                P.dma(SP, "st_sst", o_gla_s[slot, sample_idx].rearrange("h (kk p) v -> p (h kk) v", p=128), Sst, reads=XN_ALL)
            else:
                P.dma(SP, "st_sst", gla_carry[slot], Sst, reads=XN_ALL, writes=["gla_carry"])
                if last_prompt:
                    P.dma(SP, "st_sst", o_gla_p[slot].rearrange("h (kk p) v -> p (h kk) v", p=128), Sst, reads=XN_ALL)

            if GLA_CUT <= 2:
                return
            rmsnorm(g_mix, l, T)
            cur = None
            for m in range(DC):
                if m % 4 == 0:
                    cur = wload(l, "gr", m // 4)
                sr_, wr_ = cur
                off = m % 4
                i = m % 2
                pi = nextps()
                for k in range(DC):
                    P.op(PE, lambda k=k, pi=pi, wr_=wr_, off=off: PE.e.matmul(
                        psA[pi][:, 0:T], lhsT=wr_[:, k, off * 128:(off + 1) * 128], rhs=xn[:, k, 0:T],
                        start=(k == 0), stop=(k == DC - 1)),
                        reads=[("ring", sr_)] + XN_ALL, writes=[("ps", pi)], signal=(k == DC - 1))
                P.op(ACT, lambda pi=pi, i=i: ACT.e.activation(out=sil[i][:, 0:T], in_=psA[pi][:, 0:T], func=AF.Silu),
                     reads=[("ps", pi)], writes=[("sil", i)])
                P.op(DVE, lambda m=m, i=i: DVE.e.tensor_tensor(out=vT[:, m, 0:T], in0=vT[:, m, 0:T], in1=sil[i][:, 0:T], op=ALU.mult),
                     reads=[("act", 16 + m), ("sil", i)], writes=[("act", 16 + m)])
            cpbo = scratch[(l, "cwo")][5] // 128
            for b in range(scratch[(l, "cwo")][6]):
                so, wvo = wload(l, "cwo", b)
                for off in range(cpbo):
                    m = b * cpbo + off
                    pi = nextps()
                    for k in range(DC):
                        P.op(PE, lambda k=k, wvo=wvo, off=off, pi=pi: PE.e.matmul(
                            psA[pi][:, 0:T], lhsT=wvo[:, k, off * 128:(off + 1) * 128], rhs=vT[:, k, 0:T],
                            start=(k == 0), stop=(k == DC - 1)),
                            reads=[("ring", so)] + VT_K, writes=[("ps", pi)], signal=(k == DC - 1))
                    P.op(DVE, lambda m=m, pi=pi: DVE.e.tensor_tensor(out=h[:, m, 0:T], in0=psA[pi][:, 0:T],
                                                                     in1=h[:, m, 0:T], op=ALU.add),
                         reads=[("ps", pi), "h"], writes=["h"])

        def load_tokens_T(dst, n_chunks, src_rows, T, reskey):
            for tb in range((T + 127) // 128):
                rows = min(128, T - tb * 128)
                for c0 in range(0, n_chunks, 4):
                    ncol = min(4, n_chunks - c0)
                    P.dma(SP, "ld_stage", stage_i[0:rows, 0:ncol * 128],
                          src_rows[tb * 128: tb * 128 + rows, c0 * 128:(c0 + ncol) * 128], writes=["stage_i"])
                    pi = nextps()
                    for a in range(ncol):
                        P.op(PE, lambda a=a, pi=pi, rows=rows: PE.e.transpose(
                            psA[pi][:, a * 128: a * 128 + rows], stage_i[0:rows, a * 128:(a + 1) * 128],
                            ident[0:rows, 0:rows]),
                            reads=["stage_i", "const"], writes=[("ps", pi)], signal=(a == ncol - 1))
                    for a in range(ncol):
                        P.op(ACT, lambda a=a, pi=pi, rows=rows, c0=c0, tb=tb: ACT.e.activation(
                            out=dst[:, c0 + a, tb * 128: tb * 128 + rows], in_=psA[pi][:, a * 128: a * 128 + rows],
                            func=AF.Copy), reads=[("ps", pi)], writes=(reskey if isinstance(reskey, list) else [reskey]))

        def store_tokens_T(dst_rows, src, n_chunks, T, reskey, semname):
            for tb in range((T + 127) // 128):
                rows = min(128, T - tb * 128)
                for c0 in range(0, n_chunks, 4):
                    ncol = min(4, n_chunks - c0)
                    pi = nextps()
                    for a in range(ncol):
                        P.op(PE, lambda a=a, pi=pi, rows=rows, c0=c0, tb=tb: PE.e.transpose(
                            psA[pi][0:rows, a * 128:(a + 1) * 128], src[:, c0 + a, tb * 128: tb * 128 + rows], ident[:]),
                            reads=(reskey if isinstance(reskey, list) else [reskey]) + ["const"], writes=[("ps", pi)],
                            signal=(a == ncol - 1))
                    P.op(ACT, lambda pi=pi, rows=rows, ncol=ncol: ACT.e.activation(
                        out=stage_o[0:rows, 0:ncol * 128], in_=psA[pi][0:rows, 0:ncol * 128], func=AF.Copy),
                        reads=[("ps", pi)], writes=["stage_o"])
                    P.dma(SP, semname, dst_rows[tb * 128: tb * 128 + rows, c0 * 128:(c0 + ncol) * 128],
                          stage_o[0:rows, 0:ncol * 128], reads=["stage_o"])

        def ple(l, T, p_rows):
            rmsnorm(g_ple, l, T)
            load_tokens_T(pT, 2, p_rows, T, "pT")
            wbp = scratch[(l, "pproj")]
            assert wbp[6] == 1
            P.dma(SP, "ld_wpp", wpp[:].rearrange("p k c -> p (k c)"), wbp[0][0], reads=["wb"], writes=["wpp"])
            cpb = scratch[(l, "pgate")][5] // 128
            cur = None
            for m in range(DC):
                b, off = m // cpb, m % cpb
                if off == 0:
                    cur = wload(l, "pgate", b)
                s, wv = cur
                pgi, ppi = nextps(), nextps()
                for k in range(DC):
                    P.op(PE, lambda k=k, wv=wv, off=off, pgi=pgi: PE.e.matmul(
                        psA[pgi][:, 0:T], lhsT=wv[:, k, off * 128:(off + 1) * 128], rhs=xn[:, k, 0:T],
                        start=(k == 0), stop=(k == DC - 1)),
                        reads=[("ring", s)] + XN_ALL, writes=[("ps", pgi)], signal=(k == DC - 1))
                for k in range(2):
                    P.op(PE, lambda k=k, m=m, ppi=ppi: PE.e.matmul(
                        psA[ppi][:, 0:T], lhsT=wpp[:, k, m * 128:(m + 1) * 128], rhs=pT[:, k, 0:T],
                        start=(k == 0), stop=(k == 1)),
                        reads=["wpp", "pT"], writes=[("ps", ppi)], signal=(k == 1))
                P.op(ACT, lambda pgi=pgi: ACT.e.activation(out=gate[:, 0:T], in_=psA[pgi][:, 0:T], func=AF.Sigmoid),
                     reads=[("ps", pgi)], writes=["gate"])
                P.op(DVE, lambda ppi=ppi: DVE.e.tensor_tensor(out=gate[:, 0:T], in0=psA[ppi][:, 0:T], in1=gate[:, 0:T],
                                                              op=ALU.mult), reads=[("ps", ppi), "gate"], writes=["gate"])
                P.op(DVE, lambda m=m: DVE.e.tensor_tensor(out=h[:, m, 0:T], in0=h[:, m, 0:T], in1=gate[:, 0:T],
                                                          op=ALU.add), reads=["gate", "h"], writes=["h"])

        def ffn_state_out(l, dst):
            pi = nextps()
            P.op(PE, lambda pi=pi: PE.e.transpose(psA[pi][0:2 * FC, 0:128], fhist[:, l].rearrange("p r c -> p (r c)"),
                                                  ident[:]), reads=FH(l) + ["const"], writes=[("ps", pi)])
            P.op(ACT, lambda pi=pi: ACT.e.activation(out=tr_o[0:2 * FC, :], in_=psA[pi][0:2 * FC, 0:128], func=AF.Copy),
                 reads=[("ps", pi)], writes=["tr_o"])
            for r in range(2):
                P.dma(SP, "st_tr_o", dst[r].rearrange("(c p) -> c p", p=128), tr_o[r * FC:(r + 1) * FC, :], reads=["tr_o"])

        def ffn_state_in(l, src):
            for r in range(2):
                P.dma(SP, "ld_tr_i", tr_i[r * FC:(r + 1) * FC, :], src[r].rearrange("(c p) -> c p", p=128), writes=["tr_i"])
            pi = nextps()
            P.op(PE, lambda pi=pi: PE.e.transpose(psA[pi][:, 0:2 * FC], tr_i[0:2 * FC, :], ident[0:2 * FC, 0:2 * FC]),
                 reads=["tr_i", "const"], writes=[("ps", pi)])
            P.op(ACT, lambda pi=pi: ACT.e.activation(out=fhist[:, l].rearrange("p r c -> p (r c)"),
                                                     in_=psA[pi][:, 0:2 * FC], func=AF.Copy),
                 reads=[("ps", pi)], writes=FH(l))

        def run_tile(T, x_rows, y_rows, p_rows_of_layer, first_prompt, sample_idx, last_prompt, tile_idx=0):
            load_tokens_T(h, DC, x_rows, T, "h")
            for l in range(layers):
                if sample_idx is not None and not SKIP_FFN:
                    ffn_state_in(l, s_ffn[l, sample_idx])
                kind, slot = KS[l]
                if kind == 0:
                    emit = (sample_idx is not None) or last_prompt
                    attention(l, slot, T, (TS if sample_idx is not None else 64), first_prompt,
                              (NPT_R - 1 if sample_idx is not None else tile_idx), sample_idx, emit)
                    if ATTN_CUT <= 5:
                        pass
                    elif sample_idx is not None:
                        attn_cache_out(slot, T, o_k_s[slot, sample_idx], o_v_s[slot, sample_idx],
                                       ck_in[slot, sample_idx], cv_in[slot, sample_idx])
                    elif last_prompt:
                        attn_cache_out(slot, T, o_k_p[slot], o_v_p[slot])
                if kind == 2:
                    gla(l, slot, T, (TS if sample_idx is not None else 64), first_prompt, sample_idx, last_prompt)
                if kind == 1:
                    if sample_idx is not None:
                        load_tokens_T(chist[:, slot], DC, s_conv[slot, sample_idx], CW - 1, CH_ALL)
                    conformer(l, slot, T, first_prompt)
                    if sample_idx is not None:
                        store_tokens_T(o_conv_s[slot, sample_idx], chist[:, slot], DC, CW - 1, CH_ALL, "st_stage_o")
                    elif last_prompt:
                        store_tokens_T(o_conv_p[slot], chist[:, slot], DC, CW - 1, CH_ALL, "st_stage_o")
                if SKIP_FFN:
                    continue
                conv_ffn(l, T, first_prompt)
                ple(l, T, p_rows_of_layer(l))
                if sample_idx is not None:
                    ffn_state_out(l, o_ffn_s[l, sample_idx])
                elif last_prompt:
                    ffn_state_out(l, o_ffn_p[l])
            store_tokens_T(y_rows, h, DC, T, "h", "st_stage_o")

        for t in range(n_ptiles):
            run_tile(TP, x_p[t * TP:(t + 1) * TP, :], y_p[t * TP:(t + 1) * TP, :],
                     lambda l, t=t: p_p[l, t * TP:(t + 1) * TP, :], t == 0, None, t == n_ptiles - 1, t)
        for s_ in range(spc):
            run_tile(TS, x_s[s_], y_s[s_], lambda l, s_=s_: p_s[l, s_], False, s_, False)

        P.drain_all(SP)

        block = E(nc.Block())

        @block.tensor
        def _(e): PE.replay(e)

        @block.scalar
        def _(e): ACT.replay(e)

        @block.vector
        def _(e): DVE.replay(e)

        @block.gpsimd
        def _(e): POOL.replay(e)

        @block.sync
        def _(e): SP.replay(e)

        nc._n_instr_est = P.n_instr
    return nc


SHARED_KEYS = ("norm_mix", "norm_ffn", "ple_norm", "ffn_conv_w", "ffn_conv_b",
               "ffn_w_up", "ffn_w_down", "ple_w_proj", "ple_w_gate")
B_KEYS = ("b_w_pw1", "b_w_pw2", "b_w_dw", "b_dw_bias", "b_ln_g", "b_ln_b")
A_KEYS = ("a_w_qkv", "a_w_o", "a_q_norm", "a_k_norm", "a_sinks")
C_KEYS = ("c_w_in", "c_w_o", "c_w_gate_up", "c_gate_bias", "c_out_norm")
ROPE_THETA = 10000.0
PAST_LEN = 2048


def _rope_tables(n_ptiles):
    half = 32
    inv = (1.0 / (ROPE_THETA ** (np.arange(half, dtype=np.float32) / half))).astype(np.float32)
    out = np.zeros((n_ptiles + 1, 2, 128, TP), np.float32)
    rows = np.arange(128) % half
    for t in range(n_ptiles + 1):
        pos = (np.arange(TP) + t * TP) if t < n_ptiles else (PAST_LEN + np.arange(TP))
        ang = (pos.astype(np.float32)[None, :] * inv[rows][:, None]).astype(np.float32)
        out[t, 0] = np.cos(ang); out[t, 1] = np.sin(ang)
    return out


def _psign():
    m = np.zeros((128, 128), np.float32)
    for c in range(128):
        if c % 64 < 32:
            m[c + 32, c] = -1.0
        else:
            m[c - 32, c] = 1.0
    return m


def make_in_maps(inp, n_cores, n_ptiles, spc, kinds=DEFAULT_KINDS):
    layers = len(kinds)
    _, nsl = _kinds_slots(kinds)
    ident = np.eye(128, dtype=np.float32)
    LP = max(n_ptiles * TP, TP)
    shared = {k: np.ascontiguousarray(inp[k][:layers]) for k in SHARED_KEYS}
    if nsl[0]:
        for k in A_KEYS:
            shared[k] = np.ascontiguousarray(inp[k][:nsl[0]])
        shared["rope"] = _rope_tables(n_ptiles)
        shared["psign"] = _psign()
    if nsl[1]:
        for k in B_KEYS:
            shared[k] = np.ascontiguousarray(inp[k][:nsl[1]])
    if nsl[2]:
        for k in C_KEYS:
            shared[k] = np.ascontiguousarray(inp[k][:nsl[2]])
        tri = np.zeros((128, 4, 64), np.float32)
        jj, ii = np.meshgrid(np.arange(64), np.arange(64), indexing="ij")
        tri[:64] = (jj <= ii).astype(np.float32)[:, None, :]
        shared["tri4"] = tri.reshape(128, 256)
    maps = []
    for c in range(n_cores):
        b = c % 2
        sl = slice(c * spc, c * spc + max(spc, 1))
        m = dict(shared)
        m["ident"] = ident
        m["x_p"] = np.ascontiguousarray(inp["x_prompt"][b, :LP])
        m["p_p"] = np.ascontiguousarray(inp["p_prompt"][:layers, b, :LP])
        m["x_s"] = np.ascontiguousarray(inp["x_sample"][sl])
        m["p_s"] = np.ascontiguousarray(inp["p_sample"][:layers, sl])
        m["s_ffn"] = np.ascontiguousarray(inp["state_ffn_conv"][:layers, sl])
        if nsl[0]:
            m["ck"] = np.ascontiguousarray(inp["cache_k_a"][:nsl[0], sl]).reshape(nsl[0], -1, 128, 256)
            m["cv"] = np.ascontiguousarray(inp["cache_v_a"][:nsl[0], sl]).reshape(nsl[0], -1, 128, 256)
        if nsl[1]:
            m["s_conv"] = np.ascontiguousarray(inp["state_conv_b"][:nsl[1], sl])
        if nsl[2]:
            m["s_gla"] = np.ascontiguousarray(inp["state_gla_c"][:nsl[2], sl])
        maps.append(m)
    return maps


def kernel(**inp):
    n_cores = 8
    spc = DEC_B // n_cores
    n_ptiles = SEQ // TP
    nc = build_program(n_ptiles, spc)
    in_maps = make_in_maps(inp, n_cores, n_ptiles, spc)
    res = run_bass_kernel_spmd(nc, in_maps, core_ids=list(range(n_cores))).results
    f32 = np.float32
    cat_p = lambda k, ax: np.stack([res[0][k], res[1][k]], ax).astype(f32)
    cat_s = lambda k, ax: np.concatenate([res[c][k] for c in range(n_cores)], ax).astype(f32)
    y_prompt = cat_p("y_p", 0)
    y_sample = cat_s("y_s", 0)
    new_ffn_p = cat_p("o_ffn_p", 1)
    new_ffn_s = cat_s("o_ffn_s", 1)
    new_conv_p = cat_p("o_conv_p", 1)
    new_conv_s = cat_s("o_conv_s", 1)
    kv5 = lambda a: a.reshape(a.shape[0], a.shape[1], 128, 4, 64)
    new_k_p, new_v_p = kv5(cat_p("o_k_p", 1)), kv5(cat_p("o_v_p", 1))
    new_k_s, new_v_s = kv5(cat_s("o_k_s", 1)), kv5(cat_s("o_v_s", 1))
    new_gla_p = cat_p("o_gla_p", 1)
    new_gla_s = cat_s("o_gla_s", 1)
    return (y_prompt, y_sample, new_k_p, new_v_p, new_conv_p, new_gla_p, new_ffn_p,
            new_k_s, new_v_s, new_conv_s, new_gla_s, new_ffn_s)
```

```python
import numpy as np
from contextlib import ExitStack
import concourse.bass as bass
import concourse.mybir as mybir
from concourse.bass_utils import run_bass_kernel_spmd

F32 = mybir.dt.float32
BF16 = mybir.dt.bfloat16
AF = mybir.ActivationFunctionType
ALU = mybir.AluOpType

D = 2048
DC = D // 128
DFF = 5632
FC = DFF // 128
PLE = 256
DEPTH = 4
EPS = 1e-6
TP = 512
TS = 32
SEQ = 8192
DEC_B = 32
DEC_T = 32

WBLK = 8192
NSLOT = 3
PAIRED = ("up", "pw1")
CW = 31


class _Eng:
    def __init__(self, name, eng, sem, same_engine_sync):
        self.name, self.e, self.sem = name, eng, sem
        self.count = 0
        self.pending = False
        self.seen = {}
        self.same_engine_sync = same_engine_sync
        self.q = []

    def replay(self, handle):
        self.e = handle
        for item in self.q:
            if item[0] == "wait":
                handle.wait_ge(item[1], item[2])
            elif item[0] == "ins":
                ins = item[1]()
                if item[2]:
                    ins.then_inc(self.sem, 1)
            else:
                _, out, in_, kw, sem = item
                handle.dma_start(out=out, in_=in_, **kw).then_inc(sem, 16)


class Prog:
    def __init__(self, nc, st, same_engine_sync=True):
        self.nc, self.st = nc, st
        self.res = {}
        self.sems = {}
        mk = lambda n: st.enter_context(nc.semaphore(n))
        self.pe = _Eng("pe", None, mk("s_pe"), False)
        ses_ad = (same_engine_sync is True)
        ses_p = (same_engine_sync is True) or (same_engine_sync == "pool")
        self.act = _Eng("act", None, mk("s_act"), ses_ad)
        self.dve = _Eng("dve", None, mk("s_dve"), ses_ad)
        self.pool = _Eng("pool", None, mk("s_pool"), ses_p)
        self.sp = _Eng("sp", None, mk("s_sp"), False)
        self.dma_sems = {}
        self.n_instr = 0

    def _r(self, key):
        r = self.res.get(key)
        if r is None:
            r = self.res[key] = {"w": None, "r": {}}
        return r

    def _wait(self, E, tok):
        if tok is None:
            return
        sem, val = tok
        k = id(sem)
        if sem is E.sem and not E.same_engine_sync:
            return
        if E.seen.get(k, 0) >= val:
            return
        E.q.append(("wait", sem, val))
        E.seen[k] = val
        self.n_instr += 1

    def _deps(self, E, reads, writes):
        for key in reads:
            self._wait(E, self._r(key)["w"])
        for key in writes:
            r = self._r(key)
            self._wait(E, r["w"])
            for tok in r["r"].values():
                self._wait(E, tok)

    def _commit(self, tok, reads, writes):
        sem, val = tok
        for key in writes:
            r = self._r(key)
            r["w"] = tok
            r["r"] = {}
        for key in reads:
            r = self._r(key)
            old = r["r"].get(id(sem))
            if old is None or old[1] < val:
                r["r"][id(sem)] = tok

    def op(self, E, fn, reads=(), writes=(), signal=True):
        self._deps(E, reads, writes)
        E.q.append(("ins", fn, signal))
        self.n_instr += 1
        if signal:
            E.count += 1
            E.pending = False
            tok = (E.sem, E.count)
        else:
            E.pending = True
            tok = (E.sem, E.count + 1)
        self._commit(tok, reads, writes)
        return tok

    def dma_sem(self, name):
        s = self.dma_sems.get(name)
        if s is None:
            s = self.dma_sems[name] = [self.st.enter_context(self.nc.semaphore("d_" + name)), 0]
        return s

    def dma(self, Q, semname, out, in_, reads=(), writes=(), **kw):
        self._deps(Q, reads, writes)
        s = self.dma_sem(semname)
        s[1] += 16
        Q.q.append(("dma", out, in_, kw, s[0]))
        self.n_instr += 1
        tok = (s[0], s[1])
        self._commit(tok, reads, writes)
        return tok

    def drain_all(self, E):
        for s, tot in self.dma_sems.values():
            if tot:
                self._wait(E, (s, tot))
        for X in (self.pe, self.act, self.dve, self.pool):
            if X.count and X is not E:
                self._wait(E, (X.sem, X.count))


def _wspec(layer, kind, slot):
    out = []
    if kind == 0:
        out.append(("q", "a_w_qkv", slot, D, 0, 2048))
        out.append(("kd", "a_w_qkv", slot, D, 2048, 512))
        out.append(("v", "a_w_qkv", slot, D, 2304, 256))
        out.append(("wo", "a_w_o", slot, D, 0, D))
    elif kind == 1:
        out.append(("pw1", "b_w_pw1", slot, D, 0, 2 * D))
        out.append(("pw2", "b_w_pw2", slot, D, 0, D))
    else:
        out.append(("gq", "c_w_in", slot, D, 0, 1024))
        out.append(("gk", "c_w_in", slot, D, 1024, 1024))
        out.append(("ggl", "c_w_in", slot, D, 6144, 16))
        out.append(("gv", "c_w_in", slot, D, 2048, 2048))
        out.append(("gr", "c_w_in", slot, D, 4096, 2048))
        out.append(("cwo", "c_w_o", slot, D, 0, D))
    out.append(("up", "ffn_w_up", layer, D, 0, 2 * DFF))
    out.append(("down", "ffn_w_down", layer, DFF, 0, D))
    out.append(("pgate", "ple_w_gate", layer, D, 0, D))
    out.append(("pproj", "ple_w_proj", layer, PLE, 0, D))
    return out


def _blk_cols(K, nm=None):
    if K == PLE:
        return D
    if nm == "v":
        return 256
    if nm == "ggl":
        return 16
    kc = K // 128
    c = WBLK // kc
    return min(512, (c // 128) * 128)


def _kinds_slots(kinds):
    cnt = {0: 0, 1: 0, 2: 0}
    ks = []
    for k in kinds:
        if k is None:
            ks.append((None, 0))
        else:
            ks.append((k, cnt[k]))
            cnt[k] += 1
    return ks, cnt


DEFAULT_KINDS = tuple(i % 3 for i in range(DEPTH))
SKIP_FFN = False
ATTN_VSTAGE = 9
GLA_CUT = 99
ATTN_CUT = 99


def build_program(n_ptiles, spc, kinds=DEFAULT_KINDS, same_engine_sync=True):
    nc = bass.Bass("TRN2", target_bir_lowering=False)
    layers = len(kinds)
    LP = max(n_ptiles * TP, TP)
    SPC = max(spc, 1)
    KS, NSL = _kinds_slots(kinds)
    NPT_R = n_ptiles + 1
    dt = lambda name, shape, dtype=F32, kind="ExternalInput": nc.dram_tensor(name, list(shape), dtype, kind=kind).ap()

    x_p = dt("x_p", [LP, D])
    p_p = dt("p_p", [layers, LP, PLE])
    x_s = dt("x_s", [SPC, TS, D])
    p_s = dt("p_s", [layers, SPC, TS, PLE])
    s_ffn = dt("s_ffn", [layers, SPC, 2, DFF])
    ident_in = dt("ident", [128, 128])
    vecs = {}
    for nm, shp in (("norm_mix", [layers, D]), ("norm_ffn", [layers, D]), ("ple_norm", [layers, D]),
                    ("ffn_conv_w", [layers, 3, DFF]), ("ffn_conv_b", [layers, DFF])):
        vecs[nm] = dt(nm, shp)
    wts = {}
    for nm, shp in (("ffn_w_up", [layers, D, 2 * DFF]), ("ffn_w_down", [layers, DFF, D]),
                    ("ple_w_proj", [layers, PLE, D]), ("ple_w_gate", [layers, D, D])):
        wts[nm] = dt(nm, shp)
    NA = NSL[0]
    if NA:
        wts["a_w_qkv"] = dt("a_w_qkv", [NA, D, 2560]); wts["a_w_o"] = dt("a_w_o", [NA, D, D])
        for nm, shp in (("a_q_norm", [NA, 64]), ("a_k_norm", [NA, 64]), ("a_sinks", [NA, 32])):
            vecs[nm] = dt(nm, shp)
        ck_in = dt("ck", [NA, SPC, 128, 256]); cv_in = dt("cv", [NA, SPC, 128, 256])
        rope_in = dt("rope", [NPT_R, 2, 128, TP])
        psign_in = dt("psign", [128, 128])
        o_k_p = dt("o_k_p", [NA, 128, 256], kind="ExternalOutput"); o_v_p = dt("o_v_p", [NA, 128, 256], kind="ExternalOutput")
        o_k_s = dt("o_k_s", [NA, SPC, 128, 256], kind="ExternalOutput"); o_v_s = dt("o_v_s", [NA, SPC, 128, 256], kind="ExternalOutput")
    NG = NSL[2]
    if NG:
        wts["c_w_in"] = dt("c_w_in", [NG, D, 6160]); wts["c_w_o"] = dt("c_w_o", [NG, D, D])
        for nm, shp in (("c_w_gate_up", [NG, 16, 1024]), ("c_gate_bias", [NG, 1024]), ("c_out_norm", [NG, 512])):
            vecs[nm] = dt(nm, shp)
        s_gla = dt("s_gla", [NG, SPC, 4, 256, 512])
        tri_in = dt("tri4", [128, 256])
        o_gla_p = dt("o_gla_p", [NG, 4, 256, 512], kind="ExternalOutput")
        o_gla_s = dt("o_gla_s", [NG, SPC, 4, 256, 512], kind="ExternalOutput")
        gla_carry = [dt(f"gla_carry{g_}", [128, 8, 512], kind="Internal") for g_ in range(NG)]
    NB = NSL[1]
    if NB:
        wts["b_w_pw1"] = dt("b_w_pw1", [NB, D, 2 * D]); wts["b_w_pw2"] = dt("b_w_pw2", [NB, D, D])
        for nm, shp in (("b_w_dw", [NB, CW, D]), ("b_dw_bias", [NB, D]), ("b_ln_g", [NB, D]), ("b_ln_b", [NB, D])):
            vecs[nm] = dt(nm, shp)
        s_conv = dt("s_conv", [NB, SPC, CW - 1, D])
        o_conv_p = dt("o_conv_p", [NB, CW - 1, D], kind="ExternalOutput")
        o_conv_s = dt("o_conv_s", [NB, SPC, CW - 1, D], kind="ExternalOutput")

    y_p = dt("y_p", [LP, D], kind="ExternalOutput")
    y_s = dt("y_s", [SPC, TS, D], kind="ExternalOutput")
    o_ffn_p = dt("o_ffn_p", [layers, 2, DFF], kind="ExternalOutput")
    o_ffn_s = dt("o_ffn_s", [layers, SPC, 2, DFF], kind="ExternalOutput")

    scratch = {}
    for l in range(layers):
        for (nm, src, slot, K, c0, ncols) in _wspec(l, *KS[l]):
            if src not in wts:
                continue
            bc = _blk_cols(K, nm)
            nblk = ncols // bc
            assert nblk * bc == ncols, (nm, ncols, bc)
            scratch[(l, nm)] = (dt(f"wb_{l}_{nm}", [nblk, 128, (K // 128) * bc], BF16, kind="Internal"),
                                 src, slot, K, c0, bc, nblk)

    with ExitStack() as st:
        E = st.enter_context
        sb = lambda name, shape, dtype=F32: E(nc.sbuf_tensor(name, list(shape), dtype))
        P = Prog(nc, st, same_engine_sync=same_engine_sync)
        PE, ACT, DVE, POOL, SP = P.pe, P.act, P.dve, P.pool, P.sp

        ident = sb("ident_sb", [128, 128])
        ident_bf = sb("ident_bf", [128, 128], BF16)
        ones_bf = sb("ones_bf", [128, 128], BF16)
        g_mix = sb("g_mix", [128, layers, DC]); g_ffn = sb("g_ffn", [128, layers, DC]); g_ple = sb("g_ple", [128, layers, DC])
        cw = sb("cw", [128, layers, 3, FC]); cb = sb("cb", [128, layers, FC])
        h = sb("h", [128, DC, TP])
        xn = sb("xn", [128, DC, TP], BF16)
        act = sb("act", [128, FC, TP], BF16)
        sq = act
        rstd = sb("rstd", [128, TP])
        gext = [sb(f"gext{i}", [128, TP + 2]) for i in range(2)]
        acc = [sb(f"acc{i}", [128, TP]) for i in range(2)]
        sil = [sb(f"sil{i}", [128, TP]) for i in range(2)]
        fhist = sb("fhist", [128, layers, 2, FC])
        stage_i = sb("stage_i", [128, 512])
        stage_o = sb("stage_o", [128, 512])
        pT = sb("pT", [128, 2, TP], BF16)
        gate = sb("gate", [128, TP])
        tr_i = sb("tr_i", [128, 128]); tr_o = sb("tr_o", [128, 128])
        wpp = sb("wpp", [128, 2, D], BF16)
        wring = sb("wring", [128, NSLOT, WBLK], BF16)
        act32 = act[:].rearrange("p c t -> p (c t)").bitcast(F32).rearrange("p (c t) -> p c t", t=TP)
        assert tuple(act32.shape) == (128, FC // 2, TP), act32.shape
        mean = sb("mean", [128, TP])
        if NA:
            bd = sb("bd", [128, 128], BF16)
            psign = sb("psign_sb", [128, 128])
            gqk = sb("gqk", [128, NA, 2])
            rgm = sb("rgm", [128, NA, 2, 128])
            esink = sb("esink", [128, NA, 16])
            khist = sb("khist", [128, NA, 4, 128], BF16)
            vhist = sb("vhist", [64, NA, 2, 256], BF16)
            kout = sb("kout", [128, 4, 128])
            vout = sb("vout", [64, 2, 256])
            dnm = sb("dnm", [128, 256])
        if NG:
            wgu = sb("wgu", [16, NG, 1024], BF16)
            ngb = sb("ngb", [128, NG, 8])
            onw = sb("onw", [128, NG, 4])
            elast = sb("elast", [128, 8, 8])
            cmask = sb("cmask", [128, TP], BF16)
            tri4 = sb("tri4_sb", [128, 256])
            osq = sb("osq", [128, 1024], BF16)
            glT = sb("glT", [16, TP], BF16)
        if NB:
            ones_f = sb("ones_f", [128, 128])
            chist = sb("chist", [128, NB, DC, CW - 1])
            dww = sb("dww", [128, NB, CW, DC]); dwb = sb("dwb", [128, NB, DC])
            lng = sb("lng", [128, NB, DC]); lnb = sb("lnb", [128, NB, DC])
        psA = [E(nc.psum_tensor(f"ps{i}", [128, 512], F32)) for i in range(8)]

        SQ_KEYS = [("act", c) for c in range(DC)]
        XN_ALL = [("xn", c) for c in range(DC)]
        ACT_ALL = [("act", j) for j in range(FC)]
        FH = lambda l: [("fhist", l, j) for j in range(FC)]

        P.dma(SP, "const", ident[:], ident_in, writes=["const"])
        P.op(POOL, lambda: POOL.e.memset(ones_bf[:], 1.0), writes=["ones"])
        P.op(DVE, lambda: DVE.e.tensor_copy(out=ident_bf[:], in_=ident[:]), reads=["const"], writes=["ident_bf"])
        for nm, t in (("norm_mix", g_mix), ("norm_ffn", g_ffn), ("ple_norm", g_ple)):
            P.dma(SP, "const", t[:], vecs[nm].rearrange("l (c p) -> p l c", p=128), writes=["const"],
                  allow_slow_non_contiguous=True)
        P.dma(SP, "const", cw[:], vecs["ffn_conv_w"].rearrange("l t (c p) -> p l t c", p=128), writes=["const"],
              allow_slow_non_contiguous=True)
        P.dma(SP, "const", cb[:], vecs["ffn_conv_b"].rearrange("l (c p) -> p l c", p=128), writes=["const"],
              allow_slow_non_contiguous=True)

        if NA:
            P.op(POOL, lambda: POOL.e.memset(bd[:], 0.0), writes=["bd"])
            P.op(POOL, lambda: POOL.e.memset(bd[0:64, 0:64], 1.0), writes=["bd"])
            P.op(POOL, lambda: POOL.e.memset(bd[64:128, 64:128], 1.0), writes=["bd"])
            P.dma(SP, "const", psign[:], psign_in, writes=["const"])
            for a in range(NA):
                for j, nm in enumerate(("a_q_norm", "a_k_norm")):
                    for hh in range(2):
                        P.dma(SP, "const", gqk[64 * hh:64 * hh + 64, a, j:j + 1], vecs[nm][a].rearrange("(d o) -> d o", o=1),
                              writes=["const"], allow_slow_non_contiguous=True)
                for hh in range(2):
                    P.dma(SP, "const", esink[64 * hh:64 * hh + 64, a, :],
                          vecs["a_sinks"][a].rearrange("(kp h) -> h kp", h=2)[hh].partition_broadcast(64), writes=["const"],
                          allow_slow_non_contiguous=True)
            for a in range(NA):
                for j in range(2):
                    P.op(DVE, lambda a=a, j=j: DVE.e.tensor_scalar(out=rgm[:, a, j, :], in0=psign[:], scalar1=gqk[:, a, j:j + 1],
                                                                   scalar2=None, op0=ALU.mult), reads=["const"], writes=["rgm"])
                P.op(ACT, lambda a=a: ACT.e.activation(out=esink[:, a, :], in_=esink[:, a, :], func=AF.Exp),
                     reads=["const"], writes=["esink"])
        if NG:
            P.dma(SP, "const", tri4[:], tri_in, writes=["const"])
            for g_ in range(NG):
                P.dma(POOL, "ld_wgu", wgu[:, g_, :], vecs["c_w_gate_up"][g_], writes=["wgu"])
                P.dma(SP, "const", ngb[:, g_, :], vecs["c_gate_bias"][g_].rearrange("(c p) -> p c", p=128), writes=["const"],
                      allow_slow_non_contiguous=True)
                P.dma(SP, "const", onw[:, g_, :], vecs["c_out_norm"][g_].rearrange("(c p) -> p c", p=128), writes=["const"],
                      allow_slow_non_contiguous=True)
            P.op(DVE, lambda: DVE.e.tensor_scalar(out=ngb[:], in0=ngb[:], scalar1=-1.0, scalar2=None, op0=ALU.mult),
                 reads=["const"], writes=["ngb"])
            P.op(POOL, lambda: POOL.e.memset(cmask[:], 1.0), writes=["cmask"])
            P.op(POOL, lambda: POOL.e.memset(cmask[:].rearrange("p (c t) -> p c t", t=64)[:, :, 0:1], 0.0), writes=["cmask"])
        if NB:
            P.op(POOL, lambda: POOL.e.memset(ones_f[:], 1.0), writes=["ones"])
            P.dma(SP, "const", dww[:], vecs["b_w_dw"].rearrange("n t (c p) -> p n t c", p=128), writes=["const"],
                  allow_slow_non_contiguous=True)
            for nm, t in (("b_dw_bias", dwb), ("b_ln_g", lng), ("b_ln_b", lnb)):
                P.dma(SP, "const", t[:], vecs[nm].rearrange("n (c p) -> p n c", p=128), writes=["const"],
                      allow_slow_non_contiguous=True)

        for (l, nm), (wb, src, slot, K, c0, bc, nblk) in scratch.items():
            kc = K // 128
            if SKIP_FFN and nm in ("up", "down", "pgate", "pproj"):
                continue
            if nm == "kd":
                dstv = wb[0].rearrange("p (kc c) -> p kc c", kc=kc)
                for kv in range(4):
                    for r_ in range(2):
                        src_ap = wts[src][slot, :, 2048 + kv * 64: 2048 + (kv + 1) * 64].rearrange("(kc p) c -> p kc c", p=128)
                        P.dma(POOL, "wcast", dstv[:, :, kv * 128 + r_ * 64: kv * 128 + (r_ + 1) * 64], src_ap,
                              writes=["wb", ("wbk", l, nm, 0)])
                continue
            if nm in PAIRED:
                half = (nblk * bc) // 2
                for b in range(nblk):
                    dstv = wb[b].rearrange("p (kc c) -> p kc c", kc=kc)
                    for part in range(2):
                        src_ap = wts[src][slot, :, part * half + b * 256: part * half + (b + 1) * 256].rearrange(
                            "(kc p) c -> p kc c", p=128)
                        P.dma(POOL, "wcast", dstv[:, :, part * 256:(part + 1) * 256], src_ap, writes=["wb", ("wbk", l, nm, b)])
                continue
            for b in range(nblk):
                src_ap = wts[src][slot, :, c0 + b * bc: c0 + (b + 1) * bc].rearrange("(kc p) c -> p kc c", p=128)
                dst_ap = wb[b].rearrange("p (kc c) -> p kc c", kc=kc)
                P.dma(POOL, "wcast", dst_ap, src_ap, writes=["wb", ("wbk", l, nm, b)])

        ring = {"n": 0}

        def wload(l, nm, b):
            wb, src, slot_, K, c0, bc, nblk = scratch[(l, nm)]
            kc = K // 128
            s = ring["n"] % NSLOT
            ring["n"] += 1
            P.dma(SP, f"w{s}", wring[:, s, 0:kc * bc], wb[b], reads=[("wbk", l, nm, b)], writes=[("ring", s)])
            return s, wring[:, s, 0:kc * bc].rearrange("p (kc c) -> p kc c", kc=kc)

        psn = {"n": 0}

        def nextps():
            i = psn["n"] % 8
            psn["n"] += 1
            return i

        def rmsnorm(gt, l, T):
            P.op(ACT, lambda: ACT.e.activation(out=sq[:, 0:DC, 0:T], in_=h[:, :, 0:T], func=AF.Square),
                 reads=["h"], writes=SQ_KEYS)
            pi = nextps()
            for c in range(DC):
                P.op(PE, lambda c=c, pi=pi: PE.e.matmul(psA[pi][:, 0:T], lhsT=ones_bf[:], rhs=sq[:, c, 0:T],
                                                        start=(c == 0), stop=(c == DC - 1)),
                     reads=SQ_KEYS + ["ones"], writes=[("ps", pi)], signal=(c == DC - 1))
            P.op(DVE, lambda pi=pi: DVE.e.tensor_scalar(out=rstd[:, 0:T], in0=psA[pi][:, 0:T], scalar1=1.0 / D,
                                                        scalar2=EPS, op0=ALU.mult, op1=ALU.add),
                 reads=[("ps", pi)], writes=["rstd"])
            P.op(ACT, lambda: ACT.e.activation(out=rstd[:, 0:T], in_=rstd[:, 0:T], func=AF.Sqrt),
                 reads=["rstd"], writes=["rstd"])
            P.op(DVE, lambda: DVE.e.reciprocal(out=rstd[:, 0:T], in_=rstd[:, 0:T]), reads=["rstd"], writes=["rstd"])
            for c in range(DC):
                P.op(DVE, lambda c=c: DVE.e.scalar_tensor_tensor(out=xn[:, c, 0:T], in0=h[:, c, 0:T],
                                                                 scalar=gt[:, l, c:c + 1], in1=rstd[:, 0:T],
                                                                 op0=ALU.mult, op1=ALU.mult),
                     reads=["h", "rstd", "const"], writes=[("xn", c)])

        def conv_ffn(l, T, zero_hist):
            rmsnorm(g_ffn, l, T)
            wb, src, slot_, K, c0, bc, nblk = scratch[(l, "up")]
            cur_up = None
            for j in range(FC):
                if j % 2 == 0:
                    cur_up = wload(l, "up", j // 2)
                ops = [(cur_up, j % 2), (cur_up, 2 + j % 2)]
                pg, pu = nextps(), nextps()
                for ((s, wv), off), pi in zip(ops, (pg, pu)):
                    for k in range(DC):
                        P.op(PE, lambda k=k, wv=wv, off=off, pi=pi: PE.e.matmul(
                            psA[pi][:, 0:T], lhsT=wv[:, k, off * 128:(off + 1) * 128], rhs=xn[:, k, 0:T],
                            start=(k == 0), stop=(k == DC - 1)),
                            reads=[("ring", s)] + XN_ALL, writes=[("ps", pi)], signal=(k == DC - 1))
                i = j % 2
                ge, ac, si = gext[i], acc[i], sil[i]
                if zero_hist:
                    P.op(DVE, lambda ge=ge: DVE.e.memset(ge[:, 0:2], 0.0), writes=[("gext", i)])
                else:
                    P.op(ACT, lambda ge=ge, j=j: ACT.e.activation(out=ge[:, 0:2], in_=fhist[:, l, :, j], func=AF.Copy),
                         reads=[("fhist", l, j)], writes=[("gext", i)])
                P.op(ACT, lambda ge=ge, pg=pg: ACT.e.activation(out=ge[:, 2:T + 2], in_=psA[pg][:, 0:T], func=AF.Copy),
                     reads=[("ps", pg)], writes=[("gext", i)])
                P.op(ACT, lambda ge=ge, j=j: ACT.e.activation(out=fhist[:, l, :, j], in_=ge[:, T:T + 2], func=AF.Copy),
                     reads=[("gext", i)], writes=[("fhist", l, j)])
                P.op(DVE, lambda ge=ge, ac=ac, j=j: DVE.e.tensor_scalar(
                    out=ac[:, 0:T], in0=ge[:, 2:T + 2], scalar1=cw[:, l, 2, j:j + 1], scalar2=cb[:, l, j:j + 1],
                    op0=ALU.mult, op1=ALU.add), reads=[("gext", i), "const"], writes=[("acc", i)])
                for tap in (1, 0):
                    P.op(DVE, lambda ge=ge, ac=ac, j=j, tap=tap: DVE.e.scalar_tensor_tensor(
                        out=ac[:, 0:T], in0=ge[:, tap:T + tap], scalar=cw[:, l, tap, j:j + 1], in1=ac[:, 0:T],
                        op0=ALU.mult, op1=ALU.add), reads=[("gext", i), ("acc", i), "const"], writes=[("acc", i)])
                P.op(ACT, lambda ac=ac, si=si: ACT.e.activation(out=si[:, 0:T], in_=ac[:, 0:T], func=AF.Silu),
                     reads=[("acc", i)], writes=[("sil", i)])
                P.op(DVE, lambda si=si, pu=pu, j=j: DVE.e.tensor_tensor(out=act[:, j, 0:T], in0=psA[pu][:, 0:T],
                                                                       in1=si[:, 0:T], op=ALU.mult),
                     reads=[("sil", i), ("ps", pu)], writes=[("act", j)])
            wb, src, slot_, K, c0, bc, nblk = scratch[(l, "down")]
            cpb = bc // 128
            for b in range(nblk):
                s, wv = wload(l, "down", b)
                for off in range(cpb):
                    m = b * cpb + off
                    pi = nextps()
                    for k in range(FC):
                        P.op(PE, lambda k=k, wv=wv, off=off, pi=pi: PE.e.matmul(
                            psA[pi][:, 0:T], lhsT=wv[:, k, off * 128:(off + 1) * 128], rhs=act[:, k, 0:T],
                            start=(k == 0), stop=(k == FC - 1)),
                            reads=[("ring", s)] + ACT_ALL, writes=[("ps", pi)], signal=(k == FC - 1))
                    P.op(DVE, lambda m=m, pi=pi: DVE.e.tensor_tensor(out=h[:, m, 0:T], in0=psA[pi][:, 0:T],
                                                                     in1=h[:, m, 0:T], op=ALU.add),
                         reads=[("ps", pi), "h"], writes=["h"])

        def conformer(l, slot, T, zero_hist):
            rmsnorm(g_mix, l, T)
            W = CW - 1
            c32 = act32[:, 0:DC, :]
            uext = [act32[:, 16 + 2 * i: 18 + 2 * i, :].rearrange("p a t -> p (a t)") for i in range(2)]
            C32 = [("act", j) for j in range(0, 32)]
            UX = [[("act", j) for j in range(32 + 4 * i, 36 + 4 * i)] for i in range(2)]
            cur_pw = None
            for m in range(DC):
                if m % 2 == 0:
                    cur_pw = wload(l, "pw1", m // 2)
                ops = [(cur_pw, m % 2), (cur_pw, 2 + m % 2)]
                pa, pg = nextps(), nextps()
                for ((s_, wv), off), pi in zip(ops, (pa, pg)):
                    for k in range(DC):
                        P.op(PE, lambda k=k, wv=wv, off=off, pi=pi: PE.e.matmul(
                            psA[pi][:, 0:T], lhsT=wv[:, k, off * 128:(off + 1) * 128], rhs=xn[:, k, 0:T],
                            start=(k == 0), stop=(k == DC - 1)),
                            reads=[("ring", s_)] + XN_ALL, writes=[("ps", pi)], signal=(k == DC - 1))
                i = m % 2
                ux, si = uext[i], sil[i]
                if zero_hist:
                    P.op(DVE, lambda ux=ux: DVE.e.memset(ux[:, 0:W], 0.0), writes=UX[i])
                else:
                    P.op(ACT, lambda ux=ux, m=m: ACT.e.activation(out=ux[:, 0:W], in_=chist[:, slot, m, :], func=AF.Copy),
                         reads=[("chist", m)], writes=UX[i])
                P.op(ACT, lambda si=si, pg=pg: ACT.e.activation(out=si[:, 0:T], in_=psA[pg][:, 0:T], func=AF.Sigmoid),
                     reads=[("ps", pg)], writes=[("sil", i)])
                P.op(DVE, lambda ux=ux, si=si, pa=pa: DVE.e.tensor_tensor(out=ux[:, W:W + T], in0=psA[pa][:, 0:T],
                                                                        in1=si[:, 0:T], op=ALU.mult),
                     reads=[("ps", pa), ("sil", i)], writes=UX[i])
                P.op(ACT, lambda ux=ux, m=m: ACT.e.activation(out=chist[:, slot, m, :], in_=ux[:, T:T + W], func=AF.Copy),
                     reads=UX[i], writes=[("chist", m)])
                P.op(DVE, lambda ux=ux, m=m: DVE.e.tensor_scalar(
                    out=c32[:, m, 0:T], in0=ux[:, 0:T], scalar1=dww[:, slot, 0, m:m + 1], scalar2=dwb[:, slot, m:m + 1],
                    op0=ALU.mult, op1=ALU.add), reads=UX[i] + ["const"], writes=[("act", 2 * m), ("act", 2 * m + 1)])
                for j in range(1, CW):
                    P.op(DVE, lambda ux=ux, m=m, j=j: DVE.e.scalar_tensor_tensor(
                        out=c32[:, m, 0:T], in0=ux[:, j:j + T], scalar=dww[:, slot, j, m:m + 1], in1=c32[:, m, 0:T],
                        op0=ALU.mult, op1=ALU.add), reads=UX[i] + ["const", ("act", 2 * m), ("act", 2 * m + 1)],
                        writes=[("act", 2 * m), ("act", 2 * m + 1)])
            P.op(ACT, lambda: ACT.e.activation(out=xn[:, :, 0:T], in_=c32[:, :, 0:T], func=AF.Square),
                 reads=C32, writes=XN_ALL)
            p1, p2 = nextps(), nextps()
            for c in range(DC):
                P.op(PE, lambda c=c, p1=p1: PE.e.matmul(psA[p1][:, 0:T], lhsT=ones_f[:], rhs=c32[:, c, 0:T],
                                                        start=(c == 0), stop=(c == DC - 1)),
                     reads=C32 + ["ones"], writes=[("ps", p1)], signal=(c == DC - 1))
            for c in range(DC):
                P.op(PE, lambda c=c, p2=p2: PE.e.matmul(psA[p2][:, 0:T], lhsT=ones_bf[:], rhs=xn[:, c, 0:T],
                                                        start=(c == 0), stop=(c == DC - 1)),
                     reads=XN_ALL + ["ones"], writes=[("ps", p2)], signal=(c == DC - 1))
            P.op(DVE, lambda p1=p1: DVE.e.tensor_scalar(out=mean[:, 0:T], in0=psA[p1][:, 0:T], scalar1=1.0 / D, scalar2=None,
                                                        op0=ALU.mult), reads=[("ps", p1)], writes=["mean"])
            P.op(DVE, lambda: DVE.e.tensor_tensor(out=gate[:, 0:T], in0=mean[:, 0:T], in1=mean[:, 0:T], op=ALU.mult),
                 reads=["mean"], writes=["gate"])
            P.op(DVE, lambda p2=p2: DVE.e.scalar_tensor_tensor(out=rstd[:, 0:T], in0=psA[p2][:, 0:T], scalar=1.0 / D,
                                                               in1=gate[:, 0:T], op0=ALU.mult, op1=ALU.subtract),
                 reads=[("ps", p2), "gate"], writes=["rstd"])
            P.op(DVE, lambda: DVE.e.tensor_scalar(out=rstd[:, 0:T], in0=rstd[:, 0:T], scalar1=EPS, scalar2=None, op0=ALU.add),
                 reads=["rstd"], writes=["rstd"])
            P.op(ACT, lambda: ACT.e.activation(out=rstd[:, 0:T], in_=rstd[:, 0:T], func=AF.Sqrt), reads=["rstd"], writes=["rstd"])
            P.op(DVE, lambda: DVE.e.reciprocal(out=rstd[:, 0:T], in_=rstd[:, 0:T]), reads=["rstd"], writes=["rstd"])
            for m in range(DC):
                i = m % 2
                ac = acc[i]
                P.op(DVE, lambda m=m, ac=ac: DVE.e.tensor_tensor(out=ac[:, 0:T], in0=c32[:, m, 0:T], in1=mean[:, 0:T],
                                                                 op=ALU.subtract),
                     reads=[("act", 2 * m), ("act", 2 * m + 1), "mean"], writes=[("acc", i)])
                P.op(DVE, lambda ac=ac: DVE.e.tensor_tensor(out=ac[:, 0:T], in0=ac[:, 0:T], in1=rstd[:, 0:T], op=ALU.mult),
                     reads=[("acc", i), "rstd"], writes=[("acc", i)])
                P.op(ACT, lambda m=m, ac=ac: ACT.e.activation(out=xn[:, m, 0:T], in_=ac[:, 0:T], func=AF.Silu,
                                                              scale=lng[:, slot, m:m + 1], bias=lnb[:, slot, m:m + 1]),
                     reads=[("acc", i), "const"], writes=[("xn", m)])
            cpb2 = scratch[(l, "pw2")][5] // 128
            for b in range(scratch[(l, "pw2")][6]):
                s_, wv = wload(l, "pw2", b)
                for off in range(cpb2):
                    m = b * cpb2 + off
                    pi = nextps()
                    for k in range(DC):
                        P.op(PE, lambda k=k, wv=wv, off=off, pi=pi: PE.e.matmul(
                            psA[pi][:, 0:T], lhsT=wv[:, k, off * 128:(off + 1) * 128], rhs=xn[:, k, 0:T],
                            start=(k == 0), stop=(k == DC - 1)),
                            reads=[("ring", s_)] + XN_ALL, writes=[("ps", pi)], signal=(k == DC - 1))
                    P.op(DVE, lambda m=m, pi=pi: DVE.e.tensor_tensor(out=h[:, m, 0:T], in0=psA[pi][:, 0:T],
                                                                     in1=h[:, m, 0:T], op=ALU.add),
                         reads=[("ps", pi), "h"], writes=["h"])

        CH_ALL = [("chist", m) for m in range(DC)]

        def attention(l, slot, T, CL, first_prompt, rope_idx, sample_idx, emit_out):
            rmsnorm(g_mix, l, T)
            NQC = T // CL
            actf = act[:].rearrange("p c t -> p (c t)")
            qT = act[:, 0:16, :]
            oT = act[:, 16:32, :]
            KT = actf[:, 32 * TP: 32 * TP + 4 * 640].rearrange("p (k s) -> p k s", k=4)
            VV = actf[0:64, 37 * TP: 37 * TP + 10 * 256].rearrange("p (c f) -> p c f", c=10)
            sqh = [act[:, 42 + i, :] for i in range(2)]
            ET = xn[0:64, 0:4, :]
            QT_K = [("act", j) for j in range(0, 16)]; OT_K = [("act", j) for j in range(16, 32)]
            KT_K = [("act", j) for j in range(32, 37)]; VV_K = [("act", j) for j in range(37, 42)]
            cosT, sinT = gext[0], gext[1]
            if ATTN_CUT <= -1:
                return
            P.dma(SP, "ld_gext0", cosT[:, 0:TP], rope_in[rope_idx, 0], writes=[("gext", 0)])
            P.dma(SP, "ld_gext1", sinT[:, 0:TP], rope_in[rope_idx, 1], writes=[("gext", 1)])
            if sample_idx is None:
                if not first_prompt:
                    P.op(POOL, lambda: POOL.e.tensor_copy(out=KT[:, :, 0:128], in_=khist[:, slot]), reads=["khist"], writes=KT_K)
                    P.op(POOL, lambda: POOL.e.tensor_copy(out=VV[:, 0:2, :], in_=vhist[:, slot]), reads=["vhist"], writes=VV_K)
            else:
                for r_ in range(2):
                    P.dma(SP, "ld_stage", stage_i[:, :].rearrange("p (k r d) -> p k r d", k=4, r=2)[:, :, r_, :],
                          ck_in[slot, sample_idx].rearrange("s (k d) -> s k d", k=4), writes=["stage_i"])
                pi = nextps()
                for kv in range(4):
                    P.op(PE, lambda kv=kv, pi=pi: PE.e.transpose(psA[pi][:, kv * 128:(kv + 1) * 128],
                                                                 stage_i[:, kv * 128:(kv + 1) * 128], ident[:]),
                         reads=["stage_i", "const"], writes=[("ps", pi)], signal=(kv == 3))
                P.op(ACT, lambda pi=pi: ACT.e.activation(out=KT[:, :, 0:128], in_=psA[pi][:, :].rearrange("p (k s) -> p k s", k=4),
                                                         func=AF.Copy), reads=[("ps", pi)], writes=KT_K)
                for c2 in range(2):
                    P.dma(SP, "ld_vout", vout[:, c2, :], cv_in[slot, sample_idx, c2 * 64:(c2 + 1) * 64, :], writes=["vout"])
                P.op(POOL, lambda: POOL.e.tensor_copy(out=VV[:, 0:2, :], in_=vout[:]), reads=["vout"], writes=VV_K)

            if ATTN_CUT <= 0:
                return
            sv, wvv = wload(l, "v", 0)
            nvc = (T + 63) // 64
            for c in range(nvc):
                if ATTN_VSTAGE < 1:
                    break
                rows = min(64, T - c * 64)
                pi = nextps()
                for k in range(DC):
                    P.op(PE, lambda k=k, c=c, rows=rows, pi=pi: PE.e.matmul(
                        psA[pi][0:rows, 0:256], lhsT=xn[:, k, c * 64: c * 64 + rows], rhs=wvv[:, k, :],
                        start=(k == 0), stop=(k == DC - 1)),
                        reads=[("ring", sv)] + XN_ALL, writes=[("ps", pi)], signal=(k == DC - 1))
                if ATTN_VSTAGE < 2:
                    continue
                if ATTN_VSTAGE >= 3 and emit_out and c >= nvc - 2:
                    vi = (c - (nvc - 2)) if nvc >= 2 else 0
                    P.op(ACT, lambda rows=rows, pi=pi, vi=vi: ACT.e.activation(out=vout[0:rows, vi, :], in_=psA[pi][0:rows, 0:256],
                                                                              func=AF.Copy), reads=[("ps", pi)], writes=["vout"])
                    P.op(POOL, lambda c=c, rows=rows, vi=vi: POOL.e.tensor_copy(out=VV[0:rows, 2 + c, :], in_=vout[0:rows, vi, :]),
                         reads=["vout"], writes=VV_K)
                else:
                    P.op(ACT, lambda c=c, rows=rows, pi=pi: ACT.e.activation(out=VV[0:rows, 2 + c, :], in_=psA[pi][0:rows, 0:256],
                                                                            func=AF.Copy), reads=[("ps", pi)], writes=VV_K)

            if ATTN_CUT <= 1:
                return
            def qk_post(pi, gsel, dst_bf, dst_keys, dst_f32, i):
                xs, t1, rs_ = acc[i], sil[i], (rstd if i == 0 else mean)
                RS = "rstd" if i == 0 else "mean"
                P.op(ACT, lambda: ACT.e.activation(out=xs[:, 0:T], in_=psA[pi][:, 0:T], func=AF.Copy),
                     reads=[("ps", pi)], writes=[("acc", i)])
                P.op(ACT, lambda: ACT.e.activation(out=sqh[i][:, 0:T], in_=xs[:, 0:T], func=AF.Square),
                     reads=[("acc", i)], writes=[("act", 42 + i)])
                p2, pr = nextps(), nextps()
                P.op(PE, lambda: PE.e.matmul(psA[p2][:, 0:T], lhsT=bd[:], rhs=sqh[i][:, 0:T], start=True, stop=True),
                     reads=[("act", 42 + i), "bd"], writes=[("ps", p2)])
                P.op(PE, lambda: PE.e.matmul(psA[pr][:, 0:T], lhsT=rgm[:, slot, gsel, :], rhs=xs[:, 0:T], start=True, stop=True),
                     reads=[("acc", i), "rgm"], writes=[("ps", pr)])
                P.op(DVE, lambda: DVE.e.tensor_scalar(out=rs_[:, 0:T], in0=psA[p2][:, 0:T], scalar1=1.0 / 64, scalar2=EPS,
                                                      op0=ALU.mult, op1=ALU.add), reads=[("ps", p2)], writes=[RS])
                P.op(ACT, lambda: ACT.e.activation(out=rs_[:, 0:T], in_=rs_[:, 0:T], func=AF.Sqrt), reads=[RS], writes=[RS])
                P.op(DVE, lambda: DVE.e.reciprocal(out=rs_[:, 0:T], in_=rs_[:, 0:T]), reads=[RS], writes=[RS])
                P.op(DVE, lambda: DVE.e.scalar_tensor_tensor(out=t1[:, 0:T], in0=xs[:, 0:T], scalar=gqk[:, slot, gsel:gsel + 1],
                                                             in1=cosT[:, 0:T], op0=ALU.mult, op1=ALU.mult),
                     reads=[("acc", i), "const", ("gext", 0)], writes=[("sil", i)])
                P.op(DVE, lambda: DVE.e.tensor_tensor(out=xs[:, 0:T], in0=psA[pr][:, 0:T], in1=sinT[:, 0:T], op=ALU.mult),
                     reads=[("ps", pr), ("gext", 1)], writes=[("acc", i)])
                P.op(DVE, lambda: DVE.e.tensor_tensor(out=t1[:, 0:T], in0=t1[:, 0:T], in1=xs[:, 0:T], op=ALU.add),
                     reads=[("sil", i), ("acc", i)], writes=[("sil", i)])
                P.op(DVE, lambda: DVE.e.tensor_tensor(out=t1[:, 0:T], in0=t1[:, 0:T], in1=rs_[:, 0:T], op=ALU.mult),
                     reads=[("sil", i), ("rs", i)], writes=[("sil", i)])
                P.op(ACT, lambda: ACT.e.activation(out=dst_bf, in_=t1[:, 0:T], func=AF.Copy), reads=[("sil", i)], writes=dst_keys)
                if dst_f32 is not None:
                    P.op(POOL, lambda: POOL.e.tensor_copy(out=dst_f32, in_=t1[:, T - min(T, 128):T]),
                         reads=[("sil", i)], writes=["kout"])

            sk, wvk = wload(l, "kd", 0)
            for kv in range(4):
                pi = nextps()
                for k in range(DC):
                    P.op(PE, lambda k=k, kv=kv, pi=pi: PE.e.matmul(
                        psA[pi][:, 0:T], lhsT=wvk[:, k, kv * 128:(kv + 1) * 128], rhs=xn[:, k, 0:T],
                        start=(k == 0), stop=(k == DC - 1)),
                        reads=[("ring", sk)] + XN_ALL, writes=[("ps", pi)], signal=(k == DC - 1))
                qk_post(pi, 1, KT[:, kv, 128:128 + T], KT_K, kout[:, kv, 0:min(T, 128)] if emit_out else None, kv % 2)
            if ATTN_CUT <= 2:
                return
            cpbq = scratch[(l, "q")][5] // 128
            cur = None
            for m in range(DC):
                b, off = m // cpbq, m % cpbq
                if off == 0:
                    cur = wload(l, "q", b)
                sq_, wvq = cur
                pi = nextps()
                for k in range(DC):
                    P.op(PE, lambda k=k, wvq=wvq, off=off, pi=pi: PE.e.matmul(
                        psA[pi][:, 0:T], lhsT=wvq[:, k, off * 128:(off + 1) * 128], rhs=xn[:, k, 0:T],
                        start=(k == 0), stop=(k == DC - 1)),
                        reads=[("ring", sq_)] + XN_ALL, writes=[("ps", pi)], signal=(k == DC - 1))
                qk_post(pi, 0, qT[:, m, 0:T], [("act", m)], None, m % 2)

            if ATTN_CUT <= 3:
                return
            unit = 0
            sc_rot = 0
            for c in range(NQC):
                if sample_idx is None:
                    kchunks = [(c + j, 64) for j in range(3) if not (first_prompt and c + j < 2)]
                else:
                    kchunks = [(0, 64), (1, 64), (2, T)]
                for kv in range(4):
                    pO, pD = 2 * (unit % 2), 2 * (unit % 2) + 1
                    unit += 1
                    for ki, (kc, ks) in enumerate(kchunks):
                        sl_ = sc_rot % 2
                        sc_rot += 1
                        pS = [4 + 2 * sl_, 5 + 2 * sl_]
                        for par in range(2):
                            for pr_ in range(4):
                                hd = kv * 8 + 2 * pr_ + par
                                qc = hd // 2
                                P.op(PE, lambda kc=kc, ks=ks, par=par, pr_=pr_, qc=qc, kv=kv, c=c, pS=pS: PE.e.matmul(
                                    psA[pS[par]][0:ks, pr_ * CL:(pr_ + 1) * CL],
                                    lhsT=KT[64 * par:64 * par + 64, kv, kc * 64: kc * 64 + ks],
                                    rhs=qT[64 * par:64 * par + 64, qc, c * CL:(c + 1) * CL], start=True, stop=True),
                                    reads=KT_K + [("act", qc)], writes=[("ps", pS[par])], signal=(pr_ == 3))
                            P.op(ACT, lambda ks=ks, par=par, sl_=sl_, pS=pS: ACT.e.activation(
                                out=ET[0:ks, 2 * sl_ + par, 0:4 * CL], in_=psA[pS[par]][0:ks, 0:4 * CL], func=AF.Exp, scale=0.125),
                                reads=[("ps", pS[par])], writes=[("xn", 2 * sl_ + par)])
                        for par in range(2):
                            first, last = (ki == 0), (ki == len(kchunks) - 1)
                            P.op(PE, lambda kc=kc, ks=ks, par=par, sl_=sl_, kv=kv, pO=pO, first=first, last=last: PE.e.matmul(
                                psA[pO][64 * par:64 * par + 64, 0:4 * CL], lhsT=VV[0:ks, kc, kv * 64:(kv + 1) * 64],
                                rhs=ET[0:ks, 2 * sl_ + par, 0:4 * CL], start=first, stop=last),
                                reads=VV_K + [("xn", 2 * sl_ + par)], writes=[("ps", pO)], signal=False)
                            P.op(PE, lambda ks=ks, par=par, sl_=sl_, pD=pD, first=first, last=last: PE.e.matmul(
                                psA[pD][64 * par:64 * par + 64, 0:4 * CL], lhsT=ones_bf[0:ks, 0:64],
                                rhs=ET[0:ks, 2 * sl_ + par, 0:4 * CL], start=first, stop=last),
                                reads=["ones", ("xn", 2 * sl_ + par)], writes=[("ps", pD)], signal=(last and par == 1))
                    for pr_ in range(4):
                        P.op(DVE, lambda pr_=pr_, kv=kv, pD=pD: DVE.e.tensor_scalar(
                            out=dnm[:, pr_ * CL:(pr_ + 1) * CL], in0=psA[pD][:, pr_ * CL:(pr_ + 1) * CL],
                            scalar1=esink[:, slot, kv * 4 + pr_: kv * 4 + pr_ + 1], scalar2=None, op0=ALU.add),
                            reads=[("ps", pD), "esink"], writes=["dnm"])
                    P.op(DVE, lambda: DVE.e.reciprocal(out=dnm[:, 0:4 * CL], in_=dnm[:, 0:4 * CL]), reads=["dnm"], writes=["dnm"])
                    P.op(DVE, lambda kv=kv, c=c, pO=pO: DVE.e.tensor_tensor(
                        out=oT[:, kv * 4:(kv + 1) * 4, c * CL:(c + 1) * CL],
                        in0=psA[pO][:, 0:4 * CL].rearrange("p (a q) -> p a q", a=4),
                        in1=dnm[:, 0:4 * CL].rearrange("p (a q) -> p a q", a=4), op=ALU.mult),
                        reads=[("ps", pO), "dnm"], writes=[("act", 16 + kv * 4 + a_) for a_ in range(4)])

            if ATTN_CUT <= 4:
                return
            if sample_idx is None:
                P.op(POOL, lambda: POOL.e.tensor_copy(out=khist[:, slot], in_=KT[:, :, T:T + 128]), reads=KT_K, writes=["khist"])
                P.op(POOL, lambda: POOL.e.tensor_copy(out=vhist[:, slot], in_=VV[:, nvc:nvc + 2, :]), reads=VV_K, writes=["vhist"])

            cpbo = scratch[(l, "wo")][5] // 128
            for b in range(scratch[(l, "wo")][6]):
                so, wvo = wload(l, "wo", b)
                for off in range(cpbo):
                    m = b * cpbo + off
                    pi = nextps()
                    for k in range(DC):
                        P.op(PE, lambda k=k, wvo=wvo, off=off, pi=pi: PE.e.matmul(
                            psA[pi][:, 0:T], lhsT=wvo[:, k, off * 128:(off + 1) * 128], rhs=oT[:, k, 0:T],
                            start=(k == 0), stop=(k == DC - 1)),
                            reads=[("ring", so)] + OT_K, writes=[("ps", pi)], signal=(k == DC - 1))
                    P.op(DVE, lambda m=m, pi=pi: DVE.e.tensor_tensor(out=h[:, m, 0:T], in0=psA[pi][:, 0:T],
                                                                     in1=h[:, m, 0:T], op=ALU.add),
                         reads=[("ps", pi), "h"], writes=["h"])

        def attn_cache_out(slot, T, dst_k, dst_v, src_k=None, src_v=None):
            n_new = min(T, 128)
            if n_new < 128:
                P.dma(SP, "st_kcopy", dst_k[0:128 - n_new, :], src_k[n_new:128, :])
                P.dma(SP, "st_kcopy", dst_v[0:128 - n_new, :], src_v[n_new:128, :])
            pi = nextps()
            for kv in range(4):
                P.op(PE, lambda kv=kv, pi=pi: PE.e.transpose(psA[pi][0:n_new, kv * 128:(kv + 1) * 128], kout[:, kv, 0:n_new], ident[:]),
                     reads=["kout", "const"], writes=[("ps", pi)], signal=(kv == 3))
            P.op(ACT, lambda pi=pi: ACT.e.activation(
                out=stage_o[0:n_new, 0:256].rearrange("p (k d) -> p k d", k=4),
                in_=psA[pi][0:n_new, :].rearrange("p (k x) -> p k x", k=4)[:, :, 0:64], func=AF.Copy),
                reads=[("ps", pi)], writes=["stage_o"])
            P.dma(SP, "st_stage_o", dst_k[128 - n_new:128, :], stage_o[0:n_new, 0:256], reads=["stage_o"])
            if n_new == 128:
                for c2 in range(2):
                    P.dma(SP, "st_vout", dst_v[c2 * 64:(c2 + 1) * 64, :], vout[:, c2, :], reads=["vout"])
            else:
                P.dma(SP, "st_vout", dst_v[128 - n_new:128, :], vout[0:n_new, 0, :], reads=["vout"])

        def gla(l, slot, T, CL, first_prompt, sample_idx, last_prompt):
            NCH = T // CL
            actf = act[:].rearrange("p c t -> p (c t)")
            qtT = act[:, 0:8, :]; ktT = act[:, 8:16, :]; vT = act[:, 16:32, :]
            Sb = act[:, 32:40, :]
            vtok = actf[:, 40 * TP: 44 * TP]
            Sst = xn[:].rearrange("p c t -> p (c t)").bitcast(F32).rearrange("p (c t) -> p c t", t=TP)
            ktok = gext[0][:, 0:512].bitcast(BF16)
            AmT = gext[1][:, 0:128].bitcast(BF16).rearrange("p (a q) -> p a q", a=4)
            QT_K = [("act", j) for j in range(0, 8)]; KT_K = [("act", j) for j in range(8, 16)]
            VT_K = [("act", j) for j in range(16, 32)]; SB_K = [("act", j) for j in range(32, 40)]
            VTOK_K = [("act", j) for j in range(40, 44)]

            rmsnorm(g_mix, l, T)
            sg, wgl = wload(l, "ggl", 0)
            pi = nextps()
            for k in range(DC):
                P.op(PE, lambda k=k, pi=pi: PE.e.matmul(psA[pi][0:16, 0:T], lhsT=wgl[:, k, 0:16], rhs=xn[:, k, 0:T],
                                                        start=(k == 0), stop=(k == DC - 1)),
                     reads=[("ring", sg)] + XN_ALL, writes=[("ps", pi)], signal=(k == DC - 1))
            P.op(ACT, lambda pi=pi: ACT.e.activation(out=glT[:, 0:T], in_=psA[pi][0:16, 0:T], func=AF.Copy),
                 reads=[("ps", pi)], writes=["glT"])
            blk_q = {}; blk_k = {}
            for m in range(8):
                i = m % 2
                for nm_, cache in (("gq", blk_q), ("gk", blk_k)):
                    b = m // 4
                    if b not in cache:
                        cache[b] = wload(l, nm_, b)
                (sq_, wq), (sk_, wk) = blk_q[m // 4], blk_k[m // 4]
                off = m % 4
                pq, pk, pl = nextps(), nextps(), nextps()
                for (pi_, wv_, s__) in ((pq, wq, sq_), (pk, wk, sk_)):
                    for k in range(DC):
                        P.op(PE, lambda k=k, pi_=pi_, wv_=wv_, off=off: PE.e.matmul(
                            psA[pi_][:, 0:T], lhsT=wv_[:, k, off * 128:(off + 1) * 128], rhs=xn[:, k, 0:T],
                            start=(k == 0), stop=(k == DC - 1)),
                            reads=[("ring", s__)] + XN_ALL, writes=[("ps", pi_)], signal=(k == DC - 1))
                P.op(PE, lambda m=m, pl=pl: PE.e.matmul(psA[pl][:, 0:T], lhsT=wgu[:, slot, m * 128:(m + 1) * 128], rhs=glT[:, 0:T],
                                                        start=True, stop=True), reads=["wgu", "glT"], writes=[("ps", pl)])
                lt, eb, enb = acc[i], sil[i], (rstd if i == 0 else mean)
                EB = "rstd" if i == 0 else "mean"
                P.op(ACT, lambda m=m, pl=pl, lt=lt: ACT.e.activation(out=lt[:, 0:T], in_=psA[pl][:, 0:T], func=AF.Exp, scale=-1.0,
                                                                     bias=ngb[:, slot, m:m + 1]), reads=[("ps", pl), "ngb"], writes=[("acc", i)])
                P.op(ACT, lambda lt=lt: ACT.e.activation(out=lt[:, 0:T], in_=lt[:, 0:T], func=AF.Ln, bias=1.0),
                     reads=[("acc", i)], writes=[("acc", i)])
                P.op(DVE, lambda lt=lt: DVE.e.tensor_tensor_scan(out=gate[:, 0:T], data0=cmask[:, 0:T], data1=lt[:, 0:T], initial=0.0,
                                                                 op0=ALU.mult, op1=ALU.add), reads=[("acc", i), "cmask"], writes=["gate"])
                P.op(ACT, lambda eb=eb: ACT.e.activation(out=eb[:, 0:T], in_=gate[:, 0:T], func=AF.Exp, scale=-1.0 / 16),
                     reads=["gate"], writes=[("sil", i)])
                P.op(ACT, lambda enb=enb: ACT.e.activation(out=enb[:, 0:T], in_=gate[:, 0:T], func=AF.Exp, scale=1.0 / 16),
                     reads=["gate"], writes=[EB])
                P.op(DVE, lambda m=m, pq=pq, eb=eb: DVE.e.scalar_tensor_tensor(out=qtT[:, m, 0:T], in0=psA[pq][:, 0:T], scalar=1.0 / 16,
                                                                               in1=eb[:, 0:T], op0=ALU.mult, op1=ALU.mult),
                     reads=[("ps", pq), ("sil", i)], writes=[("act", m)])
                P.op(DVE, lambda m=m, pk=pk, enb=enb: DVE.e.tensor_tensor(out=ktT[:, m, 0:T], in0=psA[pk][:, 0:T], in1=enb[:, 0:T],
                                                                          op=ALU.mult), reads=[("ps", pk), EB], writes=[("act", 8 + m)])
                P.op(POOL, lambda m=m, eb=eb: POOL.e.tensor_copy(
                    out=elast[:, m, 0:NCH], in_=eb[:, 0:T].rearrange("p (c t) -> p c t", t=CL)[:, :, CL - 1]),
                    reads=[("sil", i)], writes=["elast"])
            cur = None
            for m in range(DC):
                if m % 4 == 0:
                    cur = wload(l, "gv", m // 4)
                sv_, wv_ = cur
                off = m % 4
                pi = nextps()
                for k in range(DC):
                    P.op(PE, lambda k=k, pi=pi, wv_=wv_, off=off: PE.e.matmul(
                        psA[pi][:, 0:T], lhsT=wv_[:, k, off * 128:(off + 1) * 128], rhs=xn[:, k, 0:T],
                        start=(k == 0), stop=(k == DC - 1)),
                        reads=[("ring", sv_)] + XN_ALL, writes=[("ps", pi)], signal=(k == DC - 1))
                P.op(ACT, lambda m=m, pi=pi: ACT.e.activation(out=vT[:, m, 0:T], in_=psA[pi][:, 0:T], func=AF.Copy),
                     reads=[("ps", pi)], writes=[("act", 16 + m)])

            if GLA_CUT <= 1:
                return
            if sample_idx is not None:
                P.dma(SP, "ld_sst", Sst, s_gla[slot, sample_idx].rearrange("h (kk p) v -> p (h kk) v", p=128), writes=XN_ALL)
            elif first_prompt:
                P.op(POOL, lambda: POOL.e.memset(Sst, 0.0), writes=XN_ALL)
            else:
                P.dma(SP, "ld_sst", Sst, gla_carry[slot], reads=["gla_carry"], writes=XN_ALL)
            P.op(POOL, lambda: POOL.e.memset(vtok[:, :], 0.0), writes=VTOK_K)
            P.op(POOL, lambda: POOL.e.memset(gext[0][:, 0:512], 0.0), writes=[("gext", 0)])
            P.op(POOL, lambda: POOL.e.memset(gext[1][:, 0:128], 0.0), writes=[("gext", 1)])

            for c in range(NCH):
                cs = slice(c * CL, (c + 1) * CL)
                for g4 in range(4):
                    pb = nextps()
                    for a_ in range(4):
                        m = 4 * g4 + a_
                        P.op(PE, lambda m=m, a_=a_, pb=pb, cs=cs: PE.e.matmul(
                            psA[pb][0:CL, a_ * 128:(a_ + 1) * 128], lhsT=vT[:, m, cs], rhs=ident_bf[:], start=True, stop=True),
                            reads=[("act", 16 + m), "ident_bf"], writes=[("ps", pb)], signal=(a_ == 3))
                    P.op(ACT, lambda g4=g4, pb=pb: ACT.e.activation(out=vtok[0:CL, g4 * 512:(g4 + 1) * 512], in_=psA[pb][0:CL, :],
                                                                  func=AF.Copy), reads=[("ps", pb)], writes=VTOK_K)
                for g2 in range(2):
                    pb = nextps()
                    for a_ in range(4):
                        m = 4 * g2 + a_
                        P.op(PE, lambda m=m, a_=a_, pb=pb, cs=cs: PE.e.matmul(
                            psA[pb][0:CL, a_ * 128:(a_ + 1) * 128], lhsT=ktT[:, m, cs], rhs=ident_bf[:], start=True, stop=True),
                            reads=[("act", 8 + m), "ident_bf"], writes=[("ps", pb)], signal=(a_ == 3))
                    P.op(ACT, lambda g2=g2, pb=pb: ACT.e.activation(out=ktok[0:CL, g2 * 512:(g2 + 1) * 512], in_=psA[pb][0:CL, :],
                                                                  func=AF.Copy), reads=[("ps", pb)], writes=[("gext", 0)])
                for j in range(8):
                    P.op(POOL if j % 2 else ACT, (lambda j=j: POOL.e.tensor_copy(out=Sb[:, j, :], in_=Sst[:, j, :])) if j % 2 else
                         (lambda j=j: ACT.e.activation(out=Sb[:, j, :], in_=Sst[:, j, :], func=AF.Copy)),
                         reads=[("xn", 2 * j), ("xn", 2 * j + 1)], writes=[("act", 32 + j)])
                pa = nextps()
                for hd in range(4):
                    for kk in range(2):
                        P.op(PE, lambda hd=hd, kk=kk, pa=pa, cs=cs: PE.e.matmul(
                            psA[pa][0:CL, hd * CL:(hd + 1) * CL], lhsT=ktT[:, 2 * hd + kk, cs], rhs=qtT[:, 2 * hd + kk, cs],
                            start=(kk == 0), stop=(kk == 1)),
                            reads=QT_K + KT_K, writes=[("ps", pa)], signal=(hd == 3 and kk == 1))
                P.op(DVE, lambda pa=pa: DVE.e.tensor_tensor(
                    out=AmT[0:CL, :, 0:CL], in0=psA[pa][0:CL, 0:4 * CL].rearrange("p (a q) -> p a q", a=4),
                    in1=tri4[0:CL, :].rearrange("p (a q) -> p a q", a=4)[:, :, 0:CL], op=ALU.mult),
                    reads=[("ps", pa), "const"], writes=[("gext", 1)])
                po = [nextps(), nextps()]
                for f in range(DC):
                    hd = f // 4
                    dst = psA[po[f // 8]][:, (f % 8) * CL:(f % 8 + 1) * CL]
                    for kk in range(2):
                        P.op(PE, lambda f=f, hd=hd, kk=kk, dst=dst, cs=cs: PE.e.matmul(
                            dst, lhsT=Sb[:, 2 * hd + kk, (f % 4) * 128:(f % 4 + 1) * 128], rhs=qtT[:, 2 * hd + kk, cs],
                            start=(kk == 0), stop=False),
                            reads=SB_K + QT_K, writes=[("ps", po[f // 8])], signal=False)
                    P.op(PE, lambda f=f, hd=hd, dst=dst: PE.e.matmul(
                        dst, lhsT=vtok[:, f * 128:(f + 1) * 128], rhs=AmT[:, hd, 0:CL], start=False, stop=True),
                        reads=VTOK_K + [("gext", 1)], writes=[("ps", po[f // 8])], signal=(f % 8 == 7))
                o32 = [acc[0], acc[1]]
                for hf in range(2):
                    P.op(ACT, lambda hf=hf, po=po: ACT.e.activation(out=o32[hf][:, 0:8 * CL], in_=psA[po[hf]][:, 0:8 * CL], func=AF.Copy),
                         reads=[("ps", po[hf])], writes=[("acc", hf)])
                    P.op(ACT, lambda hf=hf: ACT.e.activation(out=osq[:, hf * 512: hf * 512 + 8 * CL], in_=o32[hf][:, 0:8 * CL],
                                                             func=AF.Square), reads=[("acc", hf)], writes=[("osq", hf)])
                for j in range(8):
                    hd = j // 2
                    pd = nextps()
                    P.op(PE, lambda j=j, hd=hd, pd=pd: PE.e.matmul(psA[pd][:, :], lhsT=ktok[:, j * 128:(j + 1) * 128],
                                                                  rhs=vtok[:, hd * 512:(hd + 1) * 512], start=True, stop=True),
                         reads=[("gext", 0)] + VTOK_K, writes=[("ps", pd)])
                    SK = [("xn", 2 * j), ("xn", 2 * j + 1)]
                    P.op(POOL, lambda j=j, c=c: POOL.e.tensor_scalar(out=Sst[:, j, :], in0=Sst[:, j, :], scalar1=elast[:, j, c:c + 1],
                                                                     scalar2=None, op0=ALU.mult), reads=SK + ["elast"], writes=SK)
                    P.op(DVE, lambda j=j, c=c, pd=pd: DVE.e.scalar_tensor_tensor(
                        out=Sst[:, j, :], in0=psA[pd][:, :], scalar=elast[:, j, c:c + 1], in1=Sst[:, j, :],
                        op0=ALU.mult, op1=ALU.add), reads=[("ps", pd), "elast"] + SK, writes=SK)
                pn = nextps()
                for hd in range(4):
                    for fi in range(4):
                        f = 4 * hd + fi
                        P.op(PE, lambda hd=hd, fi=fi, f=f, pn=pn: PE.e.matmul(
                            psA[pn][:, hd * CL:(hd + 1) * CL], lhsT=ones_bf[:],
                            rhs=osq[:, (f // 8) * 512 + (f % 8) * CL: (f // 8) * 512 + (f % 8 + 1) * CL],
                            start=(fi == 0), stop=(fi == 3)),
                            reads=[("osq", f // 8), "ones"], writes=[("ps", pn)], signal=(hd == 3 and fi == 3))
                P.op(DVE, lambda pn=pn: DVE.e.tensor_scalar(out=gate[:, 0:4 * CL], in0=psA[pn][:, 0:4 * CL], scalar1=1.0 / 512, scalar2=EPS,
                                                            op0=ALU.mult, op1=ALU.add), reads=[("ps", pn)], writes=["gate"])
                P.op(ACT, lambda: ACT.e.activation(out=gate[:, 0:4 * CL], in_=gate[:, 0:4 * CL], func=AF.Sqrt), reads=["gate"], writes=["gate"])
                P.op(DVE, lambda: DVE.e.reciprocal(out=gate[:, 0:4 * CL], in_=gate[:, 0:4 * CL]), reads=["gate"], writes=["gate"])
                for f in range(DC):
                    hd, fi = f // 4, f % 4
                    P.op(DVE, lambda f=f, hd=hd, fi=fi, cs=cs: DVE.e.scalar_tensor_tensor(
                        out=vT[:, f, cs], in0=o32[f // 8][:, (f % 8) * CL:(f % 8 + 1) * CL], scalar=onw[:, slot, fi:fi + 1],
                        in1=gate[:, hd * CL:(hd + 1) * CL], op0=ALU.mult, op1=ALU.mult),
                        reads=[("acc", f // 8), "const", "gate"], writes=[("act", 16 + f)])

            if sample_idx is not None:
                P.dma(SP, "st_sst", o_gla_s[slot, sample_idx].rearrange("h (kk p) v -> p (h kk) v", p=128), Sst, reads=XN_ALL)
            else:
                P.dma(SP, "st_sst", gla_carry[slot], Sst, reads=XN_ALL, writes=["gla_carry"])
                if last_prompt:
                    P.dma(SP, "st_sst", o_gla_p[slot].rearrange("h (kk p) v -> p (h kk) v", p=128), Sst, reads=XN_ALL)

            if GLA_CUT <= 2:
                return
            rmsnorm(g_mix, l, T)
            cur = None
            for m in range(DC):
                if m % 4 == 0:
                    cur = wload(l, "gr", m // 4)
                sr_, wr_ = cur
                off = m % 4
                i = m % 2
                pi = nextps()
                for k in range(DC):
                    P.op(PE, lambda k=k, pi=pi, wr_=wr_, off=off: PE.e.matmul(
                        psA[pi][:, 0:T], lhsT=wr_[:, k, off * 128:(off + 1) * 128], rhs=xn[:, k, 0:T],
                        start=(k == 0), stop=(k == DC - 1)),
                        reads=[("ring", sr_)] + XN_ALL, writes=[("ps", pi)], signal=(k == DC - 1))
                P.op(ACT, lambda pi=pi, i=i: ACT.e.activation(out=sil[i][:, 0:T], in_=psA[pi][:, 0:T], func=AF.Silu),
                     reads=[("ps", pi)], writes=[("sil", i)])
                P.op(DVE, lambda m=m, i=i: DVE.e.tensor_tensor(out=vT[:, m, 0:T], in0=vT[:, m, 0:T], in1=sil[i][:, 0:T], op=ALU.mult),
                     reads=[("act", 16 + m), ("sil", i)], writes=[("act", 16 + m)])
            cpbo = scratch[(l, "cwo")][5] // 128
            for b in range(scratch[(l, "cwo")][6]):
                so, wvo = wload(l, "cwo", b)
                for off in range(cpbo):
                    m = b * cpbo + off
                    pi = nextps()
                    for k in range(DC):
                        P.op(PE, lambda k=k, wvo=wvo, off=off, pi=pi: PE.e.matmul(
                            psA[pi][:, 0:T], lhsT=wvo[:, k, off * 128:(off + 1) * 128], rhs=vT[:, k, 0:T],
                            start=(k == 0), stop=(k == DC - 1)),
                            reads=[("ring", so)] + VT_K, writes=[("ps", pi)], signal=(k == DC - 1))
                    P.op(DVE, lambda m=m, pi=pi: DVE.e.tensor_tensor(out=h[:, m, 0:T], in0=psA[pi][:, 0:T],
                                                                     in1=h[:, m, 0:T], op=ALU.add),
                         reads=[("ps", pi), "h"], writes=["h"])

        def load_tokens_T(dst, n_chunks, src_rows, T, reskey):
            for tb in range((T + 127) // 128):
                rows = min(128, T - tb * 128)
                for c0 in range(0, n_chunks, 4):
                    ncol = min(4, n_chunks - c0)
                    P.dma(SP, "ld_stage", stage_i[0:rows, 0:ncol * 128],
                          src_rows[tb * 128: tb * 128 + rows, c0 * 128:(c0 + ncol) * 128], writes=["stage_i"])
                    pi = nextps()
                    for a in range(ncol):
                        P.op(PE, lambda a=a, pi=pi, rows=rows: PE.e.transpose(
                            psA[pi][:, a * 128: a * 128 + rows], stage_i[0:rows, a * 128:(a + 1) * 128],
                            ident[0:rows, 0:rows]),
                            reads=["stage_i", "const"], writes=[("ps", pi)], signal=(a == ncol - 1))
                    for a in range(ncol):
                        P.op(ACT, lambda a=a, pi=pi, rows=rows, c0=c0, tb=tb: ACT.e.activation(
                            out=dst[:, c0 + a, tb * 128: tb * 128 + rows], in_=psA[pi][:, a * 128: a * 128 + rows],
                            func=AF.Copy), reads=[("ps", pi)], writes=(reskey if isinstance(reskey, list) else [reskey]))

        def store_tokens_T(dst_rows, src, n_chunks, T, reskey, semname):
            for tb in range((T + 127) // 128):
                rows = min(128, T - tb * 128)
                for c0 in range(0, n_chunks, 4):
                    ncol = min(4, n_chunks - c0)
                    pi = nextps()
                    for a in range(ncol):
                        P.op(PE, lambda a=a, pi=pi, rows=rows, c0=c0, tb=tb: PE.e.transpose(
                            psA[pi][0:rows, a * 128:(a + 1) * 128], src[:, c0 + a, tb * 128: tb * 128 + rows], ident[:]),
                            reads=(reskey if isinstance(reskey, list) else [reskey]) + ["const"], writes=[("ps", pi)],
                            signal=(a == ncol - 1))
                    P.op(ACT, lambda pi=pi, rows=rows, ncol=ncol: ACT.e.activation(
                        out=stage_o[0:rows, 0:ncol * 128], in_=psA[pi][0:rows, 0:ncol * 128], func=AF.Copy),
                        reads=[("ps", pi)], writes=["stage_o"])
                    P.dma(SP, semname, dst_rows[tb * 128: tb * 128 + rows, c0 * 128:(c0 + ncol) * 128],
                          stage_o[0:rows, 0:ncol * 128], reads=["stage_o"])

        def ple(l, T, p_rows):
            rmsnorm(g_ple, l, T)
            load_tokens_T(pT, 2, p_rows, T, "pT")
            wbp = scratch[(l, "pproj")]
            assert wbp[6] == 1
            P.dma(SP, "ld_wpp", wpp[:].rearrange("p k c -> p (k c)"), wbp[0][0], reads=[("wbk", l, "pproj", 0)], writes=["wpp"])
            cpb = scratch[(l, "pgate")][5] // 128
            cur = None
            for m in range(DC):
                b, off = m // cpb, m % cpb
                if off == 0:
                    cur = wload(l, "pgate", b)
                s, wv = cur
                pgi, ppi = nextps(), nextps()
                for k in range(DC):
                    P.op(PE, lambda k=k, wv=wv, off=off, pgi=pgi: PE.e.matmul(
                        psA[pgi][:, 0:T], lhsT=wv[:, k, off * 128:(off + 1) * 128], rhs=xn[:, k, 0:T],
                        start=(k == 0), stop=(k == DC - 1)),
                        reads=[("ring", s)] + XN_ALL, writes=[("ps", pgi)], signal=(k == DC - 1))
                for k in range(2):
                    P.op(PE, lambda k=k, m=m, ppi=ppi: PE.e.matmul(
                        psA[ppi][:, 0:T], lhsT=wpp[:, k, m * 128:(m + 1) * 128], rhs=pT[:, k, 0:T],
                        start=(k == 0), stop=(k == 1)),
                        reads=["wpp", "pT"], writes=[("ps", ppi)], signal=(k == 1))
                P.op(ACT, lambda pgi=pgi: ACT.e.activation(out=gate[:, 0:T], in_=psA[pgi][:, 0:T], func=AF.Sigmoid),
                     reads=[("ps", pgi)], writes=["gate"])
                P.op(DVE, lambda ppi=ppi: DVE.e.tensor_tensor(out=gate[:, 0:T], in0=psA[ppi][:, 0:T], in1=gate[:, 0:T],
                                                              op=ALU.mult), reads=[("ps", ppi), "gate"], writes=["gate"])
                P.op(DVE, lambda m=m: DVE.e.tensor_tensor(out=h[:, m, 0:T], in0=h[:, m, 0:T], in1=gate[:, 0:T],
                                                          op=ALU.add), reads=["gate", "h"], writes=["h"])

        def ffn_state_out(l, dst):
            pi = nextps()
            P.op(PE, lambda pi=pi: PE.e.transpose(psA[pi][0:2 * FC, 0:128], fhist[:, l].rearrange("p r c -> p (r c)"),
                                                  ident[:]), reads=FH(l) + ["const"], writes=[("ps", pi)])
            P.op(ACT, lambda pi=pi: ACT.e.activation(out=tr_o[0:2 * FC, :], in_=psA[pi][0:2 * FC, 0:128], func=AF.Copy),
                 reads=[("ps", pi)], writes=["tr_o"])
            for r in range(2):
                P.dma(SP, "st_tr_o", dst[r].rearrange("(c p) -> c p", p=128), tr_o[r * FC:(r + 1) * FC, :], reads=["tr_o"])

        def ffn_state_in(l, src):
            for r in range(2):
                P.dma(SP, "ld_tr_i", tr_i[r * FC:(r + 1) * FC, :], src[r].rearrange("(c p) -> c p", p=128), writes=["tr_i"])
            pi = nextps()
            P.op(PE, lambda pi=pi: PE.e.transpose(psA[pi][:, 0:2 * FC], tr_i[0:2 * FC, :], ident[0:2 * FC, 0:2 * FC]),
                 reads=["tr_i", "const"], writes=[("ps", pi)])
            P.op(ACT, lambda pi=pi: ACT.e.activation(out=fhist[:, l].rearrange("p r c -> p (r c)"),
                                                     in_=psA[pi][:, 0:2 * FC], func=AF.Copy),
                 reads=[("ps", pi)], writes=FH(l))

        def run_tile(T, x_rows, y_rows, p_rows_of_layer, first_prompt, sample_idx, last_prompt, tile_idx=0):
            load_tokens_T(h, DC, x_rows, T, "h")
            for l in range(layers):
                if sample_idx is not None and not SKIP_FFN:
                    ffn_state_in(l, s_ffn[l, sample_idx])
                kind, slot = KS[l]
                if kind == 0:
                    emit = (sample_idx is not None) or last_prompt
                    attention(l, slot, T, (TS if sample_idx is not None else 64), first_prompt,
                              (NPT_R - 1 if sample_idx is not None else tile_idx), sample_idx, emit)
                    if ATTN_CUT <= 5:
                        pass
                    elif sample_idx is not None:
                        attn_cache_out(slot, T, o_k_s[slot, sample_idx], o_v_s[slot, sample_idx],
                                       ck_in[slot, sample_idx], cv_in[slot, sample_idx])
                    elif last_prompt:
                        attn_cache_out(slot, T, o_k_p[slot], o_v_p[slot])
                if kind == 2:
                    gla(l, slot, T, (TS if sample_idx is not None else 64), first_prompt, sample_idx, last_prompt)
                if kind == 1:
                    if sample_idx is not None:
                        load_tokens_T(chist[:, slot], DC, s_conv[slot, sample_idx], CW - 1, CH_ALL)
                    conformer(l, slot, T, first_prompt)
                    if sample_idx is not None:
                        store_tokens_T(o_conv_s[slot, sample_idx], chist[:, slot], DC, CW - 1, CH_ALL, "st_stage_o")
                    elif last_prompt:
                        store_tokens_T(o_conv_p[slot], chist[:, slot], DC, CW - 1, CH_ALL, "st_stage_o")
                if SKIP_FFN:
                    continue
                conv_ffn(l, T, first_prompt)
                ple(l, T, p_rows_of_layer(l))
                if sample_idx is not None:
                    ffn_state_out(l, o_ffn_s[l, sample_idx])
                elif last_prompt:
                    ffn_state_out(l, o_ffn_p[l])
            store_tokens_T(y_rows, h, DC, T, "h", "st_stage_o")

        for t in range(n_ptiles):
            run_tile(TP, x_p[t * TP:(t + 1) * TP, :], y_p[t * TP:(t + 1) * TP, :],
                     lambda l, t=t: p_p[l, t * TP:(t + 1) * TP, :], t == 0, None, t == n_ptiles - 1, t)
        for s_ in range(spc):
            run_tile(TS, x_s[s_], y_s[s_], lambda l, s_=s_: p_s[l, s_], False, s_, False)

        P.drain_all(SP)

        block = E(nc.Block())

        @block.tensor
        def _(e): PE.replay(e)

        @block.scalar
        def _(e): ACT.replay(e)

        @block.vector
        def _(e): DVE.replay(e)

        @block.gpsimd
        def _(e): POOL.replay(e)

        @block.sync
        def _(e): SP.replay(e)

        nc._n_instr_est = P.n_instr
    return nc


SHARED_KEYS = ("norm_mix", "norm_ffn", "ple_norm", "ffn_conv_w", "ffn_conv_b",
               "ffn_w_up", "ffn_w_down", "ple_w_proj", "ple_w_gate")
B_KEYS = ("b_w_pw1", "b_w_pw2", "b_w_dw", "b_dw_bias", "b_ln_g", "b_ln_b")
A_KEYS = ("a_w_qkv", "a_w_o", "a_q_norm", "a_k_norm", "a_sinks")
C_KEYS = ("c_w_in", "c_w_o", "c_w_gate_up", "c_gate_bias", "c_out_norm")
ROPE_THETA = 10000.0
PAST_LEN = 2048


def _rope_tables(n_ptiles):
    half = 32
    inv = (1.0 / (ROPE_THETA ** (np.arange(half, dtype=np.float32) / half))).astype(np.float32)
    out = np.zeros((n_ptiles + 1, 2, 128, TP), np.float32)
    rows = np.arange(128) % half
    for t in range(n_ptiles + 1):
        pos = (np.arange(TP) + t * TP) if t < n_ptiles else (PAST_LEN + np.arange(TP))
        ang = (pos.astype(np.float32)[None, :] * inv[rows][:, None]).astype(np.float32)
        out[t, 0] = np.cos(ang); out[t, 1] = np.sin(ang)
    return out


def _psign():
    m = np.zeros((128, 128), np.float32)
    for c in range(128):
        if c % 64 < 32:
            m[c + 32, c] = -1.0
        else:
            m[c - 32, c] = 1.0
    return m


def make_in_maps(inp, n_cores, n_ptiles, spc, kinds=DEFAULT_KINDS):
    layers = len(kinds)
    _, nsl = _kinds_slots(kinds)
    ident = np.eye(128, dtype=np.float32)
    LP = max(n_ptiles * TP, TP)
    shared = {k: np.ascontiguousarray(inp[k][:layers]) for k in SHARED_KEYS}
    if nsl[0]:
        for k in A_KEYS:
            shared[k] = np.ascontiguousarray(inp[k][:nsl[0]])
        shared["rope"] = _rope_tables(n_ptiles)
        shared["psign"] = _psign()
    if nsl[1]:
        for k in B_KEYS:
            shared[k] = np.ascontiguousarray(inp[k][:nsl[1]])
    if nsl[2]:
        for k in C_KEYS:
            shared[k] = np.ascontiguousarray(inp[k][:nsl[2]])
        tri = np.zeros((128, 4, 64), np.float32)
        jj, ii = np.meshgrid(np.arange(64), np.arange(64), indexing="ij")
        tri[:64] = (jj <= ii).astype(np.float32)[:, None, :]
        shared["tri4"] = tri.reshape(128, 256)
    maps = []
    for c in range(n_cores):
        b = c % 2
        sl = slice(c * spc, c * spc + max(spc, 1))
        m = dict(shared)
        m["ident"] = ident
        m["x_p"] = np.ascontiguousarray(inp["x_prompt"][b, :LP])
        m["p_p"] = np.ascontiguousarray(inp["p_prompt"][:layers, b, :LP])
        m["x_s"] = np.ascontiguousarray(inp["x_sample"][sl])
        m["p_s"] = np.ascontiguousarray(inp["p_sample"][:layers, sl])
        m["s_ffn"] = np.ascontiguousarray(inp["state_ffn_conv"][:layers, sl])
        if nsl[0]:
            m["ck"] = np.ascontiguousarray(inp["cache_k_a"][:nsl[0], sl]).reshape(nsl[0], -1, 128, 256)
            m["cv"] = np.ascontiguousarray(inp["cache_v_a"][:nsl[0], sl]).reshape(nsl[0], -1, 128, 256)
        if nsl[1]:
            m["s_conv"] = np.ascontiguousarray(inp["state_conv_b"][:nsl[1], sl])
        if nsl[2]:
            m["s_gla"] = np.ascontiguousarray(inp["state_gla_c"][:nsl[2], sl])
        maps.append(m)
    return maps


def kernel(**inp):
    n_cores = 8
    spc = DEC_B // n_cores
    n_ptiles = SEQ // TP
    nc = build_program(n_ptiles, spc)
    in_maps = make_in_maps(inp, n_cores, n_ptiles, spc)
    res = run_bass_kernel_spmd(nc, in_maps, core_ids=list(range(n_cores))).results
    f32 = np.float32
    cat_p = lambda k, ax: np.stack([res[0][k], res[1][k]], ax).astype(f32)
    cat_s = lambda k, ax: np.concatenate([res[c][k] for c in range(n_cores)], ax).astype(f32)
    y_prompt = cat_p("y_p", 0)
    y_sample = cat_s("y_s", 0)
    new_ffn_p = cat_p("o_ffn_p", 1)
    new_ffn_s = cat_s("o_ffn_s", 1)
    new_conv_p = cat_p("o_conv_p", 1)
    new_conv_s = cat_s("o_conv_s", 1)
    kv5 = lambda a: a.reshape(a.shape[0], a.shape[1], 128, 4, 64)
    new_k_p, new_v_p = kv5(cat_p("o_k_p", 1)), kv5(cat_p("o_v_p", 1))
    new_k_s, new_v_s = kv5(cat_s("o_k_s", 1)), kv5(cat_s("o_v_s", 1))
    new_gla_p = cat_p("o_gla_p", 1)
    new_gla_s = cat_s("o_gla_s", 1)
    return (y_prompt, y_sample, new_k_p, new_v_p, new_conv_p, new_gla_p, new_ffn_p,
            new_k_s, new_v_s, new_conv_s, new_gla_s, new_ffn_s)
```

```python
import numpy as np
from contextlib import ExitStack
import concourse.bass as bass
import concourse.mybir as mybir
from concourse.bass_utils import run_bass_kernel_spmd

F32 = mybir.dt.float32
BF16 = mybir.dt.bfloat16
AF = mybir.ActivationFunctionType
ALU = mybir.AluOpType

D = 2048
DC = D // 128
DFF = 5632
FC = DFF // 128
PLE = 256
DEPTH = 4
EPS = 1e-6
TP = 512
TS = 32
SEQ = 8192
DEC_B = 32
DEC_T = 32

WBLK = 8192
NSLOT = 3
PAIRED = ("up", "pw1")
CW = 31


class _Eng:
    def __init__(self, name, eng, sem, same_engine_sync):
        self.name, self.e, self.sem = name, eng, sem
        self.count = 0
        self.pending = False
        self.seen = {}
        self.same_engine_sync = same_engine_sync
        self.q = []

    def replay(self, handle):
        self.e = handle
        for item in self.q:
            if item[0] == "wait":
                handle.wait_ge(item[1], item[2])
            elif item[0] == "ins":
                ins = item[1]()
                if item[2]:
                    ins.then_inc(self.sem, 1)
            else:
                _, out, in_, kw, sem = item
                handle.dma_start(out=out, in_=in_, **kw).then_inc(sem, 16)


class Prog:
    def __init__(self, nc, st, same_engine_sync=True):
        self.nc, self.st = nc, st
        self.res = {}
        self.sems = {}
        mk = lambda n: st.enter_context(nc.semaphore(n))
        self.pe = _Eng("pe", None, mk("s_pe"), False)
        ses_ad = (same_engine_sync is True)
        ses_p = (same_engine_sync is True) or (same_engine_sync == "pool")
        self.act = _Eng("act", None, mk("s_act"), ses_ad)
        self.dve = _Eng("dve", None, mk("s_dve"), ses_ad)
        self.pool = _Eng("pool", None, mk("s_pool"), ses_p)
        self.sp = _Eng("sp", None, mk("s_sp"), False)
        self.dma_sems = {}
        self.n_instr = 0

    def _r(self, key):
        r = self.res.get(key)
        if r is None:
            r = self.res[key] = {"w": None, "r": {}}
        return r

    def _wait(self, E, tok):
        if tok is None:
            return
        sem, val = tok
        k = id(sem)
        if sem is E.sem and not E.same_engine_sync:
            return
        if E.seen.get(k, 0) >= val:
            return
        E.q.append(("wait", sem, val))
        E.seen[k] = val
        self.n_instr += 1

    def _deps(self, E, reads, writes):
        for key in reads:
            self._wait(E, self._r(key)["w"])
        for key in writes:
            r = self._r(key)
            self._wait(E, r["w"])
            for tok in r["r"].values():
                self._wait(E, tok)

    def _commit(self, tok, reads, writes):
        sem, val = tok
        for key in writes:
            r = self._r(key)
            r["w"] = tok
            r["r"] = {}
        for key in reads:
            r = self._r(key)
            old = r["r"].get(id(sem))
            if old is None or old[1] < val:
                r["r"][id(sem)] = tok

    def op(self, E, fn, reads=(), writes=(), signal=True):
        self._deps(E, reads, writes)
        E.q.append(("ins", fn, signal))
        self.n_instr += 1
        if signal:
            E.count += 1
            E.pending = False
            tok = (E.sem, E.count)
        else:
            E.pending = True
            tok = (E.sem, E.count + 1)
        self._commit(tok, reads, writes)
        return tok

    def dma_sem(self, name):
        s = self.dma_sems.get(name)
        if s is None:
            s = self.dma_sems[name] = [self.st.enter_context(self.nc.semaphore("d_" + name)), 0]
        return s

    def dma(self, Q, semname, out, in_, reads=(), writes=(), **kw):
        self._deps(Q, reads, writes)
        s = self.dma_sem(semname)
        s[1] += 16
        Q.q.append(("dma", out, in_, kw, s[0]))
        self.n_instr += 1
        tok = (s[0], s[1])
        self._commit(tok, reads, writes)
        return tok

    def drain_all(self, E):
        for s, tot in self.dma_sems.values():
            if tot:
                self._wait(E, (s, tot))
        for X in (self.pe, self.act, self.dve, self.pool):
            if X.count and X is not E:
                self._wait(E, (X.sem, X.count))


def _wspec(layer, kind, slot):
    out = []
    if kind == 0:
        out.append(("q", "a_w_qkv", slot, D, 0, 2048))
        out.append(("kd", "a_w_qkv", slot, D, 2048, 512))
        out.append(("v", "a_w_qkv", slot, D, 2304, 256))
        out.append(("wo", "a_w_o", slot, D, 0, D))
    elif kind == 1:
        out.append(("pw1", "b_w_pw1", slot, D, 0, 2 * D))
        out.append(("pw2", "b_w_pw2", slot, D, 0, D))
    else:
        out.append(("gq", "c_w_in", slot, D, 0, 1024))
        out.append(("gk", "c_w_in", slot, D, 1024, 1024))
        out.append(("ggl", "c_w_in", slot, D, 6144, 16))
        out.append(("gv", "c_w_in", slot, D, 2048, 2048))
        out.append(("gr", "c_w_in", slot, D, 4096, 2048))
        out.append(("cwo", "c_w_o", slot, D, 0, D))
    out.append(("up", "ffn_w_up", layer, D, 0, 2 * DFF))
    out.append(("down", "ffn_w_down", layer, DFF, 0, D))
    out.append(("pgate", "ple_w_gate", layer, D, 0, D))
    out.append(("pproj", "ple_w_proj", layer, PLE, 0, D))
    return out


def _blk_cols(K, nm=None):
    if K == PLE:
        return D
    if nm == "v":
        return 256
    if nm == "ggl":
        return 16
    kc = K // 128
    c = WBLK // kc
    return min(512, (c // 128) * 128)


def _kinds_slots(kinds):
    cnt = {0: 0, 1: 0, 2: 0}
    ks = []
    for k in kinds:
        if k is None:
            ks.append((None, 0))
        else:
            ks.append((k, cnt[k]))
            cnt[k] += 1
    return ks, cnt


DEFAULT_KINDS = tuple(i % 3 for i in range(DEPTH))
SKIP_FFN = False
ATTN_VSTAGE = 9
GLA_CUT = 99
ATTN_CUT = 99


def build_program(n_ptiles, spc, kinds=DEFAULT_KINDS, same_engine_sync=True):
    nc = bass.Bass("TRN2", target_bir_lowering=False)
    layers = len(kinds)
    LP = max(n_ptiles * TP, TP)
    SPC = max(spc, 1)
    KS, NSL = _kinds_slots(kinds)
    NPT_R = n_ptiles + 1
    dt = lambda name, shape, dtype=F32, kind="ExternalInput": nc.dram_tensor(name, list(shape), dtype, kind=kind).ap()

    x_p = dt("x_p", [LP, D])
    p_p = dt("p_p", [layers, LP, PLE])
    x_s = dt("x_s", [SPC, TS, D])
    p_s = dt("p_s", [layers, SPC, TS, PLE])
    s_ffn = dt("s_ffn", [layers, SPC, 2, DFF])
    ident_in = dt("ident", [128, 128])
    vecs = {}
    for nm, shp in (("norm_mix", [layers, D]), ("norm_ffn", [layers, D]), ("ple_norm", [layers, D]),
                    ("ffn_conv_w", [layers, 3, DFF]), ("ffn_conv_b", [layers, DFF])):
        vecs[nm] = dt(nm, shp)
    wts = {}
    for nm, shp in (("ffn_w_up", [layers, D, 2 * DFF]), ("ffn_w_down", [layers, DFF, D]),
                    ("ple_w_proj", [layers, PLE, D]), ("ple_w_gate", [layers, D, D])):
        wts[nm] = dt(nm, shp)
    NA = NSL[0]
    if NA:
        wts["a_w_qkv"] = dt("a_w_qkv", [NA, D, 2560]); wts["a_w_o"] = dt("a_w_o", [NA, D, D])
        for nm, shp in (("a_q_norm", [NA, 64]), ("a_k_norm", [NA, 64]), ("a_sinks", [NA, 32])):
            vecs[nm] = dt(nm, shp)
        ck_in = dt("ck", [NA, SPC, 128, 256]); cv_in = dt("cv", [NA, SPC, 128, 256])
        rope_in = dt("rope", [NPT_R, 2, 128, TP])
        psign_in = dt("psign", [128, 128])
        o_k_p = dt("o_k_p", [NA, 128, 256], kind="ExternalOutput"); o_v_p = dt("o_v_p", [NA, 128, 256], kind="ExternalOutput")
        o_k_s = dt("o_k_s", [NA, SPC, 128, 256], kind="ExternalOutput"); o_v_s = dt("o_v_s", [NA, SPC, 128, 256], kind="ExternalOutput")
    NG = NSL[2]
    if NG:
        wts["c_w_in"] = dt("c_w_in", [NG, D, 6160]); wts["c_w_o"] = dt("c_w_o", [NG, D, D])
        for nm, shp in (("c_w_gate_up", [NG, 16, 1024]), ("c_gate_bias", [NG, 1024]), ("c_out_norm", [NG, 512])):
            vecs[nm] = dt(nm, shp)
        s_gla = dt("s_gla", [NG, SPC, 4, 256, 512])
        tri_in = dt("tri4", [128, 256])
        o_gla_p = dt("o_gla_p", [NG, 4, 256, 512], kind="ExternalOutput")
        o_gla_s = dt("o_gla_s", [NG, SPC, 4, 256, 512], kind="ExternalOutput")
        gla_carry = [dt(f"gla_carry{g_}", [128, 8, 512], kind="Internal") for g_ in range(NG)]
    NB = NSL[1]
    if NB:
        wts["b_w_pw1"] = dt("b_w_pw1", [NB, D, 2 * D]); wts["b_w_pw2"] = dt("b_w_pw2", [NB, D, D])
        for nm, shp in (("b_w_dw", [NB, CW, D]), ("b_dw_bias", [NB, D]), ("b_ln_g", [NB, D]), ("b_ln_b", [NB, D])):
            vecs[nm] = dt(nm, shp)
        s_conv = dt("s_conv", [NB, SPC, CW - 1, D])
        o_conv_p = dt("o_conv_p", [NB, CW - 1, D], kind="ExternalOutput")
        o_conv_s = dt("o_conv_s", [NB, SPC, CW - 1, D], kind="ExternalOutput")

    y_p = dt("y_p", [LP, D], kind="ExternalOutput")
    y_s = dt("y_s", [SPC, TS, D], kind="ExternalOutput")
    o_ffn_p = dt("o_ffn_p", [layers, 2, DFF], kind="ExternalOutput")
    o_ffn_s = dt("o_ffn_s", [layers, SPC, 2, DFF], kind="ExternalOutput")

    scratch = {}
    for l in range(layers):
        for (nm, src, slot, K, c0, ncols) in _wspec(l, *KS[l]):
            if src not in wts:
                continue
            bc = _blk_cols(K, nm)
            nblk = ncols // bc
            assert nblk * bc == ncols, (nm, ncols, bc)
            scratch[(l, nm)] = (dt(f"wb_{l}_{nm}", [nblk, 128, (K // 128) * bc], BF16, kind="Internal"),
                                 src, slot, K, c0, bc, nblk)

    with ExitStack() as st:
        E = st.enter_context
        sb = lambda name, shape, dtype=F32: E(nc.sbuf_tensor(name, list(shape), dtype))
        P = Prog(nc, st, same_engine_sync=same_engine_sync)
        PE, ACT, DVE, POOL, SP = P.pe, P.act, P.dve, P.pool, P.sp

        ident = sb("ident_sb", [128, 128])
        ident_bf = sb("ident_bf", [128, 128], BF16)
        ones_bf = sb("ones_bf", [128, 128], BF16)
        g_mix = sb("g_mix", [128, layers, DC]); g_ffn = sb("g_ffn", [128, layers, DC]); g_ple = sb("g_ple", [128, layers, DC])
        cw = sb("cw", [128, layers, 3, FC]); cb = sb("cb", [128, layers, FC])
        h = sb("h", [128, DC, TP])
        xn = sb("xn", [128, DC, TP], BF16)
        act = sb("act", [128, FC, TP], BF16)
        sq = act
        rstd = sb("rstd", [128, TP])
        gext = [sb(f"gext{i}", [128, TP + 2]) for i in range(2)]
        acc = [sb(f"acc{i}", [128, TP]) for i in range(2)]
        sil = [sb(f"sil{i}", [128, TP]) for i in range(2)]
        fhist = sb("fhist", [128, layers, 2, FC])
        stage_i = sb("stage_i", [128, 512])
        stage_o = sb("stage_o", [128, 512])
        pT = sb("pT", [128, 2, TP], BF16)
        gate = sb("gate", [128, TP])
        tr_i = sb("tr_i", [128, 128]); tr_o = sb("tr_o", [128, 128])
        wpp = sb("wpp", [128, 2, D], BF16)
        wring = sb("wring", [128, NSLOT, WBLK], BF16)
        act32 = act[:].rearrange("p c t -> p (c t)").bitcast(F32).rearrange("p (c t) -> p c t", t=TP)
        assert tuple(act32.shape) == (128, FC // 2, TP), act32.shape
        mean = sb("mean", [128, TP])
        if NA:
            bd = sb("bd", [128, 128], BF16)
            psign = sb("psign_sb", [128, 128])
            gqk = sb("gqk", [128, NA, 2])
            rgm = sb("rgm", [128, NA, 2, 128])
            esink = sb("esink", [128, NA, 16])
            khist = sb("khist", [128, NA, 4, 128], BF16)
            vhist = sb("vhist", [64, NA, 2, 256], BF16)
            kout = sb("kout", [128, 4, 128])
            vout = sb("vout", [64, 2, 256])
            dnm = sb("dnm", [128, 256])
        if NG:
            wgu = sb("wgu", [16, NG, 1024], BF16)
            ngb = sb("ngb", [128, NG, 8])
            onw = sb("onw", [128, NG, 4])
            elast = sb("elast", [128, 8, 8])
            cmask = sb("cmask", [128, TP], BF16)
            tri4 = sb("tri4_sb", [128, 256])
            osq = sb("osq", [128, 1024], BF16)
            glT = sb("glT", [16, TP], BF16)
        if NB:
            ones_f = sb("ones_f", [128, 128])
            chist = sb("chist", [128, NB, DC, CW - 1])
            dww = sb("dww", [128, NB, CW, DC]); dwb = sb("dwb", [128, NB, DC])
            lng = sb("lng", [128, NB, DC]); lnb = sb("lnb", [128, NB, DC])
        psA = [E(nc.psum_tensor(f"ps{i}", [128, 512], F32)) for i in range(8)]

        SQ_KEYS = [("act", c) for c in range(DC)]
        XN_ALL = [("xn", c) for c in range(DC)]
        ACT_ALL = [("act", j) for j in range(FC)]
        FH = lambda l: [("fhist", l, j) for j in range(FC)]

        P.dma(SP, "const", ident[:], ident_in, writes=["const"])
        P.op(POOL, lambda: POOL.e.memset(ones_bf[:], 1.0), writes=["ones"])
        P.op(DVE, lambda: DVE.e.tensor_copy(out=ident_bf[:], in_=ident[:]), reads=["const"], writes=["ident_bf"])
        for nm, t in (("norm_mix", g_mix), ("norm_ffn", g_ffn), ("ple_norm", g_ple)):
            P.dma(SP, "const", t[:], vecs[nm].rearrange("l (c p) -> p l c", p=128), writes=["const"],
                  allow_slow_non_contiguous=True)
        P.dma(SP, "const", cw[:], vecs["ffn_conv_w"].rearrange("l t (c p) -> p l t c", p=128), writes=["const"],
              allow_slow_non_contiguous=True)
        P.dma(SP, "const", cb[:], vecs["ffn_conv_b"].rearrange("l (c p) -> p l c", p=128), writes=["const"],
              allow_slow_non_contiguous=True)

        if NA:
            P.op(POOL, lambda: POOL.e.memset(bd[:], 0.0), writes=["bd"])
            P.op(POOL, lambda: POOL.e.memset(bd[0:64, 0:64], 1.0), writes=["bd"])
            P.op(POOL, lambda: POOL.e.memset(bd[64:128, 64:128], 1.0), writes=["bd"])
            P.dma(SP, "const", psign[:], psign_in, writes=["const"])
            for a in range(NA):
                for j, nm in enumerate(("a_q_norm", "a_k_norm")):
                    for hh in range(2):
                        P.dma(SP, "const", gqk[64 * hh:64 * hh + 64, a, j:j + 1], vecs[nm][a].rearrange("(d o) -> d o", o=1),
                              writes=["const"], allow_slow_non_contiguous=True)
                for hh in range(2):
                    P.dma(SP, "const", esink[64 * hh:64 * hh + 64, a, :],
                          vecs["a_sinks"][a].rearrange("(kp h) -> h kp", h=2)[hh].partition_broadcast(64), writes=["const"],
                          allow_slow_non_contiguous=True)
            for a in range(NA):
                for j in range(2):
                    P.op(DVE, lambda a=a, j=j: DVE.e.tensor_scalar(out=rgm[:, a, j, :], in0=psign[:], scalar1=gqk[:, a, j:j + 1],
                                                                   scalar2=None, op0=ALU.mult), reads=["const"], writes=["rgm"])
                P.op(ACT, lambda a=a: ACT.e.activation(out=esink[:, a, :], in_=esink[:, a, :], func=AF.Exp),
                     reads=["const"], writes=["esink"])
        if NG:
            P.dma(SP, "const", tri4[:], tri_in, writes=["const"])
            for g_ in range(NG):
                P.dma(POOL, "ld_wgu", wgu[:, g_, :], vecs["c_w_gate_up"][g_], writes=["wgu"])
                P.dma(SP, "const", ngb[:, g_, :], vecs["c_gate_bias"][g_].rearrange("(c p) -> p c", p=128), writes=["const"],
                      allow_slow_non_contiguous=True)
                P.dma(SP, "const", onw[:, g_, :], vecs["c_out_norm"][g_].rearrange("(c p) -> p c", p=128), writes=["const"],
                      allow_slow_non_contiguous=True)
            P.op(DVE, lambda: DVE.e.tensor_scalar(out=ngb[:], in0=ngb[:], scalar1=-1.0, scalar2=None, op0=ALU.mult),
                 reads=["const"], writes=["ngb"])
            P.op(POOL, lambda: POOL.e.memset(cmask[:], 1.0), writes=["cmask"])
            P.op(POOL, lambda: POOL.e.memset(cmask[:].rearrange("p (c t) -> p c t", t=64)[:, :, 0:1], 0.0), writes=["cmask"])
        if NB:
            P.op(POOL, lambda: POOL.e.memset(ones_f[:], 1.0), writes=["ones"])
            P.dma(SP, "const", dww[:], vecs["b_w_dw"].rearrange("n t (c p) -> p n t c", p=128), writes=["const"],
                  allow_slow_non_contiguous=True)
            for nm, t in (("b_dw_bias", dwb), ("b_ln_g", lng), ("b_ln_b", lnb)):
                P.dma(SP, "const", t[:], vecs[nm].rearrange("n (c p) -> p n c", p=128), writes=["const"],
                      allow_slow_non_contiguous=True)

        for (l, nm), (wb, src, slot, K, c0, bc, nblk) in scratch.items():
            kc = K // 128
            if SKIP_FFN and nm in ("up", "down", "pgate", "pproj"):
                continue
            if nm == "kd":
                dstv = wb[0].rearrange("p (kc c) -> p kc c", kc=kc)
                for kv in range(4):
                    for r_ in range(2):
                        src_ap = wts[src][slot, :, 2048 + kv * 64: 2048 + (kv + 1) * 64].rearrange("(kc p) c -> p kc c", p=128)
                        P.dma(POOL, "wcast", dstv[:, :, kv * 128 + r_ * 64: kv * 128 + (r_ + 1) * 64], src_ap,
                              writes=["wb", ("wbk", l, nm, 0)])
                continue
            if nm in PAIRED:
                half = (nblk * bc) // 2
                for b in range(nblk):
                    dstv = wb[b].rearrange("p (kc c) -> p kc c", kc=kc)
                    for part in range(2):
                        src_ap = wts[src][slot, :, part * half + b * 256: part * half + (b + 1) * 256].rearrange(
                            "(kc p) c -> p kc c", p=128)
                        P.dma(POOL, "wcast", dstv[:, :, part * 256:(part + 1) * 256], src_ap, writes=["wb", ("wbk", l, nm, b)])
                continue
            for b in range(nblk):
                src_ap = wts[src][slot, :, c0 + b * bc: c0 + (b + 1) * bc].rearrange("(kc p) c -> p kc c", p=128)
                dst_ap = wb[b].rearrange("p (kc c) -> p kc c", kc=kc)
                P.dma(POOL, "wcast", dst_ap, src_ap, writes=["wb", ("wbk", l, nm, b)])

        ring = {"n": 0}

        def wload(l, nm, b):
            wb, src, slot_, K, c0, bc, nblk = scratch[(l, nm)]
            kc = K // 128
            s = ring["n"] % NSLOT
            ring["n"] += 1
            P.dma(SP, f"w{s}", wring[:, s, 0:kc * bc], wb[b], reads=[("wbk", l, nm, b)], writes=[("ring", s)])
            return s, wring[:, s, 0:kc * bc].rearrange("p (kc c) -> p kc c", kc=kc)

        psn = {"n": 0}

        def nextps():
            i = psn["n"] % 8
            psn["n"] += 1
            return i

        def rmsnorm(gt, l, T):
            P.op(ACT, lambda: ACT.e.activation(out=sq[:, 0:DC, 0:T], in_=h[:, :, 0:T], func=AF.Square),
                 reads=["h"], writes=SQ_KEYS)
            pi = nextps()
            for c in range(DC):
                P.op(PE, lambda c=c, pi=pi: PE.e.matmul(psA[pi][:, 0:T], lhsT=ones_bf[:], rhs=sq[:, c, 0:T],
                                                        start=(c == 0), stop=(c == DC - 1)),
                     reads=SQ_KEYS + ["ones"], writes=[("ps", pi)], signal=(c == DC - 1))
            P.op(DVE, lambda pi=pi: DVE.e.tensor_scalar(out=rstd[:, 0:T], in0=psA[pi][:, 0:T], scalar1=1.0 / D,
                                                        scalar2=EPS, op0=ALU.mult, op1=ALU.add),
                 reads=[("ps", pi)], writes=["rstd"])
            P.op(ACT, lambda: ACT.e.activation(out=rstd[:, 0:T], in_=rstd[:, 0:T], func=AF.Sqrt),
                 reads=["rstd"], writes=["rstd"])
            P.op(DVE, lambda: DVE.e.reciprocal(out=rstd[:, 0:T], in_=rstd[:, 0:T]), reads=["rstd"], writes=["rstd"])
            for c in range(DC):
                P.op(DVE, lambda c=c: DVE.e.scalar_tensor_tensor(out=xn[:, c, 0:T], in0=h[:, c, 0:T],
                                                                 scalar=gt[:, l, c:c + 1], in1=rstd[:, 0:T],
                                                                 op0=ALU.mult, op1=ALU.mult),
                     reads=["h", "rstd", "const"], writes=[("xn", c)])

        def conv_ffn(l, T, zero_hist):
            rmsnorm(g_ffn, l, T)
            wb, src, slot_, K, c0, bc, nblk = scratch[(l, "up")]
            cur_up = None
            for j in range(FC):
                if j % 2 == 0:
                    cur_up = wload(l, "up", j // 2)
                ops = [(cur_up, j % 2), (cur_up, 2 + j % 2)]
                pg, pu = nextps(), nextps()
                for ((s, wv), off), pi in zip(ops, (pg, pu)):
                    for k in range(DC):
                        P.op(PE, lambda k=k, wv=wv, off=off, pi=pi: PE.e.matmul(
                            psA[pi][:, 0:T], lhsT=wv[:, k, off * 128:(off + 1) * 128], rhs=xn[:, k, 0:T],
                            start=(k == 0), stop=(k == DC - 1)),
                            reads=[("ring", s)] + XN_ALL, writes=[("ps", pi)], signal=(k == DC - 1))
                i = j % 2
                ge, ac, si = gext[i], acc[i], sil[i]
                if zero_hist:
                    P.op(DVE, lambda ge=ge: DVE.e.memset(ge[:, 0:2], 0.0), writes=[("gext", i)])
                else:
                    P.op(ACT, lambda ge=ge, j=j: ACT.e.activation(out=ge[:, 0:2], in_=fhist[:, l, :, j], func=AF.Copy),
                         reads=[("fhist", l, j)], writes=[("gext", i)])
                P.op(ACT, lambda ge=ge, pg=pg: ACT.e.activation(out=ge[:, 2:T + 2], in_=psA[pg][:, 0:T], func=AF.Copy),
                     reads=[("ps", pg)], writes=[("gext", i)])
                P.op(ACT, lambda ge=ge, j=j: ACT.e.activation(out=fhist[:, l, :, j], in_=ge[:, T:T + 2], func=AF.Copy),
                     reads=[("gext", i)], writes=[("fhist", l, j)])
                P.op(DVE, lambda ge=ge, ac=ac, j=j: DVE.e.tensor_scalar(
                    out=ac[:, 0:T], in0=ge[:, 2:T + 2], scalar1=cw[:, l, 2, j:j + 1], scalar2=cb[:, l, j:j + 1],
                    op0=ALU.mult, op1=ALU.add), reads=[("gext", i), "const"], writes=[("acc", i)])
                for tap in (1, 0):
                    P.op(DVE, lambda ge=ge, ac=ac, j=j, tap=tap: DVE.e.scalar_tensor_tensor(
                        out=ac[:, 0:T], in0=ge[:, tap:T + tap], scalar=cw[:, l, tap, j:j + 1], in1=ac[:, 0:T],
                        op0=ALU.mult, op1=ALU.add), reads=[("gext", i), ("acc", i), "const"], writes=[("acc", i)])
                P.op(ACT, lambda ac=ac, si=si: ACT.e.activation(out=si[:, 0:T], in_=ac[:, 0:T], func=AF.Silu),
                     reads=[("acc", i)], writes=[("sil", i)])
                P.op(DVE, lambda si=si, pu=pu, j=j: DVE.e.tensor_tensor(out=act[:, j, 0:T], in0=psA[pu][:, 0:T],
                                                                       in1=si[:, 0:T], op=ALU.mult),
                     reads=[("sil", i), ("ps", pu)], writes=[("act", j)])
            wb, src, slot_, K, c0, bc, nblk = scratch[(l, "down")]
            cpb = bc // 128
            for b in range(nblk):
                s, wv = wload(l, "down", b)
                for off in range(cpb):
                    m = b * cpb + off
                    pi = nextps()
                    for k in range(FC):
                        P.op(PE, lambda k=k, wv=wv, off=off, pi=pi: PE.e.matmul(
                            psA[pi][:, 0:T], lhsT=wv[:, k, off * 128:(off + 1) * 128], rhs=act[:, k, 0:T],
                            start=(k == 0), stop=(k == FC - 1)),
                            reads=[("ring", s)] + ACT_ALL, writes=[("ps", pi)], signal=(k == FC - 1))
                    P.op(DVE, lambda m=m, pi=pi: DVE.e.tensor_tensor(out=h[:, m, 0:T], in0=psA[pi][:, 0:T],
                                                                     in1=h[:, m, 0:T], op=ALU.add),
                         reads=[("ps", pi), "h"], writes=["h"])

        def conformer(l, slot, T, zero_hist):
            rmsnorm(g_mix, l, T)
            W = CW - 1
            c32 = act32[:, 0:DC, :]
            uext = [act32[:, 16 + 2 * i: 18 + 2 * i, :].rearrange("p a t -> p (a t)") for i in range(2)]
            C32 = [("act", j) for j in range(0, 32)]
            UX = [[("act", j) for j in range(32 + 4 * i, 36 + 4 * i)] for i in range(2)]
            cur_pw = None
            for m in range(DC):
                if m % 2 == 0:
                    cur_pw = wload(l, "pw1", m // 2)
                ops = [(cur_pw, m % 2), (cur_pw, 2 + m % 2)]
                pa, pg = nextps(), nextps()
                for ((s_, wv), off), pi in zip(ops, (pa, pg)):
                    for k in range(DC):
                        P.op(PE, lambda k=k, wv=wv, off=off, pi=pi: PE.e.matmul(
                            psA[pi][:, 0:T], lhsT=wv[:, k, off * 128:(off + 1) * 128], rhs=xn[:, k, 0:T],
                            start=(k == 0), stop=(k == DC - 1)),
                            reads=[("ring", s_)] + XN_ALL, writes=[("ps", pi)], signal=(k == DC - 1))
                i = m % 2
                ux, si = uext[i], sil[i]
                if zero_hist:
                    P.op(DVE, lambda ux=ux: DVE.e.memset(ux[:, 0:W], 0.0), writes=UX[i])
                else:
                    P.op(ACT, lambda ux=ux, m=m: ACT.e.activation(out=ux[:, 0:W], in_=chist[:, slot, m, :], func=AF.Copy),
                         reads=[("chist", m)], writes=UX[i])
                P.op(ACT, lambda si=si, pg=pg: ACT.e.activation(out=si[:, 0:T], in_=psA[pg][:, 0:T], func=AF.Sigmoid),
                     reads=[("ps", pg)], writes=[("sil", i)])
                P.op(DVE, lambda ux=ux, si=si, pa=pa: DVE.e.tensor_tensor(out=ux[:, W:W + T], in0=psA[pa][:, 0:T],
                                                                        in1=si[:, 0:T], op=ALU.mult),
                     reads=[("ps", pa), ("sil", i)], writes=UX[i])
                P.op(ACT, lambda ux=ux, m=m: ACT.e.activation(out=chist[:, slot, m, :], in_=ux[:, T:T + W], func=AF.Copy),
                     reads=UX[i], writes=[("chist", m)])
                CK = [("act", 2 * m), ("act", 2 * m + 1)]
                bacc = acc[i]
                P.op(DVE, lambda ux=ux, m=m: DVE.e.tensor_scalar(
                    out=c32[:, m, 0:T], in0=ux[:, 0:T], scalar1=dww[:, slot, 0, m:m + 1], scalar2=dwb[:, slot, m:m + 1],
                    op0=ALU.mult, op1=ALU.add), reads=UX[i] + ["const"], writes=CK)
                P.op(DVE, lambda ux=ux, m=m, bacc=bacc: DVE.e.tensor_scalar(
                    out=bacc[:, 0:T], in0=ux[:, 1:1 + T], scalar1=dww[:, slot, 1, m:m + 1], scalar2=None, op0=ALU.mult),
                    reads=UX[i] + ["const"], writes=[("acc", i)])
                for j in range(2, CW):
                    if j % 2 == 0:
                        P.op(DVE, lambda ux=ux, m=m, j=j: DVE.e.scalar_tensor_tensor(
                            out=c32[:, m, 0:T], in0=ux[:, j:j + T], scalar=dww[:, slot, j, m:m + 1], in1=c32[:, m, 0:T],
                            op0=ALU.mult, op1=ALU.add), reads=UX[i] + ["const"] + CK, writes=CK)
                    else:
                        P.op(DVE, lambda ux=ux, m=m, j=j, bacc=bacc: DVE.e.scalar_tensor_tensor(
                            out=bacc[:, 0:T], in0=ux[:, j:j + T], scalar=dww[:, slot, j, m:m + 1], in1=bacc[:, 0:T],
                            op0=ALU.mult, op1=ALU.add), reads=UX[i] + ["const", ("acc", i)], writes=[("acc", i)])
                P.op(DVE, lambda m=m, bacc=bacc: DVE.e.tensor_tensor(out=c32[:, m, 0:T], in0=c32[:, m, 0:T], in1=bacc[:, 0:T], op=ALU.add),
                     reads=CK + [("acc", i)], writes=CK)
            P.op(ACT, lambda: ACT.e.activation(out=xn[:, :, 0:T], in_=c32[:, :, 0:T], func=AF.Square),
                 reads=C32, writes=XN_ALL)
            p1, p2 = nextps(), nextps()
            for c in range(DC):
                P.op(PE, lambda c=c, p1=p1: PE.e.matmul(psA[p1][:, 0:T], lhsT=ones_f[:], rhs=c32[:, c, 0:T],
                                                        start=(c == 0), stop=(c == DC - 1)),
                     reads=C32 + ["ones"], writes=[("ps", p1)], signal=(c == DC - 1))
            for c in range(DC):
                P.op(PE, lambda c=c, p2=p2: PE.e.matmul(psA[p2][:, 0:T], lhsT=ones_bf[:], rhs=xn[:, c, 0:T],
                                                        start=(c == 0), stop=(c == DC - 1)),
                     reads=XN_ALL + ["ones"], writes=[("ps", p2)], signal=(c == DC - 1))
            P.op(DVE, lambda p1=p1: DVE.e.tensor_scalar(out=mean[:, 0:T], in0=psA[p1][:, 0:T], scalar1=1.0 / D, scalar2=None,
                                                        op0=ALU.mult), reads=[("ps", p1)], writes=["mean"])
            P.op(DVE, lambda: DVE.e.tensor_tensor(out=gate[:, 0:T], in0=mean[:, 0:T], in1=mean[:, 0:T], op=ALU.mult),
                 reads=["mean"], writes=["gate"])
            P.op(DVE, lambda p2=p2: DVE.e.scalar_tensor_tensor(out=rstd[:, 0:T], in0=psA[p2][:, 0:T], scalar=1.0 / D,
                                                               in1=gate[:, 0:T], op0=ALU.mult, op1=ALU.subtract),
                 reads=[("ps", p2), "gate"], writes=["rstd"])
            P.op(DVE, lambda: DVE.e.tensor_scalar(out=rstd[:, 0:T], in0=rstd[:, 0:T], scalar1=EPS, scalar2=None, op0=ALU.add),
                 reads=["rstd"], writes=["rstd"])
            P.op(ACT, lambda: ACT.e.activation(out=rstd[:, 0:T], in_=rstd[:, 0:T], func=AF.Sqrt), reads=["rstd"], writes=["rstd"])
            P.op(DVE, lambda: DVE.e.reciprocal(out=rstd[:, 0:T], in_=rstd[:, 0:T]), reads=["rstd"], writes=["rstd"])
            for m in range(DC):
                i = m % 2
                ac = acc[i]
                P.op(DVE, lambda m=m, ac=ac: DVE.e.tensor_tensor(out=ac[:, 0:T], in0=c32[:, m, 0:T], in1=mean[:, 0:T],
                                                                 op=ALU.subtract),
                     reads=[("act", 2 * m), ("act", 2 * m + 1), "mean"], writes=[("acc", i)])
                P.op(DVE, lambda ac=ac: DVE.e.tensor_tensor(out=ac[:, 0:T], in0=ac[:, 0:T], in1=rstd[:, 0:T], op=ALU.mult),
                     reads=[("acc", i), "rstd"], writes=[("acc", i)])
                P.op(ACT, lambda m=m, ac=ac: ACT.e.activation(out=xn[:, m, 0:T], in_=ac[:, 0:T], func=AF.Silu,
                                                              scale=lng[:, slot, m:m + 1], bias=lnb[:, slot, m:m + 1]),
                     reads=[("acc", i), "const"], writes=[("xn", m)])
            cpb2 = scratch[(l, "pw2")][5] // 128
            for b in range(scratch[(l, "pw2")][6]):
                s_, wv = wload(l, "pw2", b)
                for off in range(cpb2):
                    m = b * cpb2 + off
                    pi = nextps()
                    for k in range(DC):
                        P.op(PE, lambda k=k, wv=wv, off=off, pi=pi: PE.e.matmul(
                            psA[pi][:, 0:T], lhsT=wv[:, k, off * 128:(off + 1) * 128], rhs=xn[:, k, 0:T],
                            start=(k == 0), stop=(k == DC - 1)),
                            reads=[("ring", s_)] + XN_ALL, writes=[("ps", pi)], signal=(k == DC - 1))
                    P.op(DVE, lambda m=m, pi=pi: DVE.e.tensor_tensor(out=h[:, m, 0:T], in0=psA[pi][:, 0:T],
                                                                     in1=h[:, m, 0:T], op=ALU.add),
                         reads=[("ps", pi), "h"], writes=["h"])

        CH_ALL = [("chist", m) for m in range(DC)]

        def attention(l, slot, T, CL, first_prompt, rope_idx, sample_idx, emit_out):
            rmsnorm(g_mix, l, T)
            NQC = T // CL
            actf = act[:].rearrange("p c t -> p (c t)")
            qT = act[:, 0:16, :]
            oT = act[:, 16:32, :]
            KT = actf[:, 32 * TP: 32 * TP + 4 * 640].rearrange("p (k s) -> p k s", k=4)
            VV = actf[0:64, 37 * TP: 37 * TP + 10 * 256].rearrange("p (c f) -> p c f", c=10)
            sqh = [act[:, 42 + i, :] for i in range(2)]
            ET = xn[0:64, 0:4, :]
            QT_K = [("act", j) for j in range(0, 16)]; OT_K = [("act", j) for j in range(16, 32)]
            KT_K = [("act", j) for j in range(32, 37)]; VV_K = [("act", j) for j in range(37, 42)]
            cosT, sinT = gext[0], gext[1]
            if ATTN_CUT <= -1:
                return
            P.dma(SP, "ld_gext0", cosT[:, 0:TP], rope_in[rope_idx, 0], writes=[("gext", 0)])
            P.dma(SP, "ld_gext1", sinT[:, 0:TP], rope_in[rope_idx, 1], writes=[("gext", 1)])
            if sample_idx is None:
                if not first_prompt:
                    P.op(POOL, lambda: POOL.e.tensor_copy(out=KT[:, :, 0:128], in_=khist[:, slot]), reads=["khist"], writes=KT_K)
                    P.op(POOL, lambda: POOL.e.tensor_copy(out=VV[:, 0:2, :], in_=vhist[:, slot]), reads=["vhist"], writes=VV_K)
            else:
                for r_ in range(2):
                    P.dma(SP, "ld_stage", stage_i[:, :].rearrange("p (k r d) -> p k r d", k=4, r=2)[:, :, r_, :],
                          ck_in[slot, sample_idx].rearrange("s (k d) -> s k d", k=4), writes=["stage_i"])
                pi = nextps()
                for kv in range(4):
                    P.op(PE, lambda kv=kv, pi=pi: PE.e.transpose(psA[pi][:, kv * 128:(kv + 1) * 128],
                                                                 stage_i[:, kv * 128:(kv + 1) * 128], ident[:]),
                         reads=["stage_i", "const"], writes=[("ps", pi)], signal=(kv == 3))
                P.op(ACT, lambda pi=pi: ACT.e.activation(out=KT[:, :, 0:128], in_=psA[pi][:, :].rearrange("p (k s) -> p k s", k=4),
                                                         func=AF.Copy), reads=[("ps", pi)], writes=KT_K)
                for c2 in range(2):
                    P.dma(SP, "ld_vout", vout[:, c2, :], cv_in[slot, sample_idx, c2 * 64:(c2 + 1) * 64, :], writes=["vout"])
                P.op(POOL, lambda: POOL.e.tensor_copy(out=VV[:, 0:2, :], in_=vout[:]), reads=["vout"], writes=VV_K)

            if ATTN_CUT <= 0:
                return
            sv, wvv = wload(l, "v", 0)
            nvc = (T + 63) // 64
            for c in range(nvc):
                if ATTN_VSTAGE < 1:
                    break
                rows = min(64, T - c * 64)
                pi = nextps()
                for k in range(DC):
                    P.op(PE, lambda k=k, c=c, rows=rows, pi=pi: PE.e.matmul(
                        psA[pi][0:rows, 0:256], lhsT=xn[:, k, c * 64: c * 64 + rows], rhs=wvv[:, k, :],
                        start=(k == 0), stop=(k == DC - 1)),
                        reads=[("ring", sv)] + XN_ALL, writes=[("ps", pi)], signal=(k == DC - 1))
                if ATTN_VSTAGE < 2:
                    continue
                if ATTN_VSTAGE >= 3 and emit_out and c >= nvc - 2:
                    vi = (c - (nvc - 2)) if nvc >= 2 else 0
                    P.op(ACT, lambda rows=rows, pi=pi, vi=vi: ACT.e.activation(out=vout[0:rows, vi, :], in_=psA[pi][0:rows, 0:256],
                                                                              func=AF.Copy), reads=[("ps", pi)], writes=["vout"])
                    P.op(POOL, lambda c=c, rows=rows, vi=vi: POOL.e.tensor_copy(out=VV[0:rows, 2 + c, :], in_=vout[0:rows, vi, :]),
                         reads=["vout"], writes=VV_K)
                else:
                    P.op(ACT, lambda c=c, rows=rows, pi=pi: ACT.e.activation(out=VV[0:rows, 2 + c, :], in_=psA[pi][0:rows, 0:256],
                                                                            func=AF.Copy), reads=[("ps", pi)], writes=VV_K)

            if ATTN_CUT <= 1:
                return
            def qk_post(pi, gsel, dst_bf, dst_keys, dst_f32, i):
                xs, t1, rs_ = acc[i], sil[i], (rstd if i == 0 else mean)
                RS = "rstd" if i == 0 else "mean"
                P.op(ACT, lambda: ACT.e.activation(out=xs[:, 0:T], in_=psA[pi][:, 0:T], func=AF.Copy),
                     reads=[("ps", pi)], writes=[("acc", i)])
                P.op(ACT, lambda: ACT.e.activation(out=sqh[i][:, 0:T], in_=xs[:, 0:T], func=AF.Square),
                     reads=[("acc", i)], writes=[("act", 42 + i)])
                p2, pr = nextps(), nextps()
                P.op(PE, lambda: PE.e.matmul(psA[p2][:, 0:T], lhsT=bd[:], rhs=sqh[i][:, 0:T], start=True, stop=True),
                     reads=[("act", 42 + i), "bd"], writes=[("ps", p2)])
                P.op(PE, lambda: PE.e.matmul(psA[pr][:, 0:T], lhsT=rgm[:, slot, gsel, :], rhs=xs[:, 0:T], start=True, stop=True),
                     reads=[("acc", i), "rgm"], writes=[("ps", pr)])
                P.op(DVE, lambda: DVE.e.tensor_scalar(out=rs_[:, 0:T], in0=psA[p2][:, 0:T], scalar1=1.0 / 64, scalar2=EPS,
                                                      op0=ALU.mult, op1=ALU.add), reads=[("ps", p2)], writes=[RS])
                P.op(ACT, lambda: ACT.e.activation(out=rs_[:, 0:T], in_=rs_[:, 0:T], func=AF.Sqrt), reads=[RS], writes=[RS])
                P.op(DVE, lambda: DVE.e.reciprocal(out=rs_[:, 0:T], in_=rs_[:, 0:T]), reads=[RS], writes=[RS])
                P.op(DVE, lambda: DVE.e.scalar_tensor_tensor(out=t1[:, 0:T], in0=xs[:, 0:T], scalar=gqk[:, slot, gsel:gsel + 1],
                                                             in1=cosT[:, 0:T], op0=ALU.mult, op1=ALU.mult),
                     reads=[("acc", i), "const", ("gext", 0)], writes=[("sil", i)])
                P.op(DVE, lambda: DVE.e.tensor_tensor(out=xs[:, 0:T], in0=psA[pr][:, 0:T], in1=sinT[:, 0:T], op=ALU.mult),
                     reads=[("ps", pr), ("gext", 1)], writes=[("acc", i)])
                P.op(DVE, lambda: DVE.e.tensor_tensor(out=t1[:, 0:T], in0=t1[:, 0:T], in1=xs[:, 0:T], op=ALU.add),
                     reads=[("sil", i), ("acc", i)], writes=[("sil", i)])
                P.op(DVE, lambda: DVE.e.tensor_tensor(out=t1[:, 0:T], in0=t1[:, 0:T], in1=rs_[:, 0:T], op=ALU.mult),
                     reads=[("sil", i), ("rs", i)], writes=[("sil", i)])
                P.op(ACT, lambda: ACT.e.activation(out=dst_bf, in_=t1[:, 0:T], func=AF.Copy), reads=[("sil", i)], writes=dst_keys)
                if dst_f32 is not None:
                    P.op(POOL, lambda: POOL.e.tensor_copy(out=dst_f32, in_=t1[:, T - min(T, 128):T]),
                         reads=[("sil", i)], writes=["kout"])

            sk, wvk = wload(l, "kd", 0)
            for kv in range(4):
                pi = nextps()
                for k in range(DC):
                    P.op(PE, lambda k=k, kv=kv, pi=pi: PE.e.matmul(
                        psA[pi][:, 0:T], lhsT=wvk[:, k, kv * 128:(kv + 1) * 128], rhs=xn[:, k, 0:T],
                        start=(k == 0), stop=(k == DC - 1)),
                        reads=[("ring", sk)] + XN_ALL, writes=[("ps", pi)], signal=(k == DC - 1))
                qk_post(pi, 1, KT[:, kv, 128:128 + T], KT_K, kout[:, kv, 0:min(T, 128)] if emit_out else None, kv % 2)
            if ATTN_CUT <= 2:
                return
            cpbq = scratch[(l, "q")][5] // 128
            cur = None
            for m in range(DC):
                b, off = m // cpbq, m % cpbq
                if off == 0:
                    cur = wload(l, "q", b)
                sq_, wvq = cur
                pi = nextps()
                for k in range(DC):
                    P.op(PE, lambda k=k, wvq=wvq, off=off, pi=pi: PE.e.matmul(
                        psA[pi][:, 0:T], lhsT=wvq[:, k, off * 128:(off + 1) * 128], rhs=xn[:, k, 0:T],
                        start=(k == 0), stop=(k == DC - 1)),
                        reads=[("ring", sq_)] + XN_ALL, writes=[("ps", pi)], signal=(k == DC - 1))
                qk_post(pi, 0, qT[:, m, 0:T], [("act", m)], None, m % 2)

            if ATTN_CUT <= 3:
                return
            unit = 0
            sc_rot = 0
            for c in range(NQC):
                if sample_idx is None:
                    kchunks = [(c + j, 64) for j in range(3) if not (first_prompt and c + j < 2)]
                else:
                    kchunks = [(0, 64), (1, 64), (2, T)]
                for kv in range(4):
                    pO, pD = 2 * (unit % 2), 2 * (unit % 2) + 1
                    unit += 1
                    for ki, (kc, ks) in enumerate(kchunks):
                        sl_ = sc_rot % 2
                        sc_rot += 1
                        pS = [4 + 2 * sl_, 5 + 2 * sl_]
                        for par in range(2):
                            for pr_ in range(4):
                                hd = kv * 8 + 2 * pr_ + par
                                qc = hd // 2
                                P.op(PE, lambda kc=kc, ks=ks, par=par, pr_=pr_, qc=qc, kv=kv, c=c, pS=pS: PE.e.matmul(
                                    psA[pS[par]][0:ks, pr_ * CL:(pr_ + 1) * CL],
                                    lhsT=KT[64 * par:64 * par + 64, kv, kc * 64: kc * 64 + ks],
                                    rhs=qT[64 * par:64 * par + 64, qc, c * CL:(c + 1) * CL], start=True, stop=True),
                                    reads=KT_K + [("act", qc)], writes=[("ps", pS[par])], signal=(pr_ == 3))
                            P.op(ACT, lambda ks=ks, par=par, sl_=sl_, pS=pS: ACT.e.activation(
                                out=ET[0:ks, 2 * sl_ + par, 0:4 * CL], in_=psA[pS[par]][0:ks, 0:4 * CL], func=AF.Exp, scale=0.125),
                                reads=[("ps", pS[par])], writes=[("xn", 2 * sl_ + par)])
                        for par in range(2):
                            first, last = (ki == 0), (ki == len(kchunks) - 1)
                            P.op(PE, lambda kc=kc, ks=ks, par=par, sl_=sl_, kv=kv, pO=pO, first=first, last=last: PE.e.matmul(
                                psA[pO][64 * par:64 * par + 64, 0:4 * CL], lhsT=VV[0:ks, kc, kv * 64:(kv + 1) * 64],
                                rhs=ET[0:ks, 2 * sl_ + par, 0:4 * CL], start=first, stop=last),
                                reads=VV_K + [("xn", 2 * sl_ + par)], writes=[("ps", pO)], signal=False)
                            P.op(PE, lambda ks=ks, par=par, sl_=sl_, pD=pD, first=first, last=last: PE.e.matmul(
                                psA[pD][64 * par:64 * par + 64, 0:4 * CL], lhsT=ones_bf[0:ks, 0:64],
                                rhs=ET[0:ks, 2 * sl_ + par, 0:4 * CL], start=first, stop=last),
                                reads=["ones", ("xn", 2 * sl_ + par)], writes=[("ps", pD)], signal=(last and par == 1))
                    for pr_ in range(4):
                        P.op(DVE, lambda pr_=pr_, kv=kv, pD=pD: DVE.e.tensor_scalar(
                            out=dnm[:, pr_ * CL:(pr_ + 1) * CL], in0=psA[pD][:, pr_ * CL:(pr_ + 1) * CL],
                            scalar1=esink[:, slot, kv * 4 + pr_: kv * 4 + pr_ + 1], scalar2=None, op0=ALU.add),
                            reads=[("ps", pD), "esink"], writes=["dnm"])
                    P.op(DVE, lambda: DVE.e.reciprocal(out=dnm[:, 0:4 * CL], in_=dnm[:, 0:4 * CL]), reads=["dnm"], writes=["dnm"])
                    P.op(DVE, lambda kv=kv, c=c, pO=pO: DVE.e.tensor_tensor(
                        out=oT[:, kv * 4:(kv + 1) * 4, c * CL:(c + 1) * CL],
                        in0=psA[pO][:, 0:4 * CL].rearrange("p (a q) -> p a q", a=4),
                        in1=dnm[:, 0:4 * CL].rearrange("p (a q) -> p a q", a=4), op=ALU.mult),
                        reads=[("ps", pO), "dnm"], writes=[("act", 16 + kv * 4 + a_) for a_ in range(4)])

            if ATTN_CUT <= 4:
                return
            if sample_idx is None:
                P.op(POOL, lambda: POOL.e.tensor_copy(out=khist[:, slot], in_=KT[:, :, T:T + 128]), reads=KT_K, writes=["khist"])
                P.op(POOL, lambda: POOL.e.tensor_copy(out=vhist[:, slot], in_=VV[:, nvc:nvc + 2, :]), reads=VV_K, writes=["vhist"])

            cpbo = scratch[(l, "wo")][5] // 128
            for b in range(scratch[(l, "wo")][6]):
                so, wvo = wload(l, "wo", b)
                for off in range(cpbo):
                    m = b * cpbo + off
                    pi = nextps()
                    for k in range(DC):
                        P.op(PE, lambda k=k, wvo=wvo, off=off, pi=pi: PE.e.matmul(
                            psA[pi][:, 0:T], lhsT=wvo[:, k, off * 128:(off + 1) * 128], rhs=oT[:, k, 0:T],
                            start=(k == 0), stop=(k == DC - 1)),
                            reads=[("ring", so)] + OT_K, writes=[("ps", pi)], signal=(k == DC - 1))
                    P.op(DVE, lambda m=m, pi=pi: DVE.e.tensor_tensor(out=h[:, m, 0:T], in0=psA[pi][:, 0:T],
                                                                     in1=h[:, m, 0:T], op=ALU.add),
                         reads=[("ps", pi), "h"], writes=["h"])

        def attn_cache_out(slot, T, dst_k, dst_v, src_k=None, src_v=None):
            n_new = min(T, 128)
            if n_new < 128:
                P.dma(SP, "st_kcopy", dst_k[0:128 - n_new, :], src_k[n_new:128, :])
                P.dma(SP, "st_kcopy", dst_v[0:128 - n_new, :], src_v[n_new:128, :])
            pi = nextps()
            for kv in range(4):
                P.op(PE, lambda kv=kv, pi=pi: PE.e.transpose(psA[pi][0:n_new, kv * 128:(kv + 1) * 128], kout[:, kv, 0:n_new], ident[:]),
                     reads=["kout", "const"], writes=[("ps", pi)], signal=(kv == 3))
            P.op(ACT, lambda pi=pi: ACT.e.activation(
                out=stage_o[0:n_new, 0:256].rearrange("p (k d) -> p k d", k=4),
                in_=psA[pi][0:n_new, :].rearrange("p (k x) -> p k x", k=4)[:, :, 0:64], func=AF.Copy),
                reads=[("ps", pi)], writes=["stage_o"])
            P.dma(SP, "st_stage_o", dst_k[128 - n_new:128, :], stage_o[0:n_new, 0:256], reads=["stage_o"])
            if n_new == 128:
                for c2 in range(2):
                    P.dma(SP, "st_vout", dst_v[c2 * 64:(c2 + 1) * 64, :], vout[:, c2, :], reads=["vout"])
            else:
                P.dma(SP, "st_vout", dst_v[128 - n_new:128, :], vout[0:n_new, 0, :], reads=["vout"])

        def gla(l, slot, T, CL, first_prompt, sample_idx, last_prompt):
            NCH = T // CL
            actf = act[:].rearrange("p c t -> p (c t)")
            qtT = act[:, 0:8, :]; ktT = act[:, 8:16, :]; vT = act[:, 16:32, :]
            Sb = act[:, 32:40, :]
            vtok = actf[:, 40 * TP: 44 * TP]
            Sst = xn[:].rearrange("p c t -> p (c t)").bitcast(F32).rearrange("p (c t) -> p c t", t=TP)
            ktok = gext[0][:, 0:512].bitcast(BF16)
            AmT = gext[1][:, 0:128].bitcast(BF16).rearrange("p (a q) -> p a q", a=4)
            QT_K = [("act", j) for j in range(0, 8)]; KT_K = [("act", j) for j in range(8, 16)]
            VT_K = [("act", j) for j in range(16, 32)]; SB_K = [("act", j) for j in range(32, 40)]
            VTOK_K = [("act", j) for j in range(40, 44)]

            rmsnorm(g_mix, l, T)
            sg, wgl = wload(l, "ggl", 0)
            pi = nextps()
            for k in range(DC):
                P.op(PE, lambda k=k, pi=pi: PE.e.matmul(psA[pi][0:16, 0:T], lhsT=wgl[:, k, 0:16], rhs=xn[:, k, 0:T],
                                                        start=(k == 0), stop=(k == DC - 1)),
                     reads=[("ring", sg)] + XN_ALL, writes=[("ps", pi)], signal=(k == DC - 1))
            P.op(ACT, lambda pi=pi: ACT.e.activation(out=glT[:, 0:T], in_=psA[pi][0:16, 0:T], func=AF.Copy),
                 reads=[("ps", pi)], writes=["glT"])
            blk_q = {}; blk_k = {}
            for m in range(8):
                i = m % 2
                for nm_, cache in (("gq", blk_q), ("gk", blk_k)):
                    b = m // 4
                    if b not in cache:
                        cache[b] = wload(l, nm_, b)
                (sq_, wq), (sk_, wk) = blk_q[m // 4], blk_k[m // 4]
                off = m % 4
                pq, pk, pl = nextps(), nextps(), nextps()
                for (pi_, wv_, s__) in ((pq, wq, sq_), (pk, wk, sk_)):
                    for k in range(DC):
                        P.op(PE, lambda k=k, pi_=pi_, wv_=wv_, off=off: PE.e.matmul(
                            psA[pi_][:, 0:T], lhsT=wv_[:, k, off * 128:(off + 1) * 128], rhs=xn[:, k, 0:T],
                            start=(k == 0), stop=(k == DC - 1)),
                            reads=[("ring", s__)] + XN_ALL, writes=[("ps", pi_)], signal=(k == DC - 1))
                P.op(PE, lambda m=m, pl=pl: PE.e.matmul(psA[pl][:, 0:T], lhsT=wgu[:, slot, m * 128:(m + 1) * 128], rhs=glT[:, 0:T],
                                                        start=True, stop=True), reads=["wgu", "glT"], writes=[("ps", pl)])
                lt, eb, enb = acc[i], sil[i], (rstd if i == 0 else mean)
                EB = "rstd" if i == 0 else "mean"
                P.op(ACT, lambda m=m, pl=pl, lt=lt: ACT.e.activation(out=lt[:, 0:T], in_=psA[pl][:, 0:T], func=AF.Exp, scale=-1.0,
                                                                     bias=ngb[:, slot, m:m + 1]), reads=[("ps", pl), "ngb"], writes=[("acc", i)])
                P.op(ACT, lambda lt=lt: ACT.e.activation(out=lt[:, 0:T], in_=lt[:, 0:T], func=AF.Ln, bias=1.0),
                     reads=[("acc", i)], writes=[("acc", i)])
                P.op(DVE, lambda lt=lt: DVE.e.tensor_tensor_scan(out=gate[:, 0:T], data0=cmask[:, 0:T], data1=lt[:, 0:T], initial=0.0,
                                                                 op0=ALU.mult, op1=ALU.add), reads=[("acc", i), "cmask"], writes=["gate"])
                P.op(ACT, lambda eb=eb: ACT.e.activation(out=eb[:, 0:T], in_=gate[:, 0:T], func=AF.Exp, scale=-1.0 / 16),
                     reads=["gate"], writes=[("sil", i)])
                P.op(ACT, lambda enb=enb: ACT.e.activation(out=enb[:, 0:T], in_=gate[:, 0:T], func=AF.Exp, scale=1.0 / 16),
                     reads=["gate"], writes=[EB])
                P.op(DVE, lambda m=m, pq=pq, eb=eb: DVE.e.scalar_tensor_tensor(out=qtT[:, m, 0:T], in0=psA[pq][:, 0:T], scalar=1.0 / 16,
                                                                               in1=eb[:, 0:T], op0=ALU.mult, op1=ALU.mult),
                     reads=[("ps", pq), ("sil", i)], writes=[("act", m)])
                P.op(DVE, lambda m=m, pk=pk, enb=enb: DVE.e.tensor_tensor(out=ktT[:, m, 0:T], in0=psA[pk][:, 0:T], in1=enb[:, 0:T],
                                                                          op=ALU.mult), reads=[("ps", pk), EB], writes=[("act", 8 + m)])
                P.op(POOL, lambda m=m, eb=eb: POOL.e.tensor_copy(
                    out=elast[:, m, 0:NCH], in_=eb[:, 0:T].rearrange("p (c t) -> p c t", t=CL)[:, :, CL - 1]),
                    reads=[("sil", i)], writes=["elast"])
            cur = None
            for m in range(DC):
                if m % 4 == 0:
                    cur = wload(l, "gv", m // 4)
                sv_, wv_ = cur
                off = m % 4
                pi = nextps()
                for k in range(DC):
                    P.op(PE, lambda k=k, pi=pi, wv_=wv_, off=off: PE.e.matmul(
                        psA[pi][:, 0:T], lhsT=wv_[:, k, off * 128:(off + 1) * 128], rhs=xn[:, k, 0:T],
                        start=(k == 0), stop=(k == DC - 1)),
                        reads=[("ring", sv_)] + XN_ALL, writes=[("ps", pi)], signal=(k == DC - 1))
                P.op(ACT, lambda m=m, pi=pi: ACT.e.activation(out=vT[:, m, 0:T], in_=psA[pi][:, 0:T], func=AF.Copy),
                     reads=[("ps", pi)], writes=[("act", 16 + m)])

            if GLA_CUT <= 1:
                return
            if sample_idx is not None:
                P.dma(SP, "ld_sst", Sst, s_gla[slot, sample_idx].rearrange("h (kk p) v -> p (h kk) v", p=128), writes=XN_ALL)
            elif first_prompt:
                P.op(POOL, lambda: POOL.e.memset(Sst, 0.0), writes=XN_ALL)
            else:
                P.dma(SP, "ld_sst", Sst, gla_carry[slot], reads=["gla_carry"], writes=XN_ALL)
            P.op(POOL, lambda: POOL.e.memset(vtok[:, :], 0.0), writes=VTOK_K)
            P.op(POOL, lambda: POOL.e.memset(gext[0][:, 0:512], 0.0), writes=[("gext", 0)])
            P.op(POOL, lambda: POOL.e.memset(gext[1][:, 0:128], 0.0), writes=[("gext", 1)])

            for c in range(NCH):
                cs = slice(c * CL, (c + 1) * CL)
                for g4 in range(4):
                    pb = nextps()
                    for a_ in range(4):
                        m = 4 * g4 + a_
                        P.op(PE, lambda m=m, a_=a_, pb=pb, cs=cs: PE.e.matmul(
                            psA[pb][0:CL, a_ * 128:(a_ + 1) * 128], lhsT=vT[:, m, cs], rhs=ident_bf[:], start=True, stop=True),
                            reads=[("act", 16 + m), "ident_bf"], writes=[("ps", pb)], signal=(a_ == 3))
                    P.op(ACT, lambda g4=g4, pb=pb: ACT.e.activation(out=vtok[0:CL, g4 * 512:(g4 + 1) * 512], in_=psA[pb][0:CL, :],
                                                                  func=AF.Copy), reads=[("ps", pb)], writes=VTOK_K)
                for g2 in range(2):
                    pb = nextps()
                    for a_ in range(4):
                        m = 4 * g2 + a_
                        P.op(PE, lambda m=m, a_=a_, pb=pb, cs=cs: PE.e.matmul(
                            psA[pb][0:CL, a_ * 128:(a_ + 1) * 128], lhsT=ktT[:, m, cs], rhs=ident_bf[:], start=True, stop=True),
                            reads=[("act", 8 + m), "ident_bf"], writes=[("ps", pb)], signal=(a_ == 3))
                    P.op(ACT, lambda g2=g2, pb=pb: ACT.e.activation(out=ktok[0:CL, g2 * 512:(g2 + 1) * 512], in_=psA[pb][0:CL, :],
                                                                  func=AF.Copy), reads=[("ps", pb)], writes=[("gext", 0)])
                for j in range(8):
                    P.op(POOL if j % 2 else ACT, (lambda j=j: POOL.e.tensor_copy(out=Sb[:, j, :], in_=Sst[:, j, :])) if j % 2 else
                         (lambda j=j: ACT.e.activation(out=Sb[:, j, :], in_=Sst[:, j, :], func=AF.Copy)),
                         reads=[("xn", 2 * j), ("xn", 2 * j + 1)], writes=[("act", 32 + j)])
                pa = nextps()
                for hd in range(4):
                    for kk in range(2):
                        P.op(PE, lambda hd=hd, kk=kk, pa=pa, cs=cs: PE.e.matmul(
                            psA[pa][0:CL, hd * CL:(hd + 1) * CL], lhsT=ktT[:, 2 * hd + kk, cs], rhs=qtT[:, 2 * hd + kk, cs],
                            start=(kk == 0), stop=(kk == 1)),
                            reads=QT_K + KT_K, writes=[("ps", pa)], signal=(hd == 3 and kk == 1))
                P.op(DVE, lambda pa=pa: DVE.e.tensor_tensor(
                    out=AmT[0:CL, :, 0:CL], in0=psA[pa][0:CL, 0:4 * CL].rearrange("p (a q) -> p a q", a=4),
                    in1=tri4[0:CL, :].rearrange("p (a q) -> p a q", a=4)[:, :, 0:CL], op=ALU.mult),
                    reads=[("ps", pa), "const"], writes=[("gext", 1)])
                po = [nextps(), nextps()]
                for f in range(DC):
                    hd = f // 4
                    dst = psA[po[f // 8]][:, (f % 8) * CL:(f % 8 + 1) * CL]
                    for kk in range(2):
                        P.op(PE, lambda f=f, hd=hd, kk=kk, dst=dst, cs=cs: PE.e.matmul(
                            dst, lhsT=Sb[:, 2 * hd + kk, (f % 4) * 128:(f % 4 + 1) * 128], rhs=qtT[:, 2 * hd + kk, cs],
                            start=(kk == 0), stop=False),
                            reads=SB_K + QT_K, writes=[("ps", po[f // 8])], signal=False)
                    P.op(PE, lambda f=f, hd=hd, dst=dst: PE.e.matmul(
                        dst, lhsT=vtok[:, f * 128:(f + 1) * 128], rhs=AmT[:, hd, 0:CL], start=False, stop=True),
                        reads=VTOK_K + [("gext", 1)], writes=[("ps", po[f // 8])], signal=(f % 8 == 7))
                o32 = [acc[0], acc[1]]
                for hf in range(2):
                    P.op(ACT, lambda hf=hf, po=po: ACT.e.activation(out=o32[hf][:, 0:8 * CL], in_=psA[po[hf]][:, 0:8 * CL], func=AF.Copy),
                         reads=[("ps", po[hf])], writes=[("acc", hf)])
                    P.op(ACT, lambda hf=hf: ACT.e.activation(out=osq[:, hf * 512: hf * 512 + 8 * CL], in_=o32[hf][:, 0:8 * CL],
                                                             func=AF.Square), reads=[("acc", hf)], writes=[("osq", hf)])
                for j in range(8):
                    hd = j // 2
                    pd = nextps()
                    P.op(PE, lambda j=j, hd=hd, pd=pd: PE.e.matmul(psA[pd][:, :], lhsT=ktok[:, j * 128:(j + 1) * 128],
                                                                  rhs=vtok[:, hd * 512:(hd + 1) * 512], start=True, stop=True),
                         reads=[("gext", 0)] + VTOK_K, writes=[("ps", pd)])
                    SK = [("xn", 2 * j), ("xn", 2 * j + 1)]
                    P.op(POOL, lambda j=j, c=c: POOL.e.tensor_scalar(out=Sst[:, j, :], in0=Sst[:, j, :], scalar1=elast[:, j, c:c + 1],
                                                                     scalar2=None, op0=ALU.mult), reads=SK + ["elast"], writes=SK)
                    P.op(DVE, lambda j=j, c=c, pd=pd: DVE.e.scalar_tensor_tensor(
                        out=Sst[:, j, :], in0=psA[pd][:, :], scalar=elast[:, j, c:c + 1], in1=Sst[:, j, :],
                        op0=ALU.mult, op1=ALU.add), reads=[("ps", pd), "elast"] + SK, writes=SK)
                pn = nextps()
                for hd in range(4):
                    for fi in range(4):
                        f = 4 * hd + fi
                        P.op(PE, lambda hd=hd, fi=fi, f=f, pn=pn: PE.e.matmul(
                            psA[pn][:, hd * CL:(hd + 1) * CL], lhsT=ones_bf[:],
                            rhs=osq[:, (f // 8) * 512 + (f % 8) * CL: (f // 8) * 512 + (f % 8 + 1) * CL],
                            start=(fi == 0), stop=(fi == 3)),
                            reads=[("osq", f // 8), "ones"], writes=[("ps", pn)], signal=(hd == 3 and fi == 3))
                P.op(DVE, lambda pn=pn: DVE.e.tensor_scalar(out=gate[:, 0:4 * CL], in0=psA[pn][:, 0:4 * CL], scalar1=1.0 / 512, scalar2=EPS,
                                                            op0=ALU.mult, op1=ALU.add), reads=[("ps", pn)], writes=["gate"])
                P.op(ACT, lambda: ACT.e.activation(out=gate[:, 0:4 * CL], in_=gate[:, 0:4 * CL], func=AF.Sqrt), reads=["gate"], writes=["gate"])
                P.op(DVE, lambda: DVE.e.reciprocal(out=gate[:, 0:4 * CL], in_=gate[:, 0:4 * CL]), reads=["gate"], writes=["gate"])
                for f in range(DC):
                    hd, fi = f // 4, f % 4
                    P.op(DVE, lambda f=f, hd=hd, fi=fi, cs=cs: DVE.e.scalar_tensor_tensor(
                        out=vT[:, f, cs], in0=o32[f // 8][:, (f % 8) * CL:(f % 8 + 1) * CL], scalar=onw[:, slot, fi:fi + 1],
                        in1=gate[:, hd * CL:(hd + 1) * CL], op0=ALU.mult, op1=ALU.mult),
                        reads=[("acc", f // 8), "const", "gate"], writes=[("act", 16 + f)])

            if sample_idx is not None:
                P.dma(SP, "st_sst", o_gla_s[slot, sample_idx].rearrange("h (kk p) v -> p (h kk) v", p=128), Sst, reads=XN_ALL)
            else:
                P.dma(SP, "st_sst", gla_carry[slot], Sst, reads=XN_ALL, writes=["gla_carry"])
                if last_prompt:
                    P.dma(SP, "st_sst", o_gla_p[slot].rearrange("h (kk p) v -> p (h kk) v", p=128), Sst, reads=XN_ALL)

            if GLA_CUT <= 2:
                return
            rmsnorm(g_mix, l, T)
            cur = None
            for m in range(DC):
                if m % 4 == 0:
                    cur = wload(l, "gr", m // 4)
                sr_, wr_ = cur
                off = m % 4
                i = m % 2
                pi = nextps()
                for k in range(DC):
                    P.op(PE, lambda k=k, pi=pi, wr_=wr_, off=off: PE.e.matmul(
                        psA[pi][:, 0:T], lhsT=wr_[:, k, off * 128:(off + 1) * 128], rhs=xn[:, k, 0:T],
                        start=(k == 0), stop=(k == DC - 1)),
                        reads=[("ring", sr_)] + XN_ALL, writes=[("ps", pi)], signal=(k == DC - 1))
                P.op(ACT, lambda pi=pi, i=i: ACT.e.activation(out=sil[i][:, 0:T], in_=psA[pi][:, 0:T], func=AF.Silu),
                     reads=[("ps", pi)], writes=[("sil", i)])
                P.op(DVE, lambda m=m, i=i: DVE.e.tensor_tensor(out=vT[:, m, 0:T], in0=vT[:, m, 0:T], in1=sil[i][:, 0:T], op=ALU.mult),
                     reads=[("act", 16 + m), ("sil", i)], writes=[("act", 16 + m)])
            cpbo = scratch[(l, "cwo")][5] // 128
            for b in range(scratch[(l, "cwo")][6]):
                so, wvo = wload(l, "cwo", b)
                for off in range(cpbo):
                    m = b * cpbo + off
                    pi = nextps()
                    for k in range(DC):
                        P.op(PE, lambda k=k, wvo=wvo, off=off, pi=pi: PE.e.matmul(
                            psA[pi][:, 0:T], lhsT=wvo[:, k, off * 128:(off + 1) * 128], rhs=vT[:, k, 0:T],
                            start=(k == 0), stop=(k == DC - 1)),
                            reads=[("ring", so)] + VT_K, writes=[("ps", pi)], signal=(k == DC - 1))
                    P.op(DVE, lambda m=m, pi=pi: DVE.e.tensor_tensor(out=h[:, m, 0:T], in0=psA[pi][:, 0:T],
                                                                     in1=h[:, m, 0:T], op=ALU.add),
                         reads=[("ps", pi), "h"], writes=["h"])

        def load_tokens_T(dst, n_chunks, src_rows, T, reskey):
            for tb in range((T + 127) // 128):
                rows = min(128, T - tb * 128)
                for c0 in range(0, n_chunks, 4):
                    ncol = min(4, n_chunks - c0)
                    P.dma(SP, "ld_stage", stage_i[0:rows, 0:ncol * 128],
                          src_rows[tb * 128: tb * 128 + rows, c0 * 128:(c0 + ncol) * 128], writes=["stage_i"])
                    pi = nextps()
                    for a in range(ncol):
                        P.op(PE, lambda a=a, pi=pi, rows=rows: PE.e.transpose(
                            psA[pi][:, a * 128: a * 128 + rows], stage_i[0:rows, a * 128:(a + 1) * 128],
                            ident[0:rows, 0:rows]),
                            reads=["stage_i", "const"], writes=[("ps", pi)], signal=(a == ncol - 1))
                    for a in range(ncol):
                        P.op(ACT, lambda a=a, pi=pi, rows=rows, c0=c0, tb=tb: ACT.e.activation(
                            out=dst[:, c0 + a, tb * 128: tb * 128 + rows], in_=psA[pi][:, a * 128: a * 128 + rows],
                            func=AF.Copy), reads=[("ps", pi)], writes=(reskey if isinstance(reskey, list) else [reskey]))

        def store_tokens_T(dst_rows, src, n_chunks, T, reskey, semname):
            for tb in range((T + 127) // 128):
                rows = min(128, T - tb * 128)
                for c0 in range(0, n_chunks, 4):
                    ncol = min(4, n_chunks - c0)
                    pi = nextps()
                    for a in range(ncol):
                        P.op(PE, lambda a=a, pi=pi, rows=rows, c0=c0, tb=tb: PE.e.transpose(
                            psA[pi][0:rows, a * 128:(a + 1) * 128], src[:, c0 + a, tb * 128: tb * 128 + rows], ident[:]),
                            reads=(reskey if isinstance(reskey, list) else [reskey]) + ["const"], writes=[("ps", pi)],
                            signal=(a == ncol - 1))
                    P.op(ACT, lambda pi=pi, rows=rows, ncol=ncol: ACT.e.activation(
                        out=stage_o[0:rows, 0:ncol * 128], in_=psA[pi][0:rows, 0:ncol * 128], func=AF.Copy),
                        reads=[("ps", pi)], writes=["stage_o"])
                    P.dma(SP, semname, dst_rows[tb * 128: tb * 128 + rows, c0 * 128:(c0 + ncol) * 128],
                          stage_o[0:rows, 0:ncol * 128], reads=["stage_o"])

        def ple(l, T, p_rows):
            rmsnorm(g_ple, l, T)
            load_tokens_T(pT, 2, p_rows, T, "pT")
            wbp = scratch[(l, "pproj")]
            assert wbp[6] == 1
            P.dma(SP, "ld_wpp", wpp[:].rearrange("p k c -> p (k c)"), wbp[0][0], reads=[("wbk", l, "pproj", 0)], writes=["wpp"])
            cpb = scratch[(l, "pgate")][5] // 128
            cur = None
            for m in range(DC):
                b, off = m // cpb, m % cpb
                if off == 0:
                    cur = wload(l, "pgate", b)
                s, wv = cur
                pgi, ppi = nextps(), nextps()
                for k in range(DC):
                    P.op(PE, lambda k=k, wv=wv, off=off, pgi=pgi: PE.e.matmul(
                        psA[pgi][:, 0:T], lhsT=wv[:, k, off * 128:(off + 1) * 128], rhs=xn[:, k, 0:T],
                        start=(k == 0), stop=(k == DC - 1)),
                        reads=[("ring", s)] + XN_ALL, writes=[("ps", pgi)], signal=(k == DC - 1))
                for k in range(2):
                    P.op(PE, lambda k=k, m=m, ppi=ppi: PE.e.matmul(
                        psA[ppi][:, 0:T], lhsT=wpp[:, k, m * 128:(m + 1) * 128], rhs=pT[:, k, 0:T],
                        start=(k == 0), stop=(k == 1)),
                        reads=["wpp", "pT"], writes=[("ps", ppi)], signal=(k == 1))
                P.op(ACT, lambda pgi=pgi: ACT.e.activation(out=gate[:, 0:T], in_=psA[pgi][:, 0:T], func=AF.Sigmoid),
                     reads=[("ps", pgi)], writes=["gate"])
                P.op(DVE, lambda ppi=ppi: DVE.e.tensor_tensor(out=gate[:, 0:T], in0=psA[ppi][:, 0:T], in1=gate[:, 0:T],
                                                              op=ALU.mult), reads=[("ps", ppi), "gate"], writes=["gate"])
                P.op(DVE, lambda m=m: DVE.e.tensor_tensor(out=h[:, m, 0:T], in0=h[:, m, 0:T], in1=gate[:, 0:T],
                                                          op=ALU.add), reads=["gate", "h"], writes=["h"])

        def ffn_state_out(l, dst):
            pi = nextps()
            P.op(PE, lambda pi=pi: PE.e.transpose(psA[pi][0:2 * FC, 0:128], fhist[:, l].rearrange("p r c -> p (r c)"),
                                                  ident[:]), reads=FH(l) + ["const"], writes=[("ps", pi)])
            P.op(ACT, lambda pi=pi: ACT.e.activation(out=tr_o[0:2 * FC, :], in_=psA[pi][0:2 * FC, 0:128], func=AF.Copy),
                 reads=[("ps", pi)], writes=["tr_o"])
            for r in range(2):
                P.dma(SP, "st_tr_o", dst[r].rearrange("(c p) -> c p", p=128), tr_o[r * FC:(r + 1) * FC, :], reads=["tr_o"])

        def ffn_state_in(l, src):
            for r in range(2):
                P.dma(SP, "ld_tr_i", tr_i[r * FC:(r + 1) * FC, :], src[r].rearrange("(c p) -> c p", p=128), writes=["tr_i"])
            pi = nextps()
            P.op(PE, lambda pi=pi: PE.e.transpose(psA[pi][:, 0:2 * FC], tr_i[0:2 * FC, :], ident[0:2 * FC, 0:2 * FC]),
                 reads=["tr_i", "const"], writes=[("ps", pi)])
            P.op(ACT, lambda pi=pi: ACT.e.activation(out=fhist[:, l].rearrange("p r c -> p (r c)"),
                                                     in_=psA[pi][:, 0:2 * FC], func=AF.Copy),
                 reads=[("ps", pi)], writes=FH(l))

        def run_tile(T, x_rows, y_rows, p_rows_of_layer, first_prompt, sample_idx, last_prompt, tile_idx=0):
            load_tokens_T(h, DC, x_rows, T, "h")
            for l in range(layers):
                if sample_idx is not None and not SKIP_FFN:
                    ffn_state_in(l, s_ffn[l, sample_idx])
                kind, slot = KS[l]
                if kind == 0:
                    emit = (sample_idx is not None) or last_prompt
                    attention(l, slot, T, (TS if sample_idx is not None else 64), first_prompt,
                              (NPT_R - 1 if sample_idx is not None else tile_idx), sample_idx, emit)
                    if ATTN_CUT <= 5:
                        pass
                    elif sample_idx is not None:
                        attn_cache_out(slot, T, o_k_s[slot, sample_idx], o_v_s[slot, sample_idx],
                                       ck_in[slot, sample_idx], cv_in[slot, sample_idx])
                    elif last_prompt:
                        attn_cache_out(slot, T, o_k_p[slot], o_v_p[slot])
                if kind == 2:
                    gla(l, slot, T, (TS if sample_idx is not None else 64), first_prompt, sample_idx, last_prompt)
                if kind == 1:
                    if sample_idx is not None:
                        load_tokens_T(chist[:, slot], DC, s_conv[slot, sample_idx], CW - 1, CH_ALL)
                    conformer(l, slot, T, first_prompt)
                    if sample_idx is not None:
                        store_tokens_T(o_conv_s[slot, sample_idx], chist[:, slot], DC, CW - 1, CH_ALL, "st_stage_o")
                    elif last_prompt:
                        store_tokens_T(o_conv_p[slot], chist[:, slot], DC, CW - 1, CH_ALL, "st_stage_o")
                if SKIP_FFN:
                    continue
                conv_ffn(l, T, first_prompt)
                ple(l, T, p_rows_of_layer(l))
                if sample_idx is not None:
                    ffn_state_out(l, o_ffn_s[l, sample_idx])
                elif last_prompt:
                    ffn_state_out(l, o_ffn_p[l])
            store_tokens_T(y_rows, h, DC, T, "h", "st_stage_o")

        for t in range(n_ptiles):
            run_tile(TP, x_p[t * TP:(t + 1) * TP, :], y_p[t * TP:(t + 1) * TP, :],
                     lambda l, t=t: p_p[l, t * TP:(t + 1) * TP, :], t == 0, None, t == n_ptiles - 1, t)
        for s_ in range(spc):
            run_tile(TS, x_s[s_], y_s[s_], lambda l, s_=s_: p_s[l, s_], False, s_, False)

        P.drain_all(SP)

        block = E(nc.Block())

        @block.tensor
        def _(e): PE.replay(e)

        @block.scalar
        def _(e): ACT.replay(e)

        @block.vector
        def _(e): DVE.replay(e)

        @block.gpsimd
        def _(e): POOL.replay(e)

        @block.sync
        def _(e): SP.replay(e)

        nc._n_instr_est = P.n_instr
    return nc


SHARED_KEYS = ("norm_mix", "norm_ffn", "ple_norm", "ffn_conv_w", "ffn_conv_b",
               "ffn_w_up", "ffn_w_down", "ple_w_proj", "ple_w_gate")
B_KEYS = ("b_w_pw1", "b_w_pw2", "b_w_dw", "b_dw_bias", "b_ln_g", "b_ln_b")
A_KEYS = ("a_w_qkv", "a_w_o", "a_q_norm", "a_k_norm", "a_sinks")
C_KEYS = ("c_w_in", "c_w_o", "c_w_gate_up", "c_gate_bias", "c_out_norm")
ROPE_THETA = 10000.0
PAST_LEN = 2048


def _rope_tables(n_ptiles):
    half = 32
    inv = (1.0 / (ROPE_THETA ** (np.arange(half, dtype=np.float32) / half))).astype(np.float32)
    out = np.zeros((n_ptiles + 1, 2, 128, TP), np.float32)
    rows = np.arange(128) % half
    for t in range(n_ptiles + 1):
        pos = (np.arange(TP) + t * TP) if t < n_ptiles else (PAST_LEN + np.arange(TP))
        ang = (pos.astype(np.float32)[None, :] * inv[rows][:, None]).astype(np.float32)
        out[t, 0] = np.cos(ang); out[t, 1] = np.sin(ang)
    return out


def _psign():
    m = np.zeros((128, 128), np.float32)
    for c in range(128):
        if c % 64 < 32:
            m[c + 32, c] = -1.0
        else:
            m[c - 32, c] = 1.0
    return m


def make_in_maps(inp, n_cores, n_ptiles, spc, kinds=DEFAULT_KINDS):
    layers = len(kinds)
    _, nsl = _kinds_slots(kinds)
    ident = np.eye(128, dtype=np.float32)
    LP = max(n_ptiles * TP, TP)
    shared = {k: np.ascontiguousarray(inp[k][:layers]) for k in SHARED_KEYS}
    if nsl[0]:
        for k in A_KEYS:
            shared[k] = np.ascontiguousarray(inp[k][:nsl[0]])
        shared["rope"] = _rope_tables(n_ptiles)
        shared["psign"] = _psign()
    if nsl[1]:
        for k in B_KEYS:
            shared[k] = np.ascontiguousarray(inp[k][:nsl[1]])
    if nsl[2]:
        for k in C_KEYS:
            shared[k] = np.ascontiguousarray(inp[k][:nsl[2]])
        tri = np.zeros((128, 4, 64), np.float32)
        jj, ii = np.meshgrid(np.arange(64), np.arange(64), indexing="ij")
        tri[:64] = (jj <= ii).astype(np.float32)[:, None, :]
        shared["tri4"] = tri.reshape(128, 256)
    maps = []
    for c in range(n_cores):
        b = c % 2
        sl = slice(c * spc, c * spc + max(spc, 1))
        m = dict(shared)
        m["ident"] = ident
        m["x_p"] = np.ascontiguousarray(inp["x_prompt"][b, :LP])
        m["p_p"] = np.ascontiguousarray(inp["p_prompt"][:layers, b, :LP])
        m["x_s"] = np.ascontiguousarray(inp["x_sample"][sl])
        m["p_s"] = np.ascontiguousarray(inp["p_sample"][:layers, sl])
        m["s_ffn"] = np.ascontiguousarray(inp["state_ffn_conv"][:layers, sl])
        if nsl[0]:
            m["ck"] = np.ascontiguousarray(inp["cache_k_a"][:nsl[0], sl]).reshape(nsl[0], -1, 128, 256)
            m["cv"] = np.ascontiguousarray(inp["cache_v_a"][:nsl[0], sl]).reshape(nsl[0], -1, 128, 256)
        if nsl[1]:
            m["s_conv"] = np.ascontiguousarray(inp["state_conv_b"][:nsl[1], sl])
        if nsl[2]:
            m["s_gla"] = np.ascontiguousarray(inp["state_gla_c"][:nsl[2], sl])
        maps.append(m)
    return maps


def kernel(**inp):
    n_cores = 8
    spc = DEC_B // n_cores
    n_ptiles = SEQ // TP
    nc = build_program(n_ptiles, spc)
    in_maps = make_in_maps(inp, n_cores, n_ptiles, spc)
    res = run_bass_kernel_spmd(nc, in_maps, core_ids=list(range(n_cores))).results
    f32 = np.float32
    cat_p = lambda k, ax: np.stack([res[0][k], res[1][k]], ax).astype(f32)
    cat_s = lambda k, ax: np.concatenate([res[c][k] for c in range(n_cores)], ax).astype(f32)
    y_prompt = cat_p("y_p", 0)
    y_sample = cat_s("y_s", 0)
    new_ffn_p = cat_p("o_ffn_p", 1)
    new_ffn_s = cat_s("o_ffn_s", 1)
    new_conv_p = cat_p("o_conv_p", 1)
    new_conv_s = cat_s("o_conv_s", 1)
    kv5 = lambda a: a.reshape(a.shape[0], a.shape[1], 128, 4, 64)
    new_k_p, new_v_p = kv5(cat_p("o_k_p", 1)), kv5(cat_p("o_v_p", 1))
    new_k_s, new_v_s = kv5(cat_s("o_k_s", 1)), kv5(cat_s("o_v_s", 1))
    new_gla_p = cat_p("o_gla_p", 1)
    new_gla_s = cat_s("o_gla_s", 1)
    return (y_prompt, y_sample, new_k_p, new_v_p, new_conv_p, new_gla_p, new_ffn_p,
            new_k_s, new_v_s, new_conv_s, new_gla_s, new_ffn_s)
```
